# Optimizing a Trainium2 kernel written in Bass

```python
import jax, jax.numpy as jnp
from jax import lax
import numpy as np

D_MODEL = 1024
BATCH = 16
SEQ = 256
DEPTH = 1
DEC_BATCH = 2
DEC_SEQ = 2048
PAST_LEN = 256

GRID_W = 64
N_DIR = 2
RWKV_HEADS = 8
RWKV_HEAD_DIM = 64
RWKV_WIDTH = RWKV_HEADS * RWKV_HEAD_DIM
DECAY_LORA = 64
ICLR_LORA = 64
GATE_LORA = 128
MLSTM_HEADS = 4
MLSTM_HEAD_DIM = 128
MLSTM_WIDTH = MLSTM_HEADS * MLSTM_HEAD_DIM
MLSTM_CHUNK = 64
D_FF = 2816
RMS_EPS = 1e-6
RWKV_GN_EPS = 64e-5
MLSTM_GN_EPS = 1e-5
DECAY_SCALE = 0.606531

RWKV_COLS = 3 * RWKV_WIDTH + N_DIR * DECAY_LORA + N_DIR * ICLR_LORA + GATE_LORA
MLSTM_COLS = 4 * MLSTM_WIDTH + 2 * N_DIR * MLSTM_HEADS
GATE_COLS = 2 * D_MODEL
IN_COLS = RWKV_COLS + MLSTM_COLS + GATE_COLS

kernel_name = 'bidir_rwkv7_mlstm_prefix_dit_step'


def rmsnorm(x, g):
    x32 = x.astype(jnp.float32)
    y = x32 * lax.rsqrt(jnp.mean(x32 * x32, axis=-1, keepdims=True) + RMS_EPS)
    return (y * g.astype(jnp.float32)).astype(x.dtype)


def centred_shift(z, mu):
    zp = jnp.pad(z, ((0, 0), (1, 1), (0, 0)))
    return z + mu * (0.5 * (zp[:, :-2] + zp[:, 2:]) - z)


def dwconv_grid(x, w, rows):
    b, t, ch = x.shape
    img = x.reshape(b, rows, t // rows, ch)
    out = lax.conv_general_dilated(img, w[:, :, None, :].astype(x.dtype), (1, 1), 'SAME',
                                   dimension_numbers=('NHWC', 'HWIO', 'NHWC'),
                                   feature_group_count=ch)
    return out.reshape(b, t, ch)


def flip_backward(x, dir_axis, time_axis):
    fwd = jnp.take(x, 0, axis=dir_axis)
    bwd = jnp.take(x, 1, axis=dir_axis)
    t_ax = time_axis - 1 if time_axis > dir_axis else time_axis
    return jnp.stack([fwd, jnp.flip(bwd, axis=t_ax)], axis=dir_axis)


def rwkv7_bidir(z, S0, w0, w_up, a0, a_up, g_up, kk_scale, k_a, r_k, lnx_g, lnx_b):
    f32 = jnp.float32
    z = z.astype(f32)
    B, T, _ = z.shape
    H, N = RWKV_HEADS, RWKV_HEAD_DIM
    o1 = RWKV_WIDTH
    o2 = 2 * RWKV_WIDTH
    o3 = 3 * RWKV_WIDTH
    o4 = o3 + N_DIR * DECAY_LORA
    o5 = o4 + N_DIR * ICLR_LORA
    r, k, v, wd, ad, gd = jnp.split(z, [o1, o2, o3, o4, o5], axis=-1)
    wd = wd.reshape(B, T, N_DIR, DECAY_LORA)
    ad = ad.reshape(B, T, N_DIR, ICLR_LORA)
    decay = jnp.exp(-DECAY_SCALE * jax.nn.sigmoid(w0 + jnp.einsum('btdr,drc->btdc', jnp.tanh(wd), w_up)))
    a = jax.nn.sigmoid(a0 + jnp.einsum('btdr,drc->btdc', ad, a_up))
    g = jax.nn.sigmoid(gd) @ g_up
    kk = (k * kk_scale).reshape(B, T, H, N)
    kk = kk / jnp.maximum(jnp.linalg.norm(kk, axis=-1, keepdims=True), 1e-12)
    k_dir = k[:, :, None, :] * (1.0 + (a - 1.0) * k_a)

    def shared(u):
        return jnp.broadcast_to(u[:, :, None, :], (B, T, N_DIR, u.shape[-1]))

    def per_dir(u):
        return jnp.moveaxis(flip_backward(u.reshape(B, T, N_DIR, H, N), 2, 1), 1, 0)

    xs = (per_dir(shared(r)), per_dir(decay), per_dir(k_dir), per_dir(shared(v)),
          per_dir(shared(kk.reshape(B, T, RWKV_WIDTH))), per_dir(a))

    def step(S, inp):
        r_t, w_t, k_t, v_t, kk_t, a_t = inp
        removed = jnp.einsum('bdhvk,bdhk->bdhv', S, kk_t)
        S = (S * w_t[..., None, :] - removed[..., :, None] * (kk_t * a_t)[..., None, :]
             + v_t[..., :, None] * k_t[..., None, :])
        return S, jnp.einsum('bdhvk,bdhk->bdhv', S, r_t)

    S_fin, ys = lax.scan(step, S0.astype(f32), xs)
    ys = flip_backward(jnp.moveaxis(ys, 0, 1), 2, 1).sum(axis=2)
    mean = jnp.mean(ys, axis=-1, keepdims=True)
    var = jnp.var(ys, axis=-1, keepdims=True)
    y = ((ys - mean) * lax.rsqrt(var + RWKV_GN_EPS)).reshape(B, T, RWKV_WIDTH) * lnx_g + lnx_b
    bonus = (jnp.einsum('bthn,btdhn,hn->bth', r.reshape(B, T, H, N),
                        k_dir.reshape(B, T, N_DIR, H, N), r_k)[..., None]
             * v.reshape(B, T, H, N))
    return (y + bonus.reshape(B, T, RWKV_WIDTH)) * g, S_fin


def mlstm_chunkwise(q, k, v, log_i, log_f, C0, n0, m0):
    T = q.shape[-2]
    L = MLSTM_CHUNK
    nc = T // L
    lead = q.shape[:-2]
    mask = jnp.tril(jnp.ones((L, L), dtype=bool))

    def chunk(u):
        return jnp.moveaxis(u.reshape(u.shape[:-2] + (nc, L, u.shape[-1])), -3, 0)

    def chunk_g(u):
        return jnp.moveaxis(u.reshape(u.shape[:-1] + (nc, L)), -2, 0)

    def step(carry, inp):
        C, n, m = carry
        qc, kc, vc, ic, fc = inp
        b = jnp.cumsum(fc, axis=-1)
        log_d = jnp.where(mask, b[..., :, None] - b[..., None, :] + ic[..., None, :], -jnp.inf)
        log_inter = b + m[..., None]
        m_s = jnp.maximum(log_inter, jnp.max(log_d, axis=-1))
        dmat = jnp.exp(log_d - m_s[..., None])
        inter = jnp.exp(log_inter - m_s)
        s = jnp.einsum('...sk,...jk->...sj', qc, kc) * dmat
        num = (inter[..., None] * jnp.einsum('...sk,...vk->...sv', qc, C)
               + jnp.einsum('...sj,...jv->...sv', s, vc))
        den = inter * jnp.einsum('...sk,...k->...s', qc, n) + jnp.sum(s, axis=-1)
        h = num / jnp.maximum(jnp.abs(den), jnp.exp(-m_s))[..., None]
        bL = b[..., -1]
        log_w = bL[..., None] - b + ic
        m_new = jnp.maximum(bL + m, jnp.max(log_w, axis=-1))
        wj = jnp.exp(log_w - m_new[..., None])
        carry_decay = jnp.exp(bL + m - m_new)
        C_new = carry_decay[..., None, None] * C + jnp.einsum('...j,...jv,...jk->...vk', wj, vc, kc)
        n_new = carry_decay[..., None] * n + jnp.einsum('...j,...jk->...k', wj, kc)
        return (C_new, n_new, m_new), h

    (C, n, m), hs = lax.scan(step, (C0, n0, m0),
                             (chunk(q), chunk(k), chunk(v), chunk_g(log_i), chunk_g(log_f)))
    h = jnp.moveaxis(hs, 0, -3).reshape(lead + (T, v.shape[-1]))
    return h, C, n, m


def mlstm_bidir(z, C0, n0, m0, rows, conv_w, gate_b, gn_g):
    f32 = jnp.float32
    B, T, _ = z.shape
    H, dh, W = MLSTM_HEADS, MLSTM_HEAD_DIM, MLSTM_WIDTH
    qk, v, o, gates = jnp.split(z, [2 * W, 3 * W, 4 * W], axis=-1)
    qk = jax.nn.silu(dwconv_grid(qk, conv_w, rows)).astype(f32)
    q, k = jnp.split(qk, 2, axis=-1)

    def heads(u):
        u = u.astype(f32).reshape(B, T, H, dh).transpose(0, 2, 1, 3)
        return flip_backward(jnp.broadcast_to(u[:, None], (B, N_DIR, H, T, dh)), 1, 3)

    gates = gates.astype(f32).reshape(B, T, 2, N_DIR, H) + gate_b
    gates = jnp.transpose(gates, (0, 2, 3, 4, 1))
    log_i = flip_backward(gates[:, 0], 1, 3)
    log_f = flip_backward(jax.nn.log_sigmoid(gates[:, 1]), 1, 3)
    h, C, n, m = mlstm_chunkwise(heads(q * dh ** -0.5), heads(k), heads(v), log_i, log_f,
                                 C0.astype(f32), n0.astype(f32), m0.astype(f32))
    h = flip_backward(h, 1, 3).sum(axis=1)
    mean = jnp.mean(h, axis=-1, keepdims=True)
    var = jnp.var(h, axis=-1, keepdims=True)
    h = ((h - mean) * lax.rsqrt(var + MLSTM_GN_EPS)).transpose(0, 2, 1, 3).reshape(B, T, W) * gn_g
    return h * jax.nn.sigmoid(o.astype(f32)), (C, n, m)


def trunk_layer(x, cond, rows, states, p):
    mod = (jax.nn.silu(cond) @ p['ada_w'] + p['ada_b'])[:, None, :]
    sh1, sc1, g1, sh2, sc2, g2 = jnp.split(mod, 6, axis=-1)
    h = rmsnorm(x, p['norm_g'][0]) * (1.0 + sc1) + sh1
    z = h @ p['w_in']
    z_r, z_m, z_g = jnp.split(z, [RWKV_COLS, RWKV_COLS + MLSTM_COLS], axis=-1)
    S0, C0, n0, m0 = states
    y_r, S = rwkv7_bidir(centred_shift(z_r, p['rwkv_mu']), S0, p['rwkv_w0'], p['rwkv_w_up'],
                         p['rwkv_a0'], p['rwkv_a_up'], p['rwkv_g_up'], p['rwkv_kk_scale'],
                         p['rwkv_k_a'], p['rwkv_r_k'], p['rwkv_lnx_g'], p['rwkv_lnx_b'])
    y_m, (C, n, m) = mlstm_bidir(z_m, C0, n0, m0, rows, p['mlstm_conv'], p['mlstm_gate_b'],
                                 p['mlstm_gn_g'])
    gate_r, gate_m = jnp.split(jax.nn.sigmoid(z_g), 2, axis=-1)
    merged = (gate_r * (y_r.astype(x.dtype) @ p['w_branch_rwkv'])
              + gate_m * (y_m.astype(x.dtype) @ p['w_branch_mlstm']))
    x = x + g1 * rmsnorm(merged @ p['w_out'], p['norm_g'][1])
    h = rmsnorm(x, p['norm_g'][2]) * (1.0 + sc2) + sh2
    u_act, u_val = jnp.split(h @ p['ffn_up'], 2, axis=-1)
    u_act = dwconv_grid(u_act, p['ffn_conv'], rows) + p['ffn_conv_b']
    f = (jax.nn.silu(u_act) * u_val) @ p['ffn_down']
    x = x + g2 * rmsnorm(f, p['norm_g'][3])
    return x, (S, C, n, m)


def setup_inputs(seed: int = 0) -> dict:
    key = jax.random.key(seed)
    ks = iter(jax.random.split(key, 40))
    f32 = jnp.float32

    def nrm(shape, scale):
        return jax.random.normal(next(ks), shape, f32) * scale

    L, D, H, N = DEPTH, D_MODEL, RWKV_HEADS, RWKV_HEAD_DIM
    MH, dh = MLSTM_HEADS, MLSTM_HEAD_DIM
    return {
        'x_prompt': nrm((BATCH, SEQ, D), 1.0),
        'x_sample': nrm((DEC_BATCH, DEC_SEQ, D), 1.0),
        'c': nrm((DEC_BATCH, D), 1.0),
        'state_rwkv': nrm((DEC_BATCH, L, N_DIR, H, N, N), 0.3),
        'state_mlstm_C': nrm((DEC_BATCH, L, N_DIR, MH, dh, dh), 0.1),
        'state_mlstm_n': nrm((DEC_BATCH, L, N_DIR, MH, dh), 0.5),
        'state_mlstm_m': nrm((DEC_BATCH, L, N_DIR, MH), 1.0),
        'c_ctx': nrm((D,), 1.0),
        'ada_w': nrm((L, D, 6 * D), 0.5 * D ** -0.5),
        'ada_b': nrm((L, 6 * D), 0.02),
        'norm_g': 1.0 + nrm((L, 4, D), 0.02),
        'w_in': nrm((L, D, IN_COLS), D ** -0.5),
        'rwkv_mu': jax.random.uniform(next(ks), (L, RWKV_COLS), f32),
        'rwkv_w0': nrm((L, N_DIR, RWKV_WIDTH), 1.0),
        'rwkv_w_up': nrm((L, N_DIR, DECAY_LORA, RWKV_WIDTH), 0.1),
        'rwkv_a0': nrm((L, N_DIR, RWKV_WIDTH), 0.5),
        'rwkv_a_up': nrm((L, N_DIR, ICLR_LORA, RWKV_WIDTH), 0.1),
        'rwkv_g_up': nrm((L, GATE_LORA, RWKV_WIDTH), GATE_LORA ** -0.5),
        'rwkv_kk_scale': 0.85 + nrm((L, RWKV_WIDTH), 0.02),
        'rwkv_k_a': 1.0 + nrm((L, RWKV_WIDTH), 0.02),
        'rwkv_r_k': nrm((L, H, N), 0.1),
        'rwkv_lnx_g': 1.0 + nrm((L, RWKV_WIDTH), 0.02),
        'rwkv_lnx_b': nrm((L, RWKV_WIDTH), 0.02),
        'mlstm_conv': nrm((L, 3, 3, 2 * MLSTM_WIDTH), 1.0 / 3.0),
        'mlstm_gate_b': jnp.stack([nrm((L, N_DIR, MH), 0.1), 3.0 + nrm((L, N_DIR, MH), 0.5)], axis=1),
        'mlstm_gn_g': 1.0 + nrm((L, MLSTM_WIDTH), 0.02),
        'w_branch_rwkv': nrm((L, RWKV_WIDTH, D), RWKV_WIDTH ** -0.5),
        'w_branch_mlstm': nrm((L, MLSTM_WIDTH, D), MLSTM_WIDTH ** -0.5),
        'w_out': nrm((L, D, D), D ** -0.5),
        'ffn_up': nrm((L, D, 2 * D_FF), D ** -0.5),
        'ffn_conv': nrm((L, 3, 3, D_FF), 1.0 / 3.0),
        'ffn_conv_b': nrm((L, D_FF), 0.02),
        'ffn_down': nrm((L, D_FF, D), D_FF ** -0.5),
    }


def reference(x_prompt, x_sample, c, state_rwkv, state_mlstm_C, state_mlstm_n, state_mlstm_m,
              c_ctx, ada_w, ada_b, norm_g, w_in, rwkv_mu, rwkv_w0, rwkv_w_up, rwkv_a0, rwkv_a_up,
              rwkv_g_up, rwkv_kk_scale, rwkv_k_a, rwkv_r_k, rwkv_lnx_g, rwkv_lnx_b, mlstm_conv,
              mlstm_gate_b, mlstm_gn_g, w_branch_rwkv, w_branch_mlstm, w_out, ffn_up, ffn_conv,
              ffn_conv_b, ffn_down):
    f32 = jnp.float32
    B = x_prompt.shape[0]
    ctx_init = (jnp.zeros((B, N_DIR, RWKV_HEADS, RWKV_HEAD_DIM, RWKV_HEAD_DIM), f32),
                jnp.zeros((B, N_DIR, MLSTM_HEADS, MLSTM_HEAD_DIM, MLSTM_HEAD_DIM), f32),
                jnp.zeros((B, N_DIR, MLSTM_HEADS, MLSTM_HEAD_DIM), f32),
                jnp.zeros((B, N_DIR, MLSTM_HEADS), f32))
    latent_rows = x_sample.shape[1] // GRID_W
    xp, xs = x_prompt, x_sample
    new_S, new_C, new_n, new_m = [], [], [], []
    for l in range(DEPTH):
        p = dict(ada_w=ada_w[l], ada_b=ada_b[l], norm_g=norm_g[l], w_in=w_in[l],
                 rwkv_mu=rwkv_mu[l], rwkv_w0=rwkv_w0[l], rwkv_w_up=rwkv_w_up[l],
                 rwkv_a0=rwkv_a0[l], rwkv_a_up=rwkv_a_up[l], rwkv_g_up=rwkv_g_up[l],
                 rwkv_kk_scale=rwkv_kk_scale[l], rwkv_k_a=rwkv_k_a[l], rwkv_r_k=rwkv_r_k[l],
                 rwkv_lnx_g=rwkv_lnx_g[l], rwkv_lnx_b=rwkv_lnx_b[l], mlstm_conv=mlstm_conv[l],
                 mlstm_gate_b=mlstm_gate_b[l], mlstm_gn_g=mlstm_gn_g[l],
                 w_branch_rwkv=w_branch_rwkv[l], w_branch_mlstm=w_branch_mlstm[l], w_out=w_out[l],
                 ffn_up=ffn_up[l], ffn_conv=ffn_conv[l], ffn_conv_b=ffn_conv_b[l],
                 ffn_down=ffn_down[l])
        xp, (S, C, n, m) = trunk_layer(xp, c_ctx[None, :], 1, ctx_init, p)
        new_S.append(S)
        new_C.append(C)
        new_n.append(n)
        new_m.append(m)
        xs, _ = trunk_layer(xs, c, latent_rows,
                            (state_rwkv[:, l], state_mlstm_C[:, l], state_mlstm_n[:, l],
                             state_mlstm_m[:, l]), p)
    out_dtype = x_prompt.dtype
    new_state_rwkv = jnp.stack(new_S, axis=1).astype(out_dtype)
    new_state_mlstm_C = jnp.stack(new_C, axis=1).astype(out_dtype)
    new_state_mlstm_n = jnp.stack(new_n, axis=1).astype(out_dtype)
    new_state_mlstm_m = jnp.stack(new_m, axis=1).astype(out_dtype)
    return (xp, xs, new_state_rwkv, new_state_mlstm_C, new_state_mlstm_n, new_state_mlstm_m)
```

```python
import contextlib
import numpy as np
import concourse.bass as bass
import concourse.mybir as mybir
from concourse.bass_utils import run_bass_kernel_spmd

F32 = mybir.dt.float32
BF16 = mybir.dt.bfloat16
ALU = mybir.AluOpType
AF = mybir.ActivationFunctionType
AX = mybir.AxisListType
ENGS = ["pe", "act", "dve", "pool", "sp"]

D = 1024
TS = 2048
TPS = 256
DFF = 2816
NFF = 22
DSC = 0.606531
SD = F32


class Buf:
    __slots__ = ("name", "w", "r")

    def __init__(self, name=""):
        self.name = name
        self.w = {}
        self.r = {}


class Prog:
    def __init__(self, nc, n_dma_sems=48):
        self.nc = nc
        self.ops = {e: [] for e in ENGS}
        self.sem = {e: nc.alloc_semaphore(name=f"sem_{e}") for e in ENGS}
        self.cnt = {e: 0 for e in ENGS}
        self.known = {e: {} for e in ENGS}
        self.dma_sems = [nc.alloc_semaphore(name=f"dsem{i}") for i in range(n_dma_sems)]
        self.dma_tot = [0] * n_dma_sems
        self.dma_rr = 0
        self.extra_evs = []

    def _collect(self, e, reads, writes, extra=()):
        waits = {}

        def need(ev):
            sem, val = ev
            if waits.get(sem.num, (None, 0))[1] < val:
                waits[sem.num] = (sem, val)

        for b in reads:
            for ev in b.w.values():
                need(ev)
        for b in writes:
            for ev in b.w.values():
                need(ev)
            for ev in b.r.values():
                need(ev)
        for ev in extra:
            need(ev)
        wl = []
        own = self.sem[e].num
        for num, (sem, val) in waits.items():
            if self.known[e].get(num, 0) >= val:
                continue
            if num == own and e == "pe":
                continue
            self.known[e][num] = val
            wl.append((sem, val))
        return wl

    def _update(self, ev, reads, writes):
        num = ev[0].num
        ws = set(id(b) for b in writes)
        for b in writes:
            b.w = {num: ev}
            b.r = {}
        for b in reads:
            if id(b) not in ws:
                b.r[num] = ev

    def op(self, e, fn, reads=(), writes=()):
        wl = self._collect(e, reads, writes)
        self.cnt[e] += 1
        ev = (self.sem[e], self.cnt[e])
        self.ops[e].append((wl, fn, (self.sem[e], 1)))
        self._update(ev, reads, writes)
        return ev

    def dma(self, e, out, in_, reads=(), writes=()):
        s = self.dma_rr
        self.dma_rr = (self.dma_rr + 1) % len(self.dma_sems)
        sem = self.dma_sems[s]
        extra = [(sem, self.dma_tot[s])] if self.dma_tot[s] > 0 else []
        wl = self._collect(e, reads, writes, extra)
        self.dma_tot[s] += 16
        ev = (sem, self.dma_tot[s])

        def fn(eng, out=out, in_=in_):
            return eng.dma_start(out=out, in_=in_)

        self.ops[e].append((wl, fn, (sem, 16)))
        self._update(ev, reads, writes)
        return ev

    def coll(self, e, fn, reads=(), writes=()):
        sem = self.nc.alloc_semaphore(name=f"ccsem{len(self.ops[e])}")
        wl = self._collect(e, reads, writes)
        ev = (sem, 1)
        self.ops[e].append((wl, fn, (sem, 1)))
        self._update(ev, reads, writes)
        self.extra_evs.append(ev)
        return ev

    def barrier(self, final=False):
        evs = [(self.sem[e], self.cnt[e]) for e in ENGS if self.cnt[e] > 0]
        evs += [(self.dma_sems[i], self.dma_tot[i]) for i in range(len(self.dma_sems)) if self.dma_tot[i] > 0]
        if final:
            evs += self.extra_evs
        for e in ENGS:
            wl = self._collect(e, (), (), evs)
            self.ops[e].append((wl, None, None))

    def emit(self):
        engmap = {"pe": "tensor", "act": "scalar", "dve": "vector", "pool": "gpsimd", "sp": "sync"}
        with self.nc.Block() as block:
            for e in ENGS:
                ops = self.ops[e]

                def body(eng, ops=ops):
                    for wl, fn, inc in ops:
                        for sem, val in wl:
                            eng.wait_ge(sem, val)
                        if fn is not None:
                            ins = fn(eng)
                            if inc is not None:
                                ins.then_inc(inc[0], inc[1])

                getattr(block, engmap[e])(body)


class Arena:
    def __init__(self, t, ncols):
        self.t = t
        self.top = 0
        self.ncols = ncols
        self.peak = 0
        self.guard = None

    def alloc(self, cols, dt=F32):
        c32 = cols if dt == F32 else (cols + 1) // 2
        a = self.top
        self.top += c32
        self.peak = max(self.peak, self.top)
        assert self.top <= self.ncols, f"arena overflow {self.top}"
        assert self.guard is None or self.top <= self.guard, f"arena guard hit {self.top} > {self.guard}"
        ap = self.t[:, a:a + c32]
        if dt != F32:
            ap = ap.bitcast(dt)
        return ap

    def mark(self):
        return self.top

    def release(self, m):
        self.top = m


PRM = {}
_off = 0
for _n, _c in [("ada_b", 48), ("ng0", 8), ("ng1", 8), ("ng2", 8), ("ng3", 8), ("mu", 15), ("w0", 8), ("a0", 8),
               ("kks", 4), ("ka", 4), ("rk", 4), ("lng", 4), ("lnb", 4), ("mconv", 72), ("gbi", 1), ("gbf", 1),
               ("gng", 4), ("fconv", 198), ("fcb", 22)]:
    PRM[_n] = (_off, _c)
    _off += _c
NPRM = _off


def _fm(v):
    v = np.asarray(v, np.float32).reshape(-1, 128)
    return np.ascontiguousarray(v.T)


def _tile_w(W):
    K, N = W.shape
    return np.ascontiguousarray(W.reshape(K // 128, 128, N // 128, 128).transpose(2, 1, 0, 3).reshape(N // 128, 128, K))


def build(nc, dbg=False):
    P = Prog(nc)
    es = contextlib.ExitStack()
    din = lambda name, shape: nc.dram_tensor(name, shape, F32, kind="ExternalInput").ap()
    dout = lambda name, shape: nc.dram_tensor(name, shape, F32, kind="ExternalOutput").ap()
    dx = {"s": din("xs", [D, TS]), "p": din("xp", [D, 512])}
    dcond = din("cond", [128, 16])
    dsrw = din("srw", [128, 512])
    dsmC = din("smC", [128, 1024])
    dsmn = din("smn", [128, 8])
    dsmm = din("smm", [128, 8])
    dprm = din("prm", [128, NPRM])
    dlora = din("lora", [128, 1536])
    dadaw = din("adaw", [12, 128, 1024])
    dadab = din("adab", [128, 12])
    ag2_in = nc.dram_tensor("ag2_in", [128, 24], F32)
    ag2_out = nc.dram_tensor("ag2_out", [4 * 128, 24], F32)
    dwin = din("win", [49, 128, 1024])
    dwin_s = din("win_s", [12, 128, 1024])
    dprm_s = din("prm_s", [128, NPRM])
    dlora_s = din("lora_s", [128, 1536])

    dwbr = din("wbr", [8, 128, 512])
    dwbm = din("wbm", [8, 128, 512])
    dwout = din("wout", [8, 128, 1024])
    dfup = din("fup", [44, 128, 1024])
    dfdn = din("fdn", [22, 128, 1024])
    dy = {"s": dout("ys", [D, 512]), "p": dout("yp", [D, 512])}
    dxw = din("xw", [D, 1024])
    ddlt = din("dlt", [128, 4])
    drowm = din("rowm", [128, 16])
    rs_in = nc.dram_tensor("rs_in", [4 * 4 * 2 * 128, 1024], BF16)
    rs_out = nc.dram_tensor("rs_out", [4 * 2 * 128, 1024], BF16)
    do_srw = dout("o_srw", [128, 2 * 2 * 4 * 64])
    do_smC = dout("o_smC", [128, 2 * 2 * 4 * 128])
    do_smn = dout("o_smn", [128, 16])
    do_smm = dout("o_smm", [128, 2])
    if dbg:
        dxm = {"s": dout("xmid_s", [D, 1024]), "p": dout("xmid_p", [D, 512])}
    else:
        dxm = {"s": nc.dram_tensor("xmid_s", [D, 1024], F32).ap(), "p": nc.dram_tensor("xmid_p", [D, 512], F32).ap()}
    dbg_out = {}
    if dbg:
        for g, T in (("s", 1024), ("p", 512)):
            dbg_out["yr_" + g] = nc.dram_tensor("dbg_yr_" + g, [128, 4 * T], BF16, kind="ExternalOutput").ap()
            dbg_out["ym_" + g] = nc.dram_tensor("dbg_ym_" + g, [128, 4 * T], BF16, kind="ExternalOutput").ap()
            dbg_out["h_" + g] = nc.dram_tensor("dbg_h_" + g, [128, 8 * T], BF16, kind="ExternalOutput").ap()
            dbg_out["mg_" + g] = nc.dram_tensor("dbg_mg_" + g, [128, 8 * T], BF16, kind="ExternalOutput").ap()
            dbg_out["f_" + g] = dout("dbg_f_" + g, [128, 8 * 512])
    dout_buf = Buf("dram_out")
    dxm_buf = {"s": Buf(), "p": Buf()}

    NCOL = 53200
    arena_t = es.enter_context(nc.sbuf_tensor("arena", [128, NCOL], F32))
    A = Arena(arena_t, NCOL)
    psum = [es.enter_context(nc.psum_tensor(f"ps{i}", [128, 512], F32)) for i in range(8)]
    psb = [Buf(f"ps{i}") for i in range(8)]
    ps_rr = [0]

    def PS():
        i = ps_rr[0]
        ps_rr[0] = (i + 1) % 8
        return psum[i], psb[i]

    POOL_OK = [False]

    def _e(e):
        return "dve" if (e == "pool" and not POOL_OK[0]) else e

    def TT(e, out, a, b, op, R, W):
        e = _e(e)
        P.op(e, lambda g: g.tensor_tensor(out=out, in0=a, in1=b, op=op), R, W)

    def TSC(e, out, a, s1, op0, R, W, s2=None, op1=None):
        e = _e(e)
        if s2 is None:
            P.op(e, lambda g: g.tensor_scalar(out=out, in0=a, scalar1=s1, scalar2=None, op0=op0), R, W)
        else:
            P.op(e, lambda g: g.tensor_scalar(out=out, in0=a, scalar1=s1, scalar2=s2, op0=op0, op1=op1), R, W)

    def STT(e, out, a, s, b, op0, op1, R, W):
        e = "dve"
        P.op(e, lambda g: g.scalar_tensor_tensor(out=out, in0=a, scalar=s, in1=b, op0=op0, op1=op1), R, W)

    def ACT(out, in_, f, R, W, bias=None, scale=None):
        kw = {}
        if bias is not None:
            kw["bias"] = bias
        if scale is not None:
            kw["scale"] = scale
        P.op("act", lambda g: g.activation(out=out, in_=in_, func=f, **kw), R, W)

    def CP(e, out, in_, R, W):
        e = _e(e)
        if e == "act":
            P.op("act", lambda g: g.copy(out=out, in_=in_), R, W)
        else:
            P.op(e, lambda g: g.tensor_copy(out=out, in_=in_), R, W)

    def MM(out, lhsT, rhs, R, W, start=True, stop=True):
        P.op("pe", lambda g: g.matmul(out, lhsT=lhsT, rhs=rhs, start=start, stop=stop), R, W)

    def RECIP(ap, b):
        P.op("dve", lambda g, ap=ap: g.reciprocal(out=ap, in_=ap), [b], [b])

    def MSET(e, ap, v, W):
        e = _e(e)
        P.op(e, lambda g: g.memset(ap, v), (), W)

    def v3(ap, b):
        return ap.rearrange("p (a b) -> p a b", b=b)

    cb = Buf("consts")
    ident = A.alloc(128)
    blk = A.alloc(128)
    onesf = A.alloc(128)
    triu = A.alloc(128)
    M1 = A.alloc(128)
    M2 = A.alloc(64)
    Ibc = A.alloc(64)
    rmask = A.alloc(512)
    selh = A.alloc(512)
    onesb = A.alloc(128, BF16)
    prm = A.alloc(NPRM)
    lora = A.alloc(1536, BF16)
    prm_s = A.alloc(NPRM)
    lora_s = A.alloc(1536, BF16)
    condt = A.alloc(16)
    sct = A.alloc(16, BF16)
    modT = A.alloc(96)
    mA1 = A.alloc(16); mB1 = A.alloc(16); mG1 = A.alloc(16); mA2 = A.alloc(16); mB2 = A.alloc(16); mG2 = A.alloc(16)
    em0 = A.alloc(8)
    epsc = A.alloc(4)

    def prmc(name, i=0, n=1):
        o, c = PRM[name]
        return prm[:, o + i:o + i + n]

    P.dma("sp", prm, dprm, (), [cb])
    P.dma("pool", lora, dlora, (), [cb])
    P.dma("sp", prm_s, dprm_s, (), [cb])
    P.dma("pool", lora_s, dlora_s, (), [cb])
    P.dma("sp", condt, dcond, (), [cb])
    P.dma("sp", em0, dsmm, (), [cb])
    MSET("pool", ident, 1.0, [cb])
    P.op("pool", lambda g: g.affine_select(out=ident, in_=ident, pattern=[[-1, 128]], compare_op=ALU.is_equal, fill=0.0, base=0, channel_multiplier=1), [cb], [cb])
    MSET("pool", onesf, 1.0, [cb])
    MSET("pool", onesb, 1.0, [cb])
    MSET("pool", blk, 0.0, [cb])
    MSET("pool", blk[0:64, 0:64], 1.0, [cb])
    MSET("pool", blk[64:128, 64:128], 1.0, [cb])
    MSET("pool", triu, 1.0, [cb])
    P.op("pool", lambda g: g.affine_select(out=triu, in_=triu, pattern=[[1, 128]], compare_op=ALU.is_ge, fill=0.0, base=0, channel_multiplier=-1), [cb], [cb])
    CP("pool", Ibc[0:64, :], ident[0:64, 0:64], [cb], [cb])
    CP("pool", Ibc[64:128, :], ident[64:128, 64:128], [cb], [cb])
    for hp in range(2):
        sl = slice(hp * 64, hp * 64 + 64)
        CP("pool", M1[sl, 64:128], triu[sl, hp * 64:hp * 64 + 64], [cb], [cb])
        TT("pool", M1[sl, 0:64], M1[sl, 64:128], Ibc[sl, :], ALU.subtract, [cb], [cb])
    for hp in range(2):
        sl = slice(hp * 64, hp * 64 + 64)
        TT("pool", M2[sl, :], onesf[sl, 0:64], triu[sl, hp * 64:hp * 64 + 64], ALU.subtract, [cb], [cb])
    MSET("pool", rmask, 1.0, [cb])
    MSET("pool", v3(rmask, 64)[:, :, 0:1], 0.0, [cb])
    MSET("pool", selh, 0.0, [cb])
    for h in range(4):
        for base in (0, 32):
            TSC("pool", v3(selh, 128)[base:base + 4, h, :], onesf[base:base + 4, :], ident[base:base + 4, base + h:base + h + 1], ALU.mult, [cb], [cb])
    MSET("pool", epsc, 1e-6, [cb])
    ACT(em0, em0, AF.Exp, [cb], [cb])

    NWB = 2
    wts = [A.alloc(1024, BF16) for _ in range(NWB)]
    wtb = [Buf(f"wt{i}") for i in range(NWB)]
    w_rr = [0]

    wpool = [None]

    def load_w(dram_chunk, ncols=1024):
        if wpool[0] is not None:
            lst, rr = wpool[0]
            i = rr[0]
            rr[0] = (i + 1) % len(lst)
            P.dma("pool", lst[i][0][:, 0:ncols], dram_chunk, (), [lst[i][1]])
            return lst[i]
        i = w_rr[0]
        w_rr[0] = (i + 1) % NWB
        P.dma("pool", wts[i][:, 0:ncols], dram_chunk, (), [wtb[i]])
        return wts[i], wtb[i]

    def proj(wdram, kcn, rhs_fn, R, ntile, cons):
        wt, wb = load_w(wdram, kcn * 128)
        for n in range(ntile):
            ps, pb = PS()
            for kc in range(kcn):
                MM(ps[:, :], wt[:, kc * 128:(kc + 1) * 128], rhs_fn(kc, n), R + [wb], [pb], start=(kc == 0), stop=(kc == kcn - 1))
            cons(n, ps, pb)

    ACT(sct, condt, AF.Silu, [cb], [cb])
    psm, psmb = PS()
    for c in range(12):
        wt, wb = load_w(dadaw[c])
        for kc in range(8):
            MM(psm[:, 2 * c:2 * c + 2], wt[:, kc * 128:(kc + 1) * 128], sct[:, 2 * kc:2 * kc + 2], [cb, wb], [psmb], start=(kc == 0), stop=(kc == 7))
    adab_sb = A.alloc(12)
    modp = A.alloc(24)
    P.dma("sp", adab_sb, dadab, (), [cb])
    TT("dve", v3(modp, 2), v3(psm[:, 0:24], 2), adab_sb.unsqueeze(2).to_broadcast([128, 12, 2]), ALU.add, [psmb, cb], [cb])
    ag2ib, ag2ob = Buf(), Buf()
    P.dma("pool", ag2_in.ap(), modp, [cb], [ag2ib])
    P.coll("pool", lambda g: g.collective_compute("AllGather", ALU.bypass, replica_groups=[[0, 2, 4, 6], [1, 3, 5, 7]],
                                                  ins=[ag2_in.ap().opt()], outs=[ag2_out.ap().opt()]), [ag2ib], [ag2ob])
    modb = Buf("mod")
    P.dma("pool", modT.rearrange("p (r c) -> p r c", r=4), ag2_out.ap().rearrange("(r p) c -> p r c", p=128), [ag2ob], [modb])
    m3 = v3(modT, 2)

    def modc(i):
        return m3[:, 8 * i:8 * i + 8, :]

    def ngb(i):
        o, _ = PRM[f"ng{i}"]
        return prm[:, o:o + 8].unsqueeze(2).to_broadcast([128, 8, 2])

    mod_done = [False]

    def derive_mod():
        if mod_done[0]:
            return
        mod_done[0] = True
        STT("dve", v3(mA1, 2), modc(1), 1.0, ngb(0), ALU.add, ALU.mult, [cb, modb], [modb])
        CP("dve", v3(mB1, 2), modc(0), [modb], [modb])
        TT("dve", v3(mG1, 2), modc(2), ngb(1), ALU.mult, [cb, modb], [modb])
        STT("dve", v3(mA2, 2), modc(4), 1.0, ngb(2), ALU.add, ALU.mult, [cb, modb], [modb])
        CP("dve", v3(mB2, 2), modc(3), [modb], [modb])
        TT("dve", v3(mG2, 2), modc(5), ngb(3), ALU.mult, [cb, modb], [modb])

    def mcol(m, kc, ci):
        return m[:, 2 * kc + ci:2 * kc + ci + 1]

    if dbg:
        dmod = dout("dbg_mod", [128, 96 + 16 * 6])
        P.dma("sp", dmod[:, 0:96], modT, [cb], [])
        for k_, m_ in enumerate([mA1, mB1, mG1, mA2, mB2, mG2]):
            P.dma("sp", dmod[:, 96 + 16 * k_:96 + 16 * (k_ + 1)], m_, [cb], [])
    base_mark = A.mark()
    shared = {}
    TOPB = 40000
    POOL_OK[0] = False

    def run_group(gn):
        isS = gn == "s"
        prm_g = prm_s if isS else prm
        lora_g = lora_s if isS else lora
        pairs = [0] if isS else list(range(4))
        heads = [0] if isS else list(range(4))

        def prmc(name, i=0, n=1):
            o, c = PRM[name]
            return prm_g[:, o + i:o + i + n]

        def wch(kind, idx=0):
            if isS:
                return dwin_s[{"r": 0, "k": 1, "v": 2, "wd": 3, "ad": 4, "gd": 5, "mq": 6, "mk": 7, "mv": 8, "mo": 9, "gi": 10, "gf": 11}[kind]]
            return dwin[{"r": 0, "k": 4, "v": 8, "wd": 12, "ad": 13, "gd": 14, "mq": 15, "mk": 19, "mv": 23, "mo": 27, "gi": 47, "gf": 48}[kind] + idx]
        T = TS if isS else 512
        nseq = 1 if isS else 2
        Tq = T // nseq
        NT = T // 512
        ci_ = 1 if isS else 0
        xd = dx[gn]
        A.release(base_mark)
        hT = A.alloc(8 * T, BF16)
        hTb = [Buf(f"hT{n}") for n in range(NT)]
        h3 = v3(hT, T)
        hmark = A.mark()
        yr = A.alloc((1 if isS else 4) * T, BF16)
        if not isS:
            ym = A.alloc(4 * T, BF16)
            ymb = Buf("ym")
        yrb = Buf("yr")
        grp_mark = A.mark()

        def x_view(dram, n):
            return dram.rearrange("(kc p) t -> p kc t", p=128)[:, :, n * 512:(n + 1) * 512]

        def norm1(xt, xtb, sq, sqb, rstd, rstdb):
            ACT(sq, xt, AF.Square, [xtb], [sqb])
            ps, pb = PS()
            for kc in range(8):
                MM(ps[:, :], onesb, sq[:, kc * 512:(kc + 1) * 512], [cb, sqb], [pb], start=(kc == 0), stop=(kc == 7))
            ACT(rstd, ps[:, :], AF.Sqrt, [pb], [rstdb], bias=epsc[:, 0:1], scale=1.0 / D)
            RECIP(rstd, rstdb)

        def norm2(xt, xtb, mA, mB, out_fn, outb, tmp, tmpb, rstd, rstdb):
            derive_mod()
            for kc in range(8):
                STT("dve", tmp[:, kc * 512:(kc + 1) * 512], xt[:, kc * 512:(kc + 1) * 512], mcol(mA, kc, ci_), rstd, ALU.mult, ALU.mult, [xtb, rstdb, modb], [tmpb])
            for kc in range(8):
                ACT(out_fn(kc), tmp[:, kc * 512:(kc + 1) * 512], AF.Identity, [tmpb, modb], [outb], bias=mcol(mB, kc, ci_))

        def norm_mod(xt, xtb, mA, mB, out_fn, outb, tmp, tmpb, sq, sqb, rstd, rstdb):
            norm1(xt, xtb, sq, sqb, rstd, rstdb)
            norm2(xt, xtb, mA, mB, out_fn, outb, tmp, tmpb, rstd, rstdb)

        def norm1va(xv, xvb, sq, sqb, rstd, rstdb):
            ACT(v3(sq, 512), xv, AF.Square, [xvb], [sqb])
            ps, pb = PS()
            for kc in range(8):
                MM(ps[:, :], onesb, sq[:, kc * 512:(kc + 1) * 512], [cb, sqb], [pb], start=(kc == 0), stop=(kc == 7))
            ACT(rstd, ps[:, :], AF.Sqrt, [pb], [rstdb], bias=epsc[:, 0:1], scale=1.0 / D)

        def norm_tiles_v(n_tiles, load_fn, mA, mB, out_bufs, xviews, nb_):
            load_fn(0)
            norm1va(xviews[0][0], xviews[0][1], nb_[0][2], nb_[0][3], nb_[0][4], nb_[0][5])
            RECIP(nb_[0][4], nb_[0][5])
            for n in range(n_tiles):
                i = n % 2
                j = (n + 1) % 2
                xv, xvb = xviews[n]
                tmp, tmpb, rstd, rstdb = nb_[i][0], nb_[i][1], nb_[i][4], nb_[i][5]
                if n + 1 < n_tiles:
                    load_fn(n + 1)
                    norm1va(xviews[n + 1][0], xviews[n + 1][1], nb_[j][2], nb_[j][3], nb_[j][4], nb_[j][5])
                derive_mod()
                for kc in range(8):
                    STT("dve", tmp[:, kc * 512:(kc + 1) * 512], xv[:, kc, :], mcol(mA, kc, ci_), rstd, ALU.mult, ALU.mult, [xvb, rstdb, modb], [tmpb])
                if n + 1 < n_tiles:
                    RECIP(nb_[j][4], nb_[j][5])
                for kc in range(8):
                    ACT(h3[:, kc, n * 512:(n + 1) * 512], tmp[:, kc * 512:(kc + 1) * 512], AF.Identity, [tmpb, modb], [out_bufs[n]], bias=mcol(mB, kc, ci_))

        def norm_tiles(n_tiles, load_fn, mA, mB, out_bufs, xt, xtb, nb_):
            load_fn(0)
            norm1(xt[0], xtb[0], nb_[0][2], nb_[0][3], nb_[0][4], nb_[0][5])
            for n in range(n_tiles):
                i = n % 2
                if n + 1 < n_tiles:
                    j = (n + 1) % 2
                    load_fn(n + 1)
                    norm1(xt[j], xtb[j], nb_[j][2], nb_[j][3], nb_[j][4], nb_[j][5])
                norm2(xt[i], xtb[i], mA, mB, lambda kc, n=n: h3[:, kc, n * 512:(n + 1) * 512], out_bufs[n], nb_[i][0], nb_[i][1], nb_[i][4], nb_[i][5])

        mk = A.mark()
        nb_ = [(A.alloc(4096), Buf(), A.alloc(4096, BF16), Buf(), A.alloc(512), Buf()) for _ in range(2)]
        if NT >= 2:
            xbig = [A.alloc(8192) for _ in range(2)]
            xbigb = [Buf() for _ in range(2)]
            xt = [None] * NT
            xtb = [None] * NT

            def load_big(n):
                if n % 2 == 0:
                    k = (n // 2) % 2
                    P.dma("sp", v3(xbig[k], 1024), xd.rearrange("(kc p) t -> p kc t", p=128)[:, :, n * 512:n * 512 + 1024], (), [xbigb[k]])

            class _XV:
                pass
            xviews = [(v3(xbig[(n // 2) % 2], 1024)[:, :, (n % 2) * 512:(n % 2) * 512 + 512], xbigb[(n // 2) % 2]) for n in range(NT)]
            norm_tiles_v(NT, load_big, mA1, mB1, hTb, xviews, nb_)
        else:
            xt = [A.alloc(4096) for _ in range(2)]
            xtb = [Buf() for _ in range(2)]
            norm_tiles(NT, lambda n: P.dma("sp", v3(xt[n % 2], 512), x_view(xd, n), (), [xtb[n % 2]]), mA1, mB1, hTb, xt, xtb, nb_)
        if dbg and not isS:
            P.dma("sp", dbg_out["h_" + gn], hT, hTb, [])
        P.barrier()
        A.release(mk)

        def h_rhs(kc, n):
            return h3[:, kc, n * 512:(n + 1) * 512]

        mk_r = A.mark()
        if not isS:
            wpool[0] = ([(A.alloc(1024, BF16), Buf()) for _ in range(3)], [0])
        Tp = Tq + 2
        twd = A.alloc(T, BF16); ad_ = A.alloc(T, BF16); sgd = A.alloc(T, BF16)
        lb = Buf("lora_in")
        zp = A.alloc(nseq * Tp); zpb = Buf("zp")
        zp3 = v3(zp, Tp)
        MSET("pool", zp, 0.0, [zpb])
        stmp_blk = A.alloc(max(T, 2048)); stb = Buf()
        stmp = stmp_blk[:, 0:T]

        def shift_proj(wap, chunk, out, outb, post=None):
            def cons(n, ps, pb):
                if isS:
                    CP("act", zp3[:, 0, 1 + n * 512:1 + (n + 1) * 512], ps[:, :], [pb], [zpb])
                else:
                    CP("act", zp3[:, :, 1:1 + Tq], v3(ps[:, :], Tq), [pb], [zpb])
            proj(wap, 8, h_rhs, hTb, NT, cons)
            zc = zp3[:, :, 1:1 + Tq]
            s3 = v3(stmp, Tq)
            TT("pool", s3, zp3[:, :, 0:Tq], zp3[:, :, 2:2 + Tq], ALU.add, [zpb], [stb])
            STT("dve", s3, s3, 0.5, zc, ALU.mult, ALU.subtract, [stb, zpb], [stb])
            if post is None:
                STT("dve", v3(out, Tq), s3, prmc("mu", chunk), zc, ALU.mult, ALU.add, [stb, zpb, cb], [outb])
            else:
                STT("dve", s3, s3, prmc("mu", chunk), zc, ALU.mult, ALU.add, [stb, zpb, cb], [stb])
                ACT(out, stmp, post, [stb], [outb])

        shift_proj(wch("wd"), 12, twd, lb, AF.Tanh)
        shift_proj(wch("ad"), 13, ad_, lb, AF.Identity)
        shift_proj(wch("gd"), 14, sgd, lb, AF.Sigmoid)

        NPB = 1 if isS else 2
        PA = []
        for _ in range(NPB):
            PA.append({k_: (A.alloc(T), Buf(k_)) for k_ in ("rs", "ks", "vs", "kk", "bacc", "Y")})
        tn = {}
        for name in ["sgw", "aa", "t1", "kd", "G", "eGn", "eGx", "KH", "BH", "t2", "VSp", "PT0", "PT1"]:
            tn[name] = (A.alloc(512), Buf(name))
        nblk = stmp_blk
        nbuf = stb
        for i_, name in enumerate(["N0", "N1", "NT0", "NT1"]):
            tn[name] = (nblk[:, i_ * 512:(i_ + 1) * 512], nbuf)
        MS = []
        for _ in range(2):
            M = {}
            for name, sz in (("AR", 1024), ("A1", 1024), ("A2", 1024), ("KT", 512), ("BT", 512), ("VT", 512), ("eG", 512), ("TT", 512)):
                M[name] = (A.alloc(sz), Buf(name))
            MS.append(M)
        ST = [(A.alloc(64), Buf("ST0")), (A.alloc(64), Buf("ST1"))]
        Xt, Xb = A.alloc(64), Buf("X")
        Ut, Ub = A.alloc(64), Buf("U")
        sttmp, sttb = A.alloc(64), Buf()
        lorav = v3(lora_g, 512)
        srw_sb = A.alloc(512); srwb = Buf()
        if isS:
            P.dma("sp", srw_sb, dsrw, (), [srwb])
        srw4 = srw_sb.rearrange("p (d q v) -> p d q v", d=2, q=4)
        osrw5 = do_srw.rearrange("p (s d q v) -> p s d q v", s=2, d=2, q=4)

        def setup_pair(p):
            pa = PA[p % NPB]
            rs, rsb = pa["rs"]; ks, ksb = pa["ks"]; vs, vsb = pa["vs"]; kk, kkb = pa["kk"]
            shift_proj(wch("r", p), p, rs, rsb)
            shift_proj(wch("k", p), 4 + p, ks, ksb)
            shift_proj(wch("v", p), 8 + p, vs, vsb)
            for n in range(NT):
                sl = slice(n * 512, (n + 1) * 512)
                t1, t1b = tn["t1"]; t2, t2b = tn["t2"]
                TSC("pool", t1, ks[:, sl], prmc("kks", p), ALU.mult, [ksb, cb], [t1b])
                TT("pool", t2, t1, t1, ALU.mult, [t1b], [t2b])
                ps, pb = PS()
                MM(ps[:, :], blk, t2, [cb, t2b], [pb])
                TSC("dve", t2, ps[:, :], 1e-24, ALU.max, [pb], [t2b])
                ACT(t2, t2, AF.Sqrt, [t2b], [t2b])
                RECIP(t2, t2b)
                TT("dve", kk[:, sl], t1, t2, ALU.mult, [t1b, t2b], [kkb])

        def stageA(p, d, m, M):
            pa = PA[p % NPB]
            rs, rsb = pa["rs"]; ks, ksb = pa["ks"]; vs, vsb = pa["vs"]; kk, kkb = pa["kk"]; bacc, baccb = pa["bacc"]
            drows = slice(d * 64, d * 64 + 64)
            n = m if d == 0 else NT - 1 - m
            sl = slice(n * 512, (n + 1) * 512)

            def V(ap):
                a_ = ap[:, sl]
                return a_ if d == 0 else a_[:, ::-1]

            def rv(ap):
                return ap if d == 0 else ap[:, ::-1]
            sgw, sgwb = tn["sgw"]; aa, aab = tn["aa"]; t1, t1b = tn["t1"]; kd, kdb = tn["kd"]
            G, Gb = tn["G"]; eGn, eGnb = tn["eGn"]; eGx, eGxb = tn["eGx"]
            KH, KHb = tn["KH"]; BH, BHb = tn["BH"]; t2, t2b = tn["t2"]; VSp, VSpb = tn["VSp"]
            eG, eGb = M["eG"]; KT, KTb = M["KT"]; BT, BTb = M["BT"]; VT, VTb = M["VT"]
            ARt, ARb = M["AR"]; A1t, A1b = M["A1"]; A2t, A2b = M["A2"]; TTf, TTfb = M["TT"]
            ps, pb = PS()
            MM(ps[:, :], lorav[drows, 0, p * 128:(p + 1) * 128], twd[drows, sl], [cb, lb], [pb])
            ACT(sgw, ps[:, :], AF.Sigmoid, [pb, cb], [sgwb], bias=prmc("w0", d * 4 + p))
            ps, pb = PS()
            MM(ps[:, :], lorav[drows, 1, p * 128:(p + 1) * 128], ad_[drows, sl], [cb, lb], [pb])
            ACT(aa, ps[:, :], AF.Sigmoid, [pb, cb], [aab], bias=prmc("a0", d * 4 + p))
            yield
            TSC("pool", t1, aa, -1.0, ALU.add, [aab, cb], [t1b], s2=prmc("ka", p), op1=ALU.mult)
            STT("pool", kd, t1, 1.0, ks[:, sl], ALU.add, ALU.mult, [t1b, ksb], [kdb])
            if d == 0:
                STT("pool", bacc[:, sl], rs[:, sl], prmc("rk", p), kd, ALU.mult, ALU.mult, [rsb, kdb, cb], [baccb])
            else:
                STT("pool", t1, rs[:, sl], prmc("rk", p), kd, ALU.mult, ALU.mult, [rsb, kdb, cb], [t1b])
                TT("pool", bacc[:, sl], bacc[:, sl], t1, ALU.add, [baccb, t1b], [baccb])
            yield
            P.op("dve", lambda g, G=G, sgw=sgw, d=d: g.tensor_tensor_scan(out=G, data0=rmask, data1=(sgw if d == 0 else sgw[:, ::-1]), initial=0.0, op0=ALU.mult, op1=ALU.add), [sgwb, cb], [Gb])
            ACT(eG, G, AF.Exp, [Gb], [eGb], scale=-DSC)
            ACT(eGn, G, AF.Exp, [Gb], [eGnb], scale=DSC)
            TT("pool", t2, G, rv(sgw), ALU.subtract, [Gb, sgwb], [t2b])
            ACT(eGx, t2, AF.Exp, [t2b], [eGxb], scale=-DSC)
            yield
            AR4 = ARt.rearrange("p (c two l) -> p c two l", two=2, l=64)
            TT("dve", AR4[:, :, 1, :], v3(V(rs), 64), v3(eG, 64), ALU.mult, [rsb, eGb], [ARb])
            STT("pool", AR4[:, :, 0, :], v3(V(kk), 64), -1.0, v3(eGx, 64), ALU.mult, ALU.mult, [kkb, eGxb], [ARb])
            yield
            TT("dve", KH, rv(kd), eGn, ALU.mult, [kdb, eGnb], [KHb])
            TT("pool", t2, V(kk), rv(aa), ALU.mult, [kkb, aab], [t2b])
            TT("pool", BH, t2, eGn, ALU.mult, [t2b, eGnb], [BHb])
            CP("pool", VSp, V(vs), [vsb], [VSpb])
            yield
            for (src, srcb, dst, dstb) in ((KH, KHb, KT, KTb), (BH, BHb, BT, BTb), (VSp, VSpb, VT, VTb)):
                ps, pb = PS()
                for c in range(8):
                    for hp in range(2):
                        hr = slice(hp * 64, hp * 64 + 64)
                        MM(ps[hr, c * 64:(c + 1) * 64], src[hr, c * 64:(c + 1) * 64], ident[hr, hr], [srcb, cb], [pb])
                CP("act", dst, ps[:, :], [pb], [dstb])
                yield
            for (lh, lhb, dst, dstb) in ((BH, BHb, A1t, A1b), (KH, KHb, A2t, A2b)):
                for half in range(2):
                    ps, pb = PS()
                    for c4 in range(4):
                        c = half * 4 + c4
                        for hp in range(2):
                            hr = slice(hp * 64, hp * 64 + 64)
                            MM(ps[hr, c4 * 128:(c4 + 1) * 128], lh[hr, c * 64:(c + 1) * 64], ARt[hr, c * 128:(c + 1) * 128], [lhb, ARb], [pb])
                    TT("dve", v3(dst[:, half * 512:(half + 1) * 512], 128), v3(ps[:, :], 128), M1.unsqueeze(1).to_broadcast([128, 4, 128]), ALU.mult, [pb, cb], [dstb])
                    yield
            N0, N0b = tn["N0"]; N1, N1b = tn["N1"]; NT0, NT0b = tn["NT0"]; NT1, NT1b = tn["NT1"]
            PT0, PT0b = tn["PT0"]; PT1, PT1b = tn["PT1"]
            ps, pb = PS()
            for c in range(8):
                for hp in range(2):
                    hr = slice(hp * 64, hp * 64 + 64)
                    MM(ps[hr, c * 64:(c + 1) * 64], AR4[hr, c, 0, :], BH[hr, c * 64:(c + 1) * 64], [ARb, BHb], [pb])
            TT("dve", v3(N0, 64), v3(ps[:, :], 64), M2.unsqueeze(1).to_broadcast([128, 8, 64]), ALU.mult, [pb, cb], [N0b])
            A13 = v3(A1t, 128)
            CP("pool", v3(NT0, 64), A13[:, :, 0:64], [A1b], [NT0b])
            TT("pool", v3(PT0, 64), A13[:, :, 0:64], Ibc.unsqueeze(1).to_broadcast([128, 8, 64]), ALU.add, [A1b, cb], [PT0b])
            yield
            Ncur, Ncb, NTcur, NTcb, PTc, PTcb = N0, N0b, NT0, NT0b, PT0, PT0b
            Nnx, Nnb, NTnx, NTnb, PTn, PTnb = N1, N1b, NT1, NT1b, PT1, PT1b
            for lev in range(1, 6):
                ps, pb = PS()
                for c in range(8):
                    for hp in range(2):
                        hr = slice(hp * 64, hp * 64 + 64)
                        cs = slice(c * 64, (c + 1) * 64)
                        MM(ps[hr, cs], NTcur[hr, cs], Ncur[hr, cs], [NTcb, Ncb], [pb])
                if lev < 5:
                    ps2, pb2 = PS()
                    for c in range(8):
                        for hp in range(2):
                            hr = slice(hp * 64, hp * 64 + 64)
                            cs = slice(c * 64, (c + 1) * 64)
                            MM(ps2[hr, cs], Ncur[hr, cs], NTcur[hr, cs], [NTcb, Ncb], [pb2])
                CP("act", Nnx, ps[:, :], [pb], [Nnb])
                if lev < 5:
                    CP("act", NTnx, ps2[:, :], [pb2], [NTnb])
                yield
                ps3, pb3 = PS()
                for c in range(8):
                    for hp in range(2):
                        hr = slice(hp * 64, hp * 64 + 64)
                        cs = slice(c * 64, (c + 1) * 64)
                        MM(ps3[hr, cs], Nnx[hr, cs], PTc[hr, cs], [Nnb, PTcb], [pb3])
                if lev == 5:
                    TT("dve", TTf, ps3[:, :], PTc, ALU.add, [pb3, PTcb], [TTfb])
                else:
                    TT("dve", PTn, ps3[:, :], PTc, ALU.add, [pb3, PTcb], [PTnb])
                yield
                Ncur, Ncb, Nnx, Nnb = Nnx, Nnb, Ncur, Ncb
                NTcur, NTcb, NTnx, NTnb = NTnx, NTnb, NTcur, NTcb
                PTc, PTcb, PTn, PTnb = PTn, PTnb, PTc, PTcb

        def stageB(p, d, m, M):
            pa = PA[p % NPB]
            Y, Yb = pa["Y"]
            n = m if d == 0 else NT - 1 - m
            sl = slice(n * 512, (n + 1) * 512)
            eG, eGb = M["eG"]; KT, KTb = M["KT"]; BT, BTb = M["BT"]; VT, VTb = M["VT"]
            ARt, ARb = M["AR"]; A1t, A1b = M["A1"]; A2t, A2b = M["A2"]; TTm, TTb = M["TT"]
            AR4 = ARt.rearrange("p (c two l) -> p c two l", two=2, l=64)
            A13 = v3(A1t, 128)
            A23 = v3(A2t, 128)
            eG3 = v3(eG, 64)
            for c in range(8):
                cs = slice(c * 64, (c + 1) * 64)
                if isS:
                    seq = 0
                    first = (m == 0 and c == 0)
                    last = False
                else:
                    seq = (c // 4) if d == 0 else 1 - (c // 4)
                    first = (c % 4 == 0)
                    last = (c % 4 == 3)
                Sc, Scb = ST[0]
                Sn, Snb = ST[1]
                if first:
                    if isS:
                        CP("pool", Sc, srw4[:, d, p, :], [srwb], [Scb])
                    else:
                        MSET("pool", Sc, 0.0, [Scb])
                ps, pb = PS()
                for hp in range(2):
                    hr = slice(hp * 64, hp * 64 + 64)
                    MM(ps[hr, 0:64], A23[hr, c, 0:64], VT[hr, cs], [A2b, VTb], [pb], start=True, stop=False)
                    MM(ps[hr, 0:64], AR4[hr, c, 0, :], Sc[hr, :], [ARb, Scb], [pb], start=False, stop=True)
                CP("act", Xt, ps[:, 0:64], [pb], [Xb])
                yield
                ps, pb = PS()
                for hp in range(2):
                    hr = slice(hp * 64, hp * 64 + 64)
                    MM(ps[hr, 0:64], TTm[hr, cs], Xt[hr, :], [TTb, Xb], [pb])
                CP("dve", Ut, ps[:, 0:64], [pb], [Ub])
                yield
                pss, pbs = PS()
                for hp in range(2):
                    hr = slice(hp * 64, hp * 64 + 64)
                    MM(pss[hr, 0:64], BT[hr, cs], Ut[hr, :], [BTb, Ub], [pbs], start=True, stop=False)
                    MM(pss[hr, 0:64], KT[hr, cs], VT[hr, cs], [KTb, VTb], [pbs], start=False, stop=True)
                psy, pby = PS()
                for hp in range(2):
                    hr = slice(hp * 64, hp * 64 + 64)
                    MM(psy[hr, 0:64], Sc[hr, :], AR4[hr, c, 1, :], [Scb, ARb], [pby], start=True, stop=False)
                    MM(psy[hr, 0:64], Ut[hr, :], A13[hr, c, 64:128], [Ub, A1b], [pby], start=False, stop=False)
                    MM(psy[hr, 0:64], VT[hr, cs], A23[hr, c, 64:128], [VTb, A2b], [pby], start=False, stop=True)
                TT("dve", sttmp, pss[:, 0:64], Sc, ALU.add, [pbs, Scb], [sttb])
                TSC("dve", Sn, sttmp, eG3[:, c, 63:64], ALU.mult, [sttb, eGb], [Snb])
                ydst = Y[:, sl][:, cs] if d == 0 else Y[:, sl][:, ::-1][:, cs]
                if d == 0:
                    CP("act", ydst, psy[:, 0:64], [pby], [Yb])
                else:
                    TT("dve", ydst, psy[:, 0:64], ydst, ALU.add, [pby, Yb], [Yb])
                ST[0], ST[1] = ST[1], ST[0]
                if last:
                    P.dma("sp", osrw5[:, seq, d, p, :], ST[0][0], [ST[0][1]], [])
                yield

        def finalize_pair(p):
            pa = PA[p % NPB]
            vs, vsb = pa["vs"]; bacc, baccb = pa["bacc"]; Y, Yb = pa["Y"]
            for n in range(NT):
                sl = slice(n * 512, (n + 1) * 512)
                t1, t1b = tn["t1"]; t2, t2b = tn["t2"]; kd, kdb = tn["kd"]
                ps, pb = PS()
                MM(ps[:, :], blk, Y[:, sl], [cb, Yb], [pb])
                TT("pool", t1, Y[:, sl], Y[:, sl], ALU.mult, [Yb], [t1b])
                ps2, pb2 = PS()
                MM(ps2[:, :], blk, t1, [cb, t1b], [pb2])
                TSC("dve", t2, ps[:, :], 1.0 / 64, ALU.mult, [pb], [t2b])
                STT("dve", kd, t2, -1.0, t2, ALU.mult, ALU.mult, [t2b], [kdb])
                STT("dve", kd, ps2[:, :], 1.0 / 64, kd, ALU.mult, ALU.add, [pb2, kdb], [kdb])
                TSC("dve", kd, kd, 64e-5, ALU.add, [kdb], [kdb])
                ACT(kd, kd, AF.Sqrt, [kdb], [kdb])
                RECIP(kd, kdb)
                TT("dve", t2, Y[:, sl], t2, ALU.subtract, [Yb, t2b], [t2b])
                TT("dve", t2, t2, kd, ALU.mult, [t2b, kdb], [t2b])
                TSC("dve", t2, t2, prmc("lng", p), ALU.mult, [t2b, cb], [t2b], s2=prmc("lnb", p), op1=ALU.add)
                ps3, pb3 = PS()
                MM(ps3[:, :], blk, bacc[:, sl], [cb, baccb], [pb3])
                TT("dve", t1, ps3[:, :], vs[:, sl], ALU.mult, [pb3, vsb], [t1b])
                TT("dve", t2, t2, t1, ALU.add, [t2b, t1b], [t2b])
                ps4, pb4 = PS()
                MM(ps4[:, :], lorav[:, 2, p * 128:(p + 1) * 128], sgd[:, sl], [cb, lb], [pb4])
                TT("dve", v3(yr, T)[:, p, sl], t2, ps4[:, :], ALU.mult, [t2b, pb4], [yrb])

        def run_rr_g(gens):
            gens = list(gens)
            while gens:
                for g_ in list(gens):
                    try:
                        next(g_)
                    except StopIteration:
                        gens.remove(g_)
                yield

        def rwkv_body():
            units = [(p, d, m) for p in pairs for d in range(2) for m in range(NT)]
            prev = None
            for idx, u in enumerate(units + [None]):
                gens = []
                if u is not None:
                    if u[1] == 0 and u[2] == 0:
                        setup_pair(u[0])
                        yield
                    gens.append(stageA(u[0], u[1], u[2], MS[idx % 2]))
                if prev is not None:
                    gens.append(stageB(prev[0], prev[1], prev[2], MS[(idx - 1) % 2]))
                yield from run_rr_g(gens)
                if prev is not None and prev[1] == 1 and prev[2] == NT - 1:
                    finalize_pair(prev[0])
                    yield
                prev = u

        gR = rwkv_body()
        if isS:
            for _ in gR:
                pass
        if isS:
            P.barrier()
            A.release(mk_r)

        if isS:
            ym = A.alloc(T, BF16)
            ymb = Buf("ym")
        mk_m = A.mark()
        BB = A.alloc(nseq * (Tq + 1)); PSI = A.alloc(T); bbb, psib = Buf(), Buf()
        BB3 = v3(BB, Tq + 1)
        NC = Tq // 128
        PSIT = A.alloc(nseq * 2 * NC * 4); psitb = Buf()
        Mrow = A.alloc(2); mrowb = Buf()
        mtmp = A.alloc(2); mtb = Buf()
        mk_gate = A.mark()
        GI = A.alloc(T); GF = A.alloc(T); gib, gfb = Buf(), Buf()

        def mlstm_body():
            def cons_g(dst, dstb, bias):
                def cons(n, ps, pb):
                    ACT(dst[:, n * 512:(n + 1) * 512], ps[:, :], AF.Identity, [pb, cb], [dstb], bias=bias)
                return cons
            proj(wch("gi"), 8, h_rhs, hTb, NT, cons_g(GI, gib, prmc("gbi")))
            yield
            proj(wch("gf"), 8, h_rhs, hTb, NT, cons_g(GF, gfb, prmc("gbf")))
            yield
            ACT(GF, GF, AF.Exp, [gfb], [gfb], scale=-1.0)
            ACT(GF, GF, AF.Ln, [gfb], [gfb], bias=onesf[:, 0:1])
            MSET("pool", BB, 0.0, [bbb])
            MSET("pool", PSI, 0.0, [psib])
            for d in range(2):
                rr = slice(32 * d, 32 * d + 4)
                for s in range(nseq):
                    ss = slice(s * Tq, (s + 1) * Tq)
                    src = GF[rr, ss] if d == 0 else GF[rr, ss][:, ::-1]
                    P.op("dve", lambda g, s=s, rr=rr, src=src: g.tensor_tensor_scan(out=BB3[rr, s, 1:Tq + 1], data0=onesf[rr, 0:1].to_broadcast([4, Tq]), data1=src, initial=0.0, op0=ALU.mult, op1=ALU.subtract), [gfb, cb], [bbb])
                    gsrc = GI[rr, ss] if d == 0 else GI[rr, ss][:, ::-1]
                    TT("dve", PSI[rr, ss], gsrc, BB3[rr, s, 1:Tq + 1], ALU.subtract, [gib, bbb], [psib])
            PSIT5 = PSIT.rearrange("p (s d c h) -> p s d c h", s=nseq, d=2, c=NC)
            ps, pb = PS()
            psv = ps[:, 0:nseq * 2 * NC * 4].rearrange("p (s d c h) -> p s d c h", s=nseq, d=2, c=NC)
            for s in range(nseq):
                for d in range(2):
                    rr = slice(32 * d, 32 * d + 4)
                    for c in range(NC):
                        MM(psv[:, s, d, c, :], PSI[rr, s * Tq + c * 128:s * Tq + (c + 1) * 128], ident[rr, 32 * d:32 * d + 4], [psib, cb], [pb])
            CP("dve", PSIT, ps[:, 0:nseq * 2 * NC * 4], [pb], [psitb])
            if not isS:
                P.op("dve", lambda g: g.tensor_reduce(out=mtmp, in_=v3(PSI, Tq), axis=AX.X, op=ALU.max), [psib], [mtb])
                TSC("dve", mtmp, mtmp, 0.0, ALU.max, [mtb], [mtb])
                TT("dve", Mrow, mtmp, BB3[:, :, Tq], ALU.add, [mtb, bbb], [mrowb])
                P.dma("sp", do_smm, Mrow, [mrowb], [])
            if isS:
                P.barrier()
                A.release(mk_gate)
            yield

            Q = [A.alloc(T), A.alloc(T)]; K = [A.alloc(T), A.alloc(T)]; Vv = [A.alloc(T), A.alloc(T)]
            Qb, Kb, Vb = [Buf(), Buf()], [Buf(), Buf()], [Buf(), Buf()]
            Hs = [(A.alloc(T), Buf("H0")), (A.alloc(T), Buf("H1"))]
            Hh, Hb = Hs[0]
            if isS:
                cR, cW = 34, 66
            else:
                cR, cW = 2, Tq + 2
            cpad = A.alloc(cR * cW, BF16); cpb = Buf()
            cp3 = v3(cpad, cW)
            MSET("pool", cpad, 0.0, [cpb])
            dgm = A.alloc(9 * 128, BF16); dgmb = Buf()
            BBC = [A.alloc(nseq * (Tq + 1)), A.alloc(nseq * (Tq + 1))]; bbcb = [Buf(), Buf()]
            NBC = A.alloc(nseq * 2 * (NC + 1)); nbcb = Buf()
            MB = []
            for _ in range(2):
                B_ = {}
                for nm_, sz_ in (("DT", 128), ("PT", 128), ("E", 128), ("qe", 128), ("KW", 128), ("VT", 128), ("om", 2), ("dec", 2),
                                 ("CT", 128), ("NM", 128), ("dmx", 128), ("emt", 2), ("cout", 128)):
                    B_[nm_] = (A.alloc(sz_), Buf(nm_))
                MB.append(B_)
            nout, noutb = A.alloc(16), Buf()
            smC_sb = A.alloc(1024); smn_sb = A.alloc(8); smb = Buf()
            if isS:
                P.dma("sp", smC_sb, dsmC, (), [smb])
                P.dma("sp", smn_sb, dsmn, (), [smb])
            smC4 = smC_sb.rearrange("p (d h v) -> p d h v", d=2, h=4)
            osmC5 = do_smC.rearrange("p (s d h v) -> p s d h v", s=2, d=2, h=4)
            so, sob = A.alloc(512), Buf()
            gt1, gt1b = A.alloc(512), Buf(); gt2, gt2b = A.alloc(512), Buf(); gt3, gt3b = A.alloc(512), Buf()

            def conv_proj(chunk, wname, widx_fn, out, outb, post_scale):
                def cons(n, ps, pb):
                    if isS:
                        CP("act", cp3[:, 1 + 8 * n:9 + 8 * n, 1:65], v3(ps[:, :], 64), [pb], [cpb])
                    else:
                        CP("act", cp3[:, :, 1:1 + Tq], v3(ps[:, :], Tq), [pb], [cpb])
                proj(chunk, 8, h_rhs, hTb, NT, cons)
                taps_ = [(dr, dc) for dr in (-1, 0, 1) for dc in (-1, 0, 1)] if isS else [(0, dc) for dc in (-1, 0, 1)]
                dgm3 = v3(dgm, 128)
                for i_, (dr, dc) in enumerate(taps_):
                    TSC("dve", dgm3[:, i_, :], ident, widx_fn((dr + 1) * 3 + (dc + 1)), ALU.mult, [cb], [dgmb])
                for n in range(NT):
                    ps, pb = PS()
                    for i_, (dr, dc) in enumerate(taps_):
                        if isS:
                            view = cp3[:, 1 + dr + 8 * n:9 + dr + 8 * n, 1 + dc:65 + dc]
                            po = v3(ps[:, :], 64)
                        else:
                            view = cp3[:, :, 1 + dc:1 + dc + Tq]
                            po = v3(ps[:, :], Tq)
                        MM(po, dgm3[:, i_, :], view, [dgmb, cpb], [pb], start=(i_ == 0), stop=(i_ == len(taps_) - 1))
                    ACT(out[:, n * 512:(n + 1) * 512], ps[:, :], AF.Silu, [pb], [outb])
                if post_scale != 1.0:
                    TSC("pool", out, out, post_scale, ALU.mult, [outb], [outb])

            def conv_apply(wname, widx_fn, out, outb, bias_col):
                taps = [(dr, dc) for dr in (-1, 0, 1) for dc in (-1, 0, 1)] if isS else [(0, dc) for dc in (-1, 0, 1)]
                o3 = v3(out, 64) if isS else v3(out, Tq)
                e = "dve"
                for i, (dr, dc) in enumerate(taps):
                    if isS:
                        view = cp3[:, 1 + dr:33 + dr, 1 + dc:65 + dc]
                    else:
                        view = cp3[:, :, 1 + dc:1 + dc + Tq]
                    wcol = widx_fn((dr + 1) * 3 + (dc + 1))
                    if i == 0:
                        if bias_col is None:
                            TSC(e, o3, view, wcol, ALU.mult, [cpb, cb], [outb])
                        else:
                            TSC(e, o3, view, wcol, ALU.mult, [cpb, cb], [outb], s2=bias_col, op1=ALU.add)
                    else:
                        STT(e, o3, view, wcol, o3, ALU.mult, ALU.add, [cpb, cb, outb], [outb])

            omc, _ = PRM["mconv"]
            for h in heads:
                conv_proj(wch("mq", h), "mconv", lambda tap, h=h: prm_g[:, omc + tap * 8 + h:omc + tap * 8 + h + 1], Q[0], Qb[0], 128 ** -0.5)
                yield
                conv_proj(wch("mk", h), "mconv", lambda tap, h=h: prm_g[:, omc + tap * 8 + 4 + h:omc + tap * 8 + 4 + h + 1], K[0], Kb[0], 1.0)

                def cons_v(n, ps, pb):
                    CP("act", Vv[0][:, n * 512:(n + 1) * 512], ps[:, :], [pb], [Vb[0]])
                yield
                proj(wch("mv", h), 8, h_rhs, hTb, NT, cons_v)
                yield
                for (src, srcb) in ((Q, Qb), (K, Kb), (Vv, Vb)):
                    for s in range(nseq):
                        ss = slice(s * Tq, (s + 1) * Tq)
                        CP("pool", src[1][:, ss], src[0][:, ss][:, ::-1], [srcb[0]], [srcb[1]])
                for d in range(2):
                    rr = slice(32 * d, 32 * d + 4)
                    bc3 = v3(BBC[d], Tq + 1)
                    for s in range(nseq):
                        for c0 in range(0, Tq + 1, 512):
                            w = min(512, Tq + 1 - c0)
                            ps, pb = PS()
                            MM(ps[:, 0:w], v3(selh, 128)[rr, h, :], BB3[rr, s, c0:c0 + w], [cb, bbb], [pb])
                            CP("act", bc3[:, s, c0:c0 + w], ps[:, 0:w], [pb], [bbcb[d]])
                    NBC4 = NBC.rearrange("p (s d c) -> p s d c", s=nseq, d=2)
                    for s in range(nseq):
                        TSC("pool", NBC4[:, s, d, :], bc3[:, s, 0:Tq + 1:128], -1.0, ALU.mult, [bbcb[d]], [nbcb])
                def mstream(h, d, B_):
                    DT_, DTb = B_["DT"]; PTm, PTmb = B_["PT"]; Et, Etb = B_["E"]; qe, qeb = B_["qe"]; KW, KWb = B_["KW"]; VTm, VTmb = B_["VT"]
                    om, omb = B_["om"]; dec, decb = B_["dec"]; CT, CTb = B_["CT"]; NM, NMb = B_["NM"]; dmx, dmxb = B_["dmx"]
                    emt, emtb = B_["emt"]; cout, coutb = B_["cout"]
                    Hd, Hdb = Hs[d]
                    bc3 = v3(BBC[d], Tq + 1)
                    NBC4 = NBC.rearrange("p (s d c) -> p s d c", s=nseq, d=2)
                    for s in range(nseq):
                        if isS:
                            TSC("dve", CT, smC4[:, d, h, :], em0[:, d * 4 + h:d * 4 + h + 1], ALU.mult, [smb, cb], [CTb])
                            TSC("dve", NM, onesf, smn_sb[:, d * 4 + h:d * 4 + h + 1], ALU.mult, [smb, cb], [NMb], s2=em0[:, d * 4 + h:d * 4 + h + 1], op1=ALU.mult)
                        else:
                            MSET("pool", CT, 0.0, [CTb])
                            MSET("pool", NM, 0.0, [NMb])
                        yield
                        for c in range(NC):
                            t0 = s * Tq + c * 128
                            ts_ = slice(t0, t0 + 128)
                            psic = PSIT5[:, s, d, c, h:h + 1]
                            ACT(DT_, bc3[:, s, 1 + c * 128:1 + (c + 1) * 128], AF.Exp, [bbcb[d], psitb], [DTb], bias=psic)
                            TT("pool", DT_, DT_, triu, ALU.mult, [DTb, cb], [DTb])
                            ps, pb = PS()
                            MM(ps[:, 0:128], K[d][:, ts_], Q[d][:, ts_], [Kb[d], Qb[d]], [pb])
                            TT("dve", PTm, ps[:, 0:128], DT_, ALU.mult, [pb, DTb], [PTmb])
                            yield
                            ACT(Et, bc3[:, s, 1 + c * 128:1 + (c + 1) * 128], AF.Exp, [bbcb[d], nbcb], [Etb], bias=NBC4[:, s, d, c:c + 1])
                            TT("pool", qe, Q[d][:, ts_], Et, ALU.mult, [Qb[d], Etb], [qeb])
                            ACT(om[:, 0:1], psic, AF.Exp, [psitb, bbcb[d]], [omb], bias=bc3[:, s, (c + 1) * 128:(c + 1) * 128 + 1])
                            ACT(dec[:, 0:1], bc3[:, s, (c + 1) * 128:(c + 1) * 128 + 1], AF.Exp, [bbcb[d], nbcb], [decb], bias=NBC4[:, s, d, c:c + 1])
                            yield
                            ps, pb = PS()
                            MM(ps[:, 0:128], K[d][:, ts_], ident, [Kb[d], cb], [pb])
                            TSC("dve", KW, ps[:, 0:128], om[:, 0:1], ALU.mult, [pb, omb], [KWb])
                            ps, pb = PS()
                            MM(ps[:, 0:128], Vv[d][:, ts_], ident, [Vb[d], cb], [pb])
                            CP("act", VTm, ps[:, 0:128], [pb], [VTmb])
                            yield
                            psn, pbn = PS()
                            MM(psn[:, 0:128], VTm, PTm, [VTmb, PTmb], [pbn], start=True, stop=False)
                            MM(psn[:, 0:128], CT, qe, [CTb, qeb], [pbn], start=False, stop=True)
                            psd, pbd = PS()
                            MM(psd[:, 0:128], onesf, PTm, [cb, PTmb], [pbd], start=True, stop=False)
                            MM(psd[:, 0:128], NM, qe, [NMb, qeb], [pbd], start=False, stop=True)
                            psc, pbc = PS()
                            MM(psc[:, 0:128], KW, VTm, [KWb, VTmb], [pbc])
                            psn2, pbn2 = PS()
                            MM(psn2[:, 0:128], KW, onesf, [KWb, cb], [pbn2])
                            STT("dve", CT, CT, dec[:, 0:1], psc[:, 0:128], ALU.mult, ALU.add, [CTb, decb, pbc], [CTb])
                            STT("dve", NM, NM, dec[:, 0:1], psn2[:, 0:128], ALU.mult, ALU.add, [NMb, decb, pbn2], [NMb])
                            ACT(dmx, psd[:, 0:128], AF.Abs, [pbd], [dmxb])
                            TSC("dve", dmx, dmx, 1.0, ALU.max, [dmxb], [dmxb])
                            RECIP(dmx, dmxb)
                            hdst = Hd[:, ts_] if d == 0 else Hd[:, s * Tq:(s + 1) * Tq][:, ::-1][:, c * 128:(c + 1) * 128]
                            TT("dve", hdst, psn[:, 0:128], dmx, ALU.mult, [pbn, dmxb], [Hdb])
                            yield
                        if not isS:
                            rr = slice(32 * d, 32 * d + 4)
                            ps, pb = PS()
                            MM(ps[:, 0:1], v3(selh, 128)[rr, h, :], Mrow[rr, s:s + 1], [cb, mrowb], [pb])
                            ACT(emt[:, 0:1], ps[:, 0:1], AF.Exp, [pb], [emtb], scale=-1.0)
                            TSC("dve", cout, CT, emt[:, 0:1], ALU.mult, [CTb, emtb], [coutb])
                            ni = s * 8 + d * 4 + h
                            TSC("dve", nout[:, ni:ni + 1], NM[:, 0:1], emt[:, 0:1], ALU.mult, [NMb, emtb], [noutb])
                            P.dma("sp", osmC5[:, s, d, h, :], cout, [coutb], [])
                            yield

                yield from run_rr_g([mstream(h, 0, MB[0]), mstream(h, 1, MB[1])])
                for n in range(NT):
                    sl = slice(n * 512, (n + 1) * 512)
                    TT("dve", Hh[:, sl], Hh[:, sl], Hs[1][0][:, sl], ALU.add, [Hb, Hs[1][1]], [Hb])
                wt, wb = load_w(wch("mo", h))
                for n in range(NT):
                    sl = slice(n * 512, (n + 1) * 512)
                    ps, pb = PS()
                    for kc in range(8):
                        MM(ps[:, :], wt[:, kc * 128:(kc + 1) * 128], h_rhs(kc, n), hTb + [wb], [pb], start=(kc == 0), stop=(kc == 7))
                    ACT(so, ps[:, :], AF.Sigmoid, [pb], [sob])
                    ps1, pb1 = PS()
                    MM(ps1[:, :], onesf, Hh[:, sl], [cb, Hb], [pb1])
                    TT("pool", gt1, Hh[:, sl], Hh[:, sl], ALU.mult, [Hb], [gt1b])
                    ps2, pb2 = PS()
                    MM(ps2[:, :], onesf, gt1, [cb, gt1b], [pb2])
                    TSC("dve", gt2, ps1[:, :], 1.0 / 128, ALU.mult, [pb1], [gt2b])
                    STT("dve", gt3, gt2, -1.0, gt2, ALU.mult, ALU.mult, [gt2b], [gt3b])
                    STT("dve", gt3, ps2[:, :], 1.0 / 128, gt3, ALU.mult, ALU.add, [pb2, gt3b], [gt3b])
                    TSC("dve", gt3, gt3, 1e-5, ALU.add, [gt3b], [gt3b])
                    ACT(gt3, gt3, AF.Sqrt, [gt3b], [gt3b])
                    RECIP(gt3, gt3b)
                    TT("dve", gt2, Hh[:, sl], gt2, ALU.subtract, [Hb, gt2b], [gt2b])
                    TT("dve", gt2, gt2, gt3, ALU.mult, [gt2b, gt3b], [gt2b])
                    STT("dve", v3(ym, T)[:, h, sl], gt2, prmc("gng", h), so, ALU.mult, ALU.mult, [gt2b, sob, cb], [ymb])
                    yield
            if not isS:
                P.dma("sp", do_smn, nout, [noutb], [])

        gM = mlstm_body()
        if isS:
            for _ in gM:
                pass
        else:
            for _ in run_rr_g([gR, gM]):
                pass
        P.barrier()
        A.release(mk_m if isS else mk_r)
        wpool[0] = None
        if isS:
            ypad = [A.alloc(2560, BF16), A.alloc(2560, BF16)]
            ypb = [Buf(), Buf()]
            for k_, (src_, srcb_) in enumerate(((yr, yrb), (ym, ymb))):
                MSET("dve", ypad[k_], 0.0, [ypb[k_]])
                CP("dve", ypad[k_][:, 256:256 + TS], src_[:, 0:TS], [srcb_], [ypb[k_]])
            dlt_sb = A.alloc(4)
            dltb = Buf()
            P.dma("sp", dlt_sb, ddlt, (), [dltb])
            msk = [(A.alloc(2560, BF16), Buf()) for _ in range(2)]
            rs5 = rs_in.ap().rearrange("(q r k p) t -> p q r k t", q=4, r=4, k=2, p=128)
            rsibs = []
            for r_ in range(4):
                for k_ in range(2):
                    m_, mb_ = msk[(r_ * 2 + k_) % 2]
                    TSC("dve", m_, ypad[k_], dlt_sb[:, r_:r_ + 1], ALU.mult, [ypb[k_], dltb], [mb_])
                    bb_ = Buf()
                    src_ = bass.AP(m_.tensor, m_.offset, [list(m_.ap[0]), [512, 4], [1, 1024]])
                    P.dma("sp", rs5[:, :, r_, k_, :], src_, [mb_], [bb_])
                    rsibs.append(bb_)
            rsob = Buf("rs_out")
            P.coll("pool", lambda g: g.collective_compute("ReduceScatter", ALU.add, replica_groups=[[0, 2, 4, 6], [1, 3, 5, 7]],
                                                          ins=[rs_in.ap().opt()], outs=[rs_out.ap().opt()]), rsibs, [rsob])
            shared["rsob"] = rsob
        if isS:
            P.barrier()
            yield
            A.release(base_mark)
            T = 1024
            NT = 2
            xd = dxw
            rowm_sb = A.alloc(16)
            P.dma("sp", rowm_sb, drowm, (), [cb])
            hT = A.alloc(8 * T, BF16)
            hTb = [Buf(f"hq{n}") for n in range(NT)]
            h3 = v3(hT, T)
            hmark = A.mark()
            yr, yrb, ym, ymb, xb_pref, xbb_pref = shared["pref"]
            mkq = A.mark()
            nb_ = [(A.alloc(4096), Buf(), A.alloc(4096, BF16), Buf(), A.alloc(512), Buf()) for _ in range(2)]
            xbig = [xb_pref]
            xbigb = [xbb_pref]
            xviews = [(v3(xbig[0], 1024)[:, :, n * 512:n * 512 + 512], xbigb[0]) for n in range(NT)]
            norm_tiles_v(NT, lambda n: None, mA1, mB1, hTb, xviews, nb_)
            P.barrier()
            A.release(mkq)
        if dbg:
            if isS:
                P.dma("sp", dbg_out["h_" + gn], hT, hTb, [])
            P.dma("sp", dbg_out["yr_" + gn], yr, [yrb], [])
            P.dma("sp", dbg_out["ym_" + gn], ym, [ymb], [])
        Ti = 512 if isS else T
        inner = [(0, 256)] if isS else [(n, n * 512) for n in range(NT)]

        mk_g = A.mark()
        mg = A.alloc(8 * T, BF16); mgb = Buf()
        mg3 = v3(mg, T)
        yr3 = v3(yr, T); ym3 = v3(ym, T)
        wo = A.alloc(8 * 1024, BF16)
        wobs = [Buf() for _ in range(8)]
        for i in range(8):
            P.dma("pool", wo[:, i * 1024:(i + 1) * 1024], dwout[i], (), [wobs[i]])
        mk_g2 = A.mark()
        gtmp = [[(A.alloc(512), Buf()) for _ in range(3)] for _ in range(2)]
        for i in range(8):
            if i == 0:
                mwb = [[(A.alloc(1024, BF16), Buf()), (A.alloc(1024, BF16), Buf()), (A.alloc(512, BF16), Buf()), (A.alloc(512, BF16), Buf())] for _ in range(2)]
            (wgr, wgrb), (wgm, wgmb), (wt_r, wrb), (wt_m, wmb) = mwb[i % 2]
            P.dma("pool", wgr, dwin[31 + i], (), [wgrb])
            P.dma("pool", wgm, dwin[39 + i], (), [wgmb])
            P.dma("pool", wt_r, dwbr[i], (), [wrb])
            P.dma("pool", wt_m, dwbm[i], (), [wmb])
            for n in range(NT):
                sl = slice(n * 512, (n + 1) * 512)
                (gr, grb), (gm, gmb), (mt, mtb2) = gtmp[(i * NT + n) % 2]
                ps, pb = PS()
                for kc in range(8):
                    MM(ps[:, :], wgr[:, kc * 128:(kc + 1) * 128], h_rhs(kc, n), hTb + [wgrb], [pb], start=(kc == 0), stop=(kc == 7))
                ACT(gr, ps[:, :], AF.Sigmoid, [pb], [grb])
                ps, pb = PS()
                for kc in range(8):
                    MM(ps[:, :], wgm[:, kc * 128:(kc + 1) * 128], h_rhs(kc, n), hTb + [wgmb], [pb], start=(kc == 0), stop=(kc == 7))
                ACT(gm, ps[:, :], AF.Sigmoid, [pb], [gmb])
                ps, pb = PS()
                for kc in range(4):
                    MM(ps[:, :], wt_r[:, kc * 128:(kc + 1) * 128], yr3[:, kc, sl], [wrb, yrb], [pb], start=(kc == 0), stop=(kc == 3))
                TT("dve", mt, ps[:, :], gr, ALU.mult, [pb, grb], [mtb2])
                ps, pb = PS()
                for kc in range(4):
                    MM(ps[:, :], wt_m[:, kc * 128:(kc + 1) * 128], ym3[:, kc, sl], [wmb, ymb], [pb], start=(kc == 0), stop=(kc == 3))
                TT("dve", gm, ps[:, :], gm, ALU.mult, [pb, gmb], [gmb])
                TT("pool", mg3[:, i, sl], mt, gm, ALU.add, [mtb2, gmb], [mgb])
        if dbg:
            P.dma("sp", dbg_out["mg_" + gn], mg, [mgb], [])
        P.barrier()
        A.release(mk_g2)
        if isS:
            A.guard = None
        h2, h2b = hT, hTb
        ot, otb = A.alloc(4096), Buf()
        if dbg:
            dwo = nc.dram_tensor("dbg_wo_" + gn, [128, 8192], BF16, kind="ExternalOutput").ap()
            P.dma("sp", dwo, wo, wobs, [])
            dot = dout("dbg_ot_" + gn, [128, 4096])
        xt2, xt2b = A.alloc(4096), Buf()
        tmp = A.alloc(4096); tmpb = Buf()
        sq = A.alloc(4096, BF16); sqb = Buf()
        rstd = A.alloc(512); rstdb = Buf()
        for n in range(NT):
            sl = slice(n * 512, (n + 1) * 512)
            for i in range(8):
                ps, pb = PS()
                for kc in range(8):
                    MM(ps[:, :], wo[:, i * 1024 + kc * 128:i * 1024 + (kc + 1) * 128], mg3[:, kc, sl], [wobs[i], mgb], [pb], start=(kc == 0), stop=(kc == 7))
                CP("act", ot[:, i * 512:(i + 1) * 512], ps[:, :], [pb], [otb])
            P.dma("sp", v3(xt2, 512), x_view(xd, n), (), [xt2b])
            if dbg and n == 0:
                P.dma("sp", dot, ot, [otb], [])
            ACT(sq, ot, AF.Square, [otb], [sqb])
            ps, pb = PS()
            for kc in range(8):
                MM(ps[:, :], onesb, sq[:, kc * 512:(kc + 1) * 512], [cb, sqb], [pb], start=(kc == 0), stop=(kc == 7))
            ACT(rstd, ps[:, :], AF.Sqrt, [pb], [rstdb], bias=epsc[:, 0:1], scale=1.0 / D)
            RECIP(rstd, rstdb)
            for kc in range(8):
                ks_ = slice(kc * 512, (kc + 1) * 512)
                STT("dve" if kc % 2 == 0 else "pool", ot[:, ks_], ot[:, ks_], mcol(mG1, kc, ci_), rstd, ALU.mult, ALU.mult, [otb, rstdb, modb], [otb])
            TT("dve", xt2, xt2, ot, ALU.add, [xt2b, otb], [xt2b])
            P.dma("sp", x_view(dxm[gn], n), v3(xt2, 512), [xt2b], [dxm_buf[gn]])
            norm_mod(xt2, xt2b, mA2, mB2, lambda kc, n=n: h3[:, kc, n * 512:(n + 1) * 512], h2b[n], tmp, tmpb, sq, sqb, rstd, rstdb)
        P.barrier()
        A.release(mk_g)

        A.release(hmark)
        mk_f = A.mark()
        if (not isS) and ("rsob" in shared) and ("pref" not in shared):
            yr_t = arena_t[:, TOPB:TOPB + 2048].bitcast(BF16)
            ym_t = arena_t[:, TOPB + 2048:TOPB + 4096].bitcast(BF16)
            xb_t = arena_t[:, TOPB + 4096:TOPB + 4096 + 8192]
            yrb_t, ymb_t, xbb_t = Buf("yr_t"), Buf("ym_t"), Buf("xb_t")
            rso4_ = rs_out.ap().rearrange("(r k p) t -> p r k t", r=4, k=2, p=128)
            P.dma("sp", v3(yr_t, 1024), rso4_[:, :, 0, :], [shared["rsob"]], [yrb_t])
            P.dma("sp", v3(ym_t, 1024), rso4_[:, :, 1, :], [shared["rsob"]], [ymb_t])
            P.dma("sp", v3(xb_t, 1024), dxw.rearrange("(kc p) t -> p kc t", p=128), (), [xbb_t])
            shared["pref"] = (yr_t, yrb_t, ym_t, ymb_t, xb_t, xbb_t)
            A.guard = TOPB
        facc = A.alloc(8 * Ti); faccb = Buf()
        f3 = v3(facc, Ti)
        mk_f2 = A.mark()
        if isS:
            cR, cW = 18, 66
        else:
            cR, cW = 2, Tq + 2
        cpads = [A.alloc(cR * cW, BF16) for _ in range(2)]
        cpbs = [Buf(), Buf()]
        for c_ in range(2):
            MSET("pool", cpads[c_], 0.0, [cpbs[c_]])
        dgs = [(A.alloc(9 * 128, BF16), Buf()) for _ in range(2)]
        uas = [(A.alloc(Ti), Buf()) for _ in range(2)]
        uvs = [(A.alloc(Ti), Buf()) for _ in range(2)]
        pch = [A.alloc(Ti, BF16) for _ in range(4)]
        pchb = [Buf() for _ in range(4)]
        GS = 2
        fdw = [A.alloc(GS * 1024, BF16) for _ in range(2)]
        fdwb = [Buf(), Buf()]
        ftm = [(A.alloc(512), Buf()) for _ in range(2)]
        ftm_rr = [0]
        faccbs = [[Buf() for _ in range(len(inner))] for _ in range(8)]
        faccb = [bb for row in faccbs for bb in row]
        ofc, _ = PRM["fconv"]
        ngroups = (NFF + GS - 1) // GS
        if A.top + 6 * 512 + 2048 < A.ncols:
            wpool[0] = ([(A.alloc(1024, BF16), Buf()) for _ in range(6)], [0])

        def down(g_):
            j0 = g_ * GS
            nj = min(GS, NFF - j0)
            fi = g_ % 2
            for i in range(8):
                for n in range(len(inner)):
                    sl = slice(n * 512, (n + 1) * 512)
                    ps, pb = PS()
                    for jj in range(nj):
                        pi = (j0 + jj) % 4
                        MM(ps[:, :], fdw[fi][:, jj * 1024 + i * 128:jj * 1024 + (i + 1) * 128], pch[pi][:, sl], [fdwb[fi], pchb[pi]], [pb], start=(jj == 0), stop=(jj == nj - 1))
                    if g_ == 0:
                        CP("act", f3[:, i, sl], ps[:, :], [pb], [faccbs[i][n]])
                    else:
                        TT("dve", f3[:, i, sl], f3[:, i, sl], ps[:, :], ALU.add, [faccbs[i][n], pb], [faccbs[i][n]])

        taps = [(dr, dc) for dr in (-1, 0, 1) for dc in (-1, 0, 1)] if isS else [(0, dc) for dc in (-1, 0, 1)]

        def front(j):
            g_ = j // GS
            if j % GS == 0:
                nj = min(GS, NFF - j)
                fi = g_ % 2
                P.dma("pool", v3(fdw[fi], 1024)[:, 0:nj, :], dfdn[j:j + nj].rearrange("k p n -> p k n"), (), [fdwb[fi]])
            cpad, cpb = cpads[j % 2], cpbs[j % 2]
            cp3 = v3(cpad, cW)
            uv, uvb = uvs[j % 2]
            dg, dgb = dgs[j % 2]
            dg3 = v3(dg, 128)
            for i, (dr, dc) in enumerate(taps):
                tap = (dr + 1) * 3 + (dc + 1)
                wcol = prm[:, ofc + tap * 22 + j:ofc + tap * 22 + j + 1]
                TSC("dve", dg3[:, i, :], ident, wcol, ALU.mult, [cb], [dgb])

            def cons_a(n, ps, pb, cp3=cp3, cpb=cpb):
                if isS:
                    CP("act", cp3[:, 1 + 8 * n:9 + 8 * n, 1:65], v3(ps[:, :], 64), [pb], [cpb])
                else:
                    CP("act", cp3[:, :, 1:1 + Tq], v3(ps[:, :], Tq), [pb], [cpb])
            proj(dfup[j], 8, h_rhs, h2b, NT, cons_a)
            if isS:
                ci3 = cp3[:, 1:17, 1:65]
                TT("dve", ci3, ci3, rowm_sb.unsqueeze(2).to_broadcast([128, 16, 64]), ALU.mult, [cpb, cb], [cpb])

            def cons_vv(n, ps, pb, uv=uv, uvb=uvb):
                CP("act", uv[:, n * 512:(n + 1) * 512], ps[:, :], [pb], [uvb])
            proj(dfup[22 + j], 8, lambda kc, k_: h3[:, kc, inner[k_][1]:inner[k_][1] + 512], h2b, len(inner), cons_vv)

        def back(j):
            cpad, cpb = cpads[j % 2], cpbs[j % 2]
            cp3 = v3(cpad, cW)
            ua, uab = uas[j % 2]
            uv, uvb = uvs[j % 2]
            dg, dgb = dgs[j % 2]
            dg3 = v3(dg, 128)
            fcbcol = prm[:, PRM["fcb"][0] + j:PRM["fcb"][0] + j + 1]
            for n in range(len(inner)):
                ps, pb = PS()
                for i, (dr, dc) in enumerate(taps):
                    if isS:
                        r0_ = inner[n][1] // 64
                        view = cp3[:, 1 + dr + r0_:9 + dr + r0_, 1 + dc:65 + dc]
                        po = v3(ps[:, :], 64)
                    else:
                        view = cp3[:, :, 1 + dc:1 + dc + Tq]
                        po = v3(ps[:, :], Tq)
                    MM(po, dg3[:, i, :], view, [dgb, cpb], [pb], start=(i == 0), stop=(i == len(taps) - 1))
                ACT(ua[:, n * 512:(n + 1) * 512], ps[:, :], AF.Silu, [pb, cb], [uab], bias=fcbcol)
            TT("pool", pch[j % 4], ua, uv, ALU.mult, [uab, uvb], [pchb[j % 4]])

        front(0)
        for j in range(NFF):
            if j + 1 < NFF:
                front(j + 1)
            back(j)
            if j % GS == 0 and j >= GS:
                down(j // GS - 1)
        down(ngroups - 1)
        wpool[0] = None
        if dbg:
            P.dma("sp", dbg_out["f_" + gn], facc, faccb, [])
        P.barrier()
        A.release(mk_f2)
        xt2, xt2b = A.alloc(4096), Buf()
        ft, ftb = A.alloc(4096), Buf()
        sq = A.alloc(4096, BF16); sqb = Buf()
        rstd = A.alloc(512); rstdb = Buf()
        for n in range(len(inner)):
            sl = slice(n * 512, (n + 1) * 512)
            P.dma("sp", v3(xt2, 512), dxm[gn].rearrange("(kc p) t -> p kc t", p=128)[:, :, inner[n][1]:inner[n][1] + 512], [dxm_buf[gn]], [xt2b])
            CP("pool", v3(ft, 512), f3[:, :, sl], [faccbs[i_][n] for i_ in range(8)], [ftb])
            ACT(sq, ft, AF.Square, [ftb], [sqb])
            ps, pb = PS()
            for kc in range(8):
                MM(ps[:, :], onesb, sq[:, kc * 512:(kc + 1) * 512], [cb, sqb], [pb], start=(kc == 0), stop=(kc == 7))
            ACT(rstd, ps[:, :], AF.Sqrt, [pb], [rstdb], bias=epsc[:, 0:1], scale=1.0 / D)
            RECIP(rstd, rstdb)
            for kc in range(8):
                ks_ = slice(kc * 512, (kc + 1) * 512)
                STT("dve" if kc % 2 == 0 else "pool", ft[:, ks_], ft[:, ks_], mcol(mG2, kc, ci_), rstd, ALU.mult, ALU.mult, [ftb, rstdb, modb], [ftb])
            TT("dve", xt2, xt2, ft, ALU.add, [xt2b, ftb], [xt2b])
            P.dma("sp", x_view(dy[gn], n), v3(xt2, 512), [xt2b], [])
        P.barrier()
        A.release(mk_f)
        yield

    gS = run_group("s")
    next(gS)
    for _ in run_group("p"):
        pass
    for _ in gS:
        pass
    P.barrier(final=True)
    P.emit()
    es.close()
    return P, A


_CACHE = {}


def _prep_shared(inp):
    f = lambda k: np.asarray(inp[k], np.float32)
    prm = np.zeros((128, NPRM), np.float32)

    def put(name, arr):
        o, c = PRM[name]
        assert arr.shape == (128, c), (name, arr.shape)
        prm[:, o:o + c] = arr
    put("ada_b", _fm(f("ada_b")[0]))
    for i in range(4):
        put(f"ng{i}", _fm(f("norm_g")[0, i]))
    put("mu", _fm(f("rwkv_mu")[0]))
    put("w0", _fm(f("rwkv_w0")[0].reshape(-1)))
    put("a0", _fm(f("rwkv_a0")[0].reshape(-1)))
    put("kks", _fm(f("rwkv_kk_scale")[0]))
    put("ka", _fm(f("rwkv_k_a")[0]))
    put("rk", _fm(f("rwkv_r_k")[0].reshape(-1)))
    put("lng", _fm(f("rwkv_lnx_g")[0]))
    put("lnb", _fm(f("rwkv_lnx_b")[0]))
    put("mconv", np.ascontiguousarray(f("mlstm_conv")[0].reshape(9, 8, 128).transpose(2, 0, 1).reshape(128, 72)))
    gb = f("mlstm_gate_b")[0]
    gbi = np.zeros((128, 1), np.float32)
    gbf = np.zeros((128, 1), np.float32)
    for d in range(2):
        gbi[32 * d:32 * d + 4, 0] = gb[0, d]
        gbf[32 * d:32 * d + 4, 0] = gb[1, d]
    put("gbi", gbi)
    put("gbf", gbf)
    put("gng", _fm(f("mlstm_gn_g")[0]))
    put("fconv", np.ascontiguousarray(f("ffn_conv")[0].reshape(9, 22, 128).transpose(2, 0, 1).reshape(128, 198)))
    put("fcb", _fm(f("ffn_conv_b")[0]))
    w_in = f("w_in")[0]
    gi = np.zeros((1024, 128), np.float32)
    gf = np.zeros((1024, 128), np.float32)
    for d in range(2):
        gi[:, 32 * d:32 * d + 4] = w_in[:, 3968 + d * 4:3968 + d * 4 + 4]
        gf[:, 32 * d:32 * d + 4] = w_in[:, 3968 + 8 + d * 4:3968 + 8 + d * 4 + 4]
    w_in2 = np.concatenate([w_in[:, 0:3968], w_in[:, 3984:6032], gi, gf], axis=1)
    lora = np.concatenate([f("rwkv_w_up")[0].reshape(128, 512), f("rwkv_a_up")[0].reshape(128, 512), f("rwkv_g_up")[0]], axis=1)
    sh = {
        "prm": prm,
        "lora": np.ascontiguousarray(lora),
        "adaw_full": _tile_w(f("ada_w")[0]),
        "win": _tile_w(w_in2),
        "wbr": _tile_w(f("w_branch_rwkv")[0]),
        "wbm": _tile_w(f("w_branch_mlstm")[0]),
        "wout": _tile_w(f("w_out")[0]),
        "fup": _tile_w(f("ffn_up")[0]),
        "fdn": np.ascontiguousarray(f("ffn_down")[0].reshape(22, 128, 1024)),
    }
    aux = {"win_t": sh["win"], "prm": prm, "lora": sh["lora"], "w_in": w_in}
    return sh, aux


def _prep_core(inp, i, sh, aux):
    f = lambda k: np.asarray(inp[k], np.float32)
    b = i % 2
    r = i // 2
    m = dict(sh)
    adaw_full = m.pop("adaw_full")
    m["adaw"] = np.ascontiguousarray(adaw_full[12 * r:12 * r + 12])
    o_ab, _ = PRM["ada_b"]
    m["adab"] = np.ascontiguousarray(aux["prm"][:, o_ab + 12 * r:o_ab + 12 * r + 12])
    wt = aux["win_t"]
    w_in = aux["w_in"]
    gi = np.zeros((1024, 128), np.float32)
    gf = np.zeros((1024, 128), np.float32)
    for d in range(2):
        gi[:, 32 * d] = w_in[:, 3968 + d * 4 + r]
        gf[:, 32 * d] = w_in[:, 3968 + 8 + d * 4 + r]
    m["win_s"] = np.ascontiguousarray(np.concatenate([wt[[r, 4 + r, 8 + r, 12, 13, 14, 15 + r, 19 + r, 23 + r, 27 + r]], _tile_w(np.concatenate([gi, gf], axis=1))], axis=0))
    ps = aux["prm"].copy()

    def mv(name, dst, src):
        o, c = PRM[name]
        ps[:, o + dst] = aux["prm"][:, o + src]
    for base in (0, 4, 8):
        mv("mu", base, base + r)
    for d in range(2):
        mv("w0", d * 4, d * 4 + r)
        mv("a0", d * 4, d * 4 + r)
    for nm in ("kks", "ka", "rk", "lng", "lnb", "gng"):
        mv(nm, 0, r)
    for tap in range(9):
        mv("mconv", tap * 8, tap * 8 + r)
        mv("mconv", tap * 8 + 4, tap * 8 + 4 + r)
    gb = f("mlstm_gate_b")[0]
    og, _ = PRM["gbi"]
    ogf, _ = PRM["gbf"]
    ps[:, og] = 0.0
    ps[:, ogf] = 0.0
    for d in range(2):
        ps[32 * d, og] = gb[0, d, r]
        ps[32 * d, ogf] = gb[1, d, r]
    m["prm_s"] = ps
    lo = aux["lora"].copy()
    for k in range(3):
        lo[:, k * 512:k * 512 + 128] = aux["lora"][:, k * 512 + r * 128:k * 512 + (r + 1) * 128]
    m["lora_s"] = lo
    m["xs"] = np.ascontiguousarray(f("x_sample")[b].T)
    xpad = np.zeros((1024, 256 + 2048 + 256), np.float32)
    xpad[:, 256:256 + 2048] = m["xs"]
    m["xw"] = np.ascontiguousarray(xpad[:, 512 * r:512 * r + 1024])
    dl = np.zeros((128, 4), np.float32)
    dl[:, r] = 1.0
    m["dlt"] = dl
    rows = 8 * r - 4 + np.arange(16)
    m["rowm"] = np.ascontiguousarray(np.broadcast_to(((rows >= 0) & (rows < 32)).astype(np.float32)[None, :], (128, 16)))
    m["xp"] = np.ascontiguousarray(np.concatenate([f("x_prompt")[2 * i].T, f("x_prompt")[2 * i + 1].T], axis=1))
    vecs = np.stack([f("c_ctx"), f("c")[b]])
    m["cond"] = np.ascontiguousarray(vecs.reshape(2, 8, 128).transpose(2, 1, 0).reshape(128, 16))
    sr = f("state_rwkv")[b, 0]
    sr5 = sr.reshape(2, 4, 2, 64, 64).transpose(2, 4, 0, 1, 3).copy()
    sr5[:, :, :, 0] = sr5[:, :, :, r]
    m["srw"] = np.ascontiguousarray(sr5.reshape(128, 512))
    sc = f("state_mlstm_C")[b, 0].transpose(3, 0, 1, 2).copy()
    sc[:, :, 0] = sc[:, :, r]
    m["smC"] = np.ascontiguousarray(sc.reshape(128, 1024))
    sn = f("state_mlstm_n")[b, 0].transpose(2, 0, 1).copy()
    sn[:, :, 0] = sn[:, :, r]
    m["smn"] = np.ascontiguousarray(sn.reshape(128, 8))
    sm = f("state_mlstm_m")[b, 0].copy()
    sm[:, 0] = sm[:, r]
    m["smm"] = np.ascontiguousarray(np.broadcast_to(sm.reshape(1, 8), (128, 8)))
    return m


def _get_nc(dbg=False):
    key = ("nc", dbg)
    if key not in _CACHE:
        nc = bass.Bass("TRN2", target_bir_lowering=False)
        build(nc, dbg=dbg)
        _CACHE[key] = nc
    return _CACHE[key]


def kernel(**inp):
    nc = _get_nc(False)
    sh, aux = _prep_shared(inp)
    in_maps = [_prep_core(inp, i, sh, aux) for i in range(8)]
    res = run_bass_kernel_spmd(nc, in_maps, core_ids=list(range(8)))
    R = res.results
    y_prompt = np.zeros((16, 256, 1024), np.float32)
    y_sample = np.zeros((2, 2048, 1024), np.float32)
    nS = np.zeros((16, 1, 2, 8, 64, 64), np.float32)
    nC = np.zeros((16, 1, 2, 4, 128, 128), np.float32)
    nn = np.zeros((16, 1, 2, 4, 128), np.float32)
    nm = np.zeros((16, 1, 2, 4), np.float32)
    for i in range(8):
        r = R[i]
        yp = np.asarray(r["yp"])
        for s in range(2):
            y_prompt[2 * i + s] = yp[:, s * 256:(s + 1) * 256].T
        q = i // 2
        y_sample[i % 2, 512 * q:512 * q + 512] = np.asarray(r["ys"]).T
        osr = np.asarray(r["o_srw"]).reshape(2, 64, 2, 2, 4, 64)
        nS[2 * i:2 * i + 2, 0] = osr.transpose(2, 3, 4, 0, 5, 1).reshape(2, 2, 8, 64, 64)
        oc = np.asarray(r["o_smC"]).reshape(128, 2, 2, 4, 128)
        nC[2 * i:2 * i + 2, 0] = oc.transpose(1, 2, 3, 4, 0)
        on = np.asarray(r["o_smn"]).reshape(128, 2, 2, 4)
        nn[2 * i:2 * i + 2, 0] = on.transpose(1, 2, 3, 0)
        om = np.asarray(r["o_smm"])
        for d in range(2):
            nm[2 * i:2 * i + 2, 0, d, :] = om[32 * d:32 * d + 4, :].T
    return (y_prompt, y_sample, nS, nC, nn, nm)
```

```python
import contextlib
import numpy as np
import concourse.bass as bass
import concourse.mybir as mybir
from concourse.bass_utils import run_bass_kernel_spmd

F32 = mybir.dt.float32
BF16 = mybir.dt.bfloat16
ALU = mybir.AluOpType
AF = mybir.ActivationFunctionType
AX = mybir.AxisListType
ENGS = ["pe", "act", "dve", "pool", "sp"]

D = 1024
TS = 2048
TPS = 256
DFF = 2816
NFF = 22
DSC = 0.606531
SD = F32


class Buf:
    __slots__ = ("name", "w", "r")

    def __init__(self, name=""):
        self.name = name
        self.w = {}
        self.r = {}


class Prog:
    def __init__(self, nc, n_dma_sems=48):
        self.nc = nc
        self.ops = {e: [] for e in ENGS}
        self.sem = {e: nc.alloc_semaphore(name=f"sem_{e}") for e in ENGS}
        self.cnt = {e: 0 for e in ENGS}
        self.known = {e: {} for e in ENGS}
        self.dma_sems = [nc.alloc_semaphore(name=f"dsem{i}") for i in range(n_dma_sems)]
        self.dma_tot = [0] * n_dma_sems
        self.dma_rr = 0
        self.extra_evs = []

    def _collect(self, e, reads, writes, extra=()):
        waits = {}

        def need(ev):
            sem, val = ev
            if waits.get(sem.num, (None, 0))[1] < val:
                waits[sem.num] = (sem, val)

        for b in reads:
            for ev in b.w.values():
                need(ev)
        for b in writes:
            for ev in b.w.values():
                need(ev)
            for ev in b.r.values():
                need(ev)
        for ev in extra:
            need(ev)
        wl = []
        own = self.sem[e].num
        for num, (sem, val) in waits.items():
            if self.known[e].get(num, 0) >= val:
                continue
            if num == own and e == "pe":
                continue
            self.known[e][num] = val
            wl.append((sem, val))
        return wl

    def _update(self, ev, reads, writes):
        num = ev[0].num
        ws = set(id(b) for b in writes)
        for b in writes:
            b.w = {num: ev}
            b.r = {}
        for b in reads:
            if id(b) not in ws:
                b.r[num] = ev

    def op(self, e, fn, reads=(), writes=()):
        wl = self._collect(e, reads, writes)
        self.cnt[e] += 1
        ev = (self.sem[e], self.cnt[e])
        self.ops[e].append((wl, fn, (self.sem[e], 1)))
        self._update(ev, reads, writes)
        return ev

    def dma(self, e, out, in_, reads=(), writes=()):
        s = self.dma_rr
        self.dma_rr = (self.dma_rr + 1) % len(self.dma_sems)
        sem = self.dma_sems[s]
        extra = [(sem, self.dma_tot[s])] if self.dma_tot[s] > 0 else []
        wl = self._collect(e, reads, writes, extra)
        self.dma_tot[s] += 16
        ev = (sem, self.dma_tot[s])

        def fn(eng, out=out, in_=in_):
            return eng.dma_start(out=out, in_=in_)

        self.ops[e].append((wl, fn, (sem, 16)))
        self._update(ev, reads, writes)
        return ev

    def coll(self, e, fn, reads=(), writes=()):
        sem = self.nc.alloc_semaphore(name=f"ccsem{len(self.ops[e])}")
        wl = self._collect(e, reads, writes)
        ev = (sem, 1)
        self.ops[e].append((wl, fn, (sem, 1)))
        self._update(ev, reads, writes)
        self.extra_evs.append(ev)
        return ev

    def barrier(self, final=False):
        evs = [(self.sem[e], self.cnt[e]) for e in ENGS if self.cnt[e] > 0]
        evs += [(self.dma_sems[i], self.dma_tot[i]) for i in range(len(self.dma_sems)) if self.dma_tot[i] > 0]
        if final:
            evs += self.extra_evs
        for e in ENGS:
            wl = self._collect(e, (), (), evs)
            self.ops[e].append((wl, None, None))

    def emit(self):
        engmap = {"pe": "tensor", "act": "scalar", "dve": "vector", "pool": "gpsimd", "sp": "sync"}
        with self.nc.Block() as block:
            for e in ENGS:
                ops = self.ops[e]

                def body(eng, ops=ops):
                    for wl, fn, inc in ops:
                        for sem, val in wl:
                            eng.wait_ge(sem, val)
                        if fn is not None:
                            ins = fn(eng)
                            if inc is not None:
                                ins.then_inc(inc[0], inc[1])

                getattr(block, engmap[e])(body)


class Arena:
    def __init__(self, t, ncols):
        self.t = t
        self.top = 0
        self.ncols = ncols
        self.peak = 0
        self.guard = None

    def alloc(self, cols, dt=F32):
        c32 = cols if dt == F32 else (cols + 1) // 2
        a = self.top
        self.top += c32
        self.peak = max(self.peak, self.top)
        assert self.top <= self.ncols, f"arena overflow {self.top}"
        assert self.guard is None or self.top <= self.guard, f"arena guard hit {self.top} > {self.guard}"
        ap = self.t[:, a:a + c32]
        if dt != F32:
            ap = ap.bitcast(dt)
        return ap

    def mark(self):
        return self.top

    def release(self, m):
        self.top = m


PRM = {}
_off = 0
for _n, _c in [("ada_b", 48), ("ng0", 8), ("ng1", 8), ("ng2", 8), ("ng3", 8), ("mu", 15), ("w0", 8), ("a0", 8),
               ("kks", 4), ("ka", 4), ("rk", 4), ("lng", 4), ("lnb", 4), ("mconv", 72), ("gbi", 1), ("gbf", 1),
               ("gng", 4), ("fconv", 198), ("fcb", 22)]:
    PRM[_n] = (_off, _c)
    _off += _c
NPRM = _off


def _fm(v):
    v = np.asarray(v, np.float32).reshape(-1, 128)
    return np.ascontiguousarray(v.T)


def _tile_w(W):
    K, N = W.shape
    return np.ascontiguousarray(W.reshape(K // 128, 128, N // 128, 128).transpose(2, 1, 0, 3).reshape(N // 128, 128, K))


def build(nc, dbg=False):
    P = Prog(nc)
    es = contextlib.ExitStack()
    din = lambda name, shape: nc.dram_tensor(name, shape, F32, kind="ExternalInput").ap()
    dout = lambda name, shape: nc.dram_tensor(name, shape, F32, kind="ExternalOutput").ap()
    dx = {"s": din("xs", [D, TS]), "p": din("xp", [D, 512])}
    dcond = din("cond", [128, 16])
    dsrw = din("srw", [128, 512])
    dsmC = din("smC", [128, 1024])
    dsmn = din("smn", [128, 8])
    dsmm = din("smm", [128, 8])
    dprm = din("prm", [128, NPRM])
    dlora = din("lora", [128, 1536])
    dadaw = din("adaw", [12, 128, 1024])
    dadab = din("adab", [128, 12])
    ag2_in = nc.dram_tensor("ag2_in", [128, 24], F32)
    ag2_out = nc.dram_tensor("ag2_out", [4 * 128, 24], F32)
    dwin = din("win", [49, 128, 1024])
    dwin_s = din("win_s", [12, 128, 1024])
    dprm_s = din("prm_s", [128, NPRM])
    dlora_s = din("lora_s", [128, 1536])

    dwbr = din("wbr", [8, 128, 512])
    dwbm = din("wbm", [8, 128, 512])
    dwout = din("wout", [8, 128, 1024])
    dfup = din("fup", [44, 128, 1024])
    dfdn = din("fdn", [22, 128, 1024])
    dy = {"s": dout("ys", [D, 512]), "p": dout("yp", [D, 512])}
    dxw = din("xw", [D, 1024])
    ddlt = din("dlt", [128, 4])
    drowm = din("rowm", [128, 16])
    rs_in = nc.dram_tensor("rs_in", [4 * 4 * 2 * 128, 1024], BF16)
    rs_out = nc.dram_tensor("rs_out", [4 * 2 * 128, 1024], BF16)
    do_srw = dout("o_srw", [128, 2 * 2 * 4 * 64])
    do_smC = dout("o_smC", [128, 2 * 2 * 4 * 128])
    do_smn = dout("o_smn", [128, 16])
    do_smm = dout("o_smm", [128, 2])
    if dbg:
        dxm = {"s": dout("xmid_s", [D, 1024]), "p": dout("xmid_p", [D, 512])}
    else:
        dxm = {"s": nc.dram_tensor("xmid_s", [D, 1024], F32).ap(), "p": nc.dram_tensor("xmid_p", [D, 512], F32).ap()}
    dbg_out = {}
    if dbg:
        for g, T in (("s", 1024), ("p", 512)):
            dbg_out["yr_" + g] = nc.dram_tensor("dbg_yr_" + g, [128, 4 * T], BF16, kind="ExternalOutput").ap()
            dbg_out["ym_" + g] = nc.dram_tensor("dbg_ym_" + g, [128, 4 * T], BF16, kind="ExternalOutput").ap()
            dbg_out["h_" + g] = nc.dram_tensor("dbg_h_" + g, [128, 8 * T], BF16, kind="ExternalOutput").ap()
            dbg_out["mg_" + g] = nc.dram_tensor("dbg_mg_" + g, [128, 8 * T], BF16, kind="ExternalOutput").ap()
            dbg_out["f_" + g] = dout("dbg_f_" + g, [128, 8 * 512])
    dout_buf = Buf("dram_out")
    dxm_buf = {"s": Buf(), "p": Buf()}

    NCOL = 53200
    arena_t = es.enter_context(nc.sbuf_tensor("arena", [128, NCOL], F32))
    A = Arena(arena_t, NCOL)
    psum = [es.enter_context(nc.psum_tensor(f"ps{i}", [128, 512], F32)) for i in range(8)]
    psb = [Buf(f"ps{i}") for i in range(8)]
    ps_rr = [0]

    def PS():
        i = ps_rr[0]
        ps_rr[0] = (i + 1) % 8
        return psum[i], psb[i]

    POOL_OK = [False]

    def _e(e):
        return "dve" if (e == "pool" and not POOL_OK[0]) else e

    def TT(e, out, a, b, op, R, W):
        e = _e(e)
        P.op(e, lambda g: g.tensor_tensor(out=out, in0=a, in1=b, op=op), R, W)

    def TSC(e, out, a, s1, op0, R, W, s2=None, op1=None):
        e = _e(e)
        if s2 is None:
            P.op(e, lambda g: g.tensor_scalar(out=out, in0=a, scalar1=s1, scalar2=None, op0=op0), R, W)
        else:
            P.op(e, lambda g: g.tensor_scalar(out=out, in0=a, scalar1=s1, scalar2=s2, op0=op0, op1=op1), R, W)

    def STT(e, out, a, s, b, op0, op1, R, W):
        e = "dve"
        P.op(e, lambda g: g.scalar_tensor_tensor(out=out, in0=a, scalar=s, in1=b, op0=op0, op1=op1), R, W)

    def ACT(out, in_, f, R, W, bias=None, scale=None):
        kw = {}
        if bias is not None:
            kw["bias"] = bias
        if scale is not None:
            kw["scale"] = scale
        P.op("act", lambda g: g.activation(out=out, in_=in_, func=f, **kw), R, W)

    def CP(e, out, in_, R, W):
        e = _e(e)
        if e == "act":
            P.op("act", lambda g: g.copy(out=out, in_=in_), R, W)
        else:
            P.op(e, lambda g: g.tensor_copy(out=out, in_=in_), R, W)

    def MM(out, lhsT, rhs, R, W, start=True, stop=True):
        P.op("pe", lambda g: g.matmul(out, lhsT=lhsT, rhs=rhs, start=start, stop=stop), R, W)

    def RECIP(ap, b):
        P.op("dve", lambda g, ap=ap: g.reciprocal(out=ap, in_=ap), [b], [b])

    def MSET(e, ap, v, W):
        e = _e(e)
        P.op(e, lambda g: g.memset(ap, v), (), W)

    def v3(ap, b):
        return ap.rearrange("p (a b) -> p a b", b=b)

    cb = Buf("consts")
    ident = A.alloc(128)
    blk = A.alloc(128)
    onesf = A.alloc(128)
    triu = A.alloc(128)
    M1 = A.alloc(128)
    M2 = A.alloc(64)
    Ibc = A.alloc(64)
    rmask = A.alloc(512)
    selh = A.alloc(512)
    onesb = A.alloc(128, BF16)
    prm = A.alloc(NPRM)
    lora = A.alloc(1536, BF16)
    prm_s = A.alloc(NPRM)
    lora_s = A.alloc(1536, BF16)
    condt = A.alloc(16)
    sct = A.alloc(16, BF16)
    modT = A.alloc(96)
    mA1 = A.alloc(16); mB1 = A.alloc(16); mG1 = A.alloc(16); mA2 = A.alloc(16); mB2 = A.alloc(16); mG2 = A.alloc(16)
    em0 = A.alloc(8)
    epsc = A.alloc(4)

    def prmc(name, i=0, n=1):
        o, c = PRM[name]
        return prm[:, o + i:o + i + n]

    P.dma("sp", prm, dprm, (), [cb])
    P.dma("pool", lora, dlora, (), [cb])
    P.dma("sp", prm_s, dprm_s, (), [cb])
    P.dma("pool", lora_s, dlora_s, (), [cb])
    P.dma("sp", condt, dcond, (), [cb])
    P.dma("sp", em0, dsmm, (), [cb])
    MSET("pool", ident, 1.0, [cb])
    P.op("pool", lambda g: g.affine_select(out=ident, in_=ident, pattern=[[-1, 128]], compare_op=ALU.is_equal, fill=0.0, base=0, channel_multiplier=1), [cb], [cb])
    MSET("pool", onesf, 1.0, [cb])
    MSET("pool", onesb, 1.0, [cb])
    MSET("pool", blk, 0.0, [cb])
    MSET("pool", blk[0:64, 0:64], 1.0, [cb])
    MSET("pool", blk[64:128, 64:128], 1.0, [cb])
    MSET("pool", triu, 1.0, [cb])
    P.op("pool", lambda g: g.affine_select(out=triu, in_=triu, pattern=[[1, 128]], compare_op=ALU.is_ge, fill=0.0, base=0, channel_multiplier=-1), [cb], [cb])
    CP("pool", Ibc[0:64, :], ident[0:64, 0:64], [cb], [cb])
    CP("pool", Ibc[64:128, :], ident[64:128, 64:128], [cb], [cb])
    for hp in range(2):
        sl = slice(hp * 64, hp * 64 + 64)
        CP("pool", M1[sl, 64:128], triu[sl, hp * 64:hp * 64 + 64], [cb], [cb])
        TT("pool", M1[sl, 0:64], M1[sl, 64:128], Ibc[sl, :], ALU.subtract, [cb], [cb])
    for hp in range(2):
        sl = slice(hp * 64, hp * 64 + 64)
        TT("pool", M2[sl, :], onesf[sl, 0:64], triu[sl, hp * 64:hp * 64 + 64], ALU.subtract, [cb], [cb])
    MSET("pool", rmask, 1.0, [cb])
    MSET("pool", v3(rmask, 64)[:, :, 0:1], 0.0, [cb])
    MSET("pool", selh, 0.0, [cb])
    for h in range(4):
        for base in (0, 32):
            TSC("pool", v3(selh, 128)[base:base + 4, h, :], onesf[base:base + 4, :], ident[base:base + 4, base + h:base + h + 1], ALU.mult, [cb], [cb])
    MSET("pool", epsc, 1e-6, [cb])
    ACT(em0, em0, AF.Exp, [cb], [cb])

    NWB = 2
    wts = [A.alloc(1024, BF16) for _ in range(NWB)]
    wtb = [Buf(f"wt{i}") for i in range(NWB)]
    w_rr = [0]

    wpool = [None]

    def load_w(dram_chunk, ncols=1024):
        if wpool[0] is not None:
            lst, rr = wpool[0]
            i = rr[0]
            rr[0] = (i + 1) % len(lst)
            P.dma("pool", lst[i][0][:, 0:ncols], dram_chunk, (), [lst[i][1]])
            return lst[i]
        i = w_rr[0]
        w_rr[0] = (i + 1) % NWB
        P.dma("pool", wts[i][:, 0:ncols], dram_chunk, (), [wtb[i]])
        return wts[i], wtb[i]

    def proj(wdram, kcn, rhs_fn, R, ntile, cons):
        wt, wb = load_w(wdram, kcn * 128)
        for n in range(ntile):
            ps, pb = PS()
            for kc in range(kcn):
                MM(ps[:, :], wt[:, kc * 128:(kc + 1) * 128], rhs_fn(kc, n), R + [wb], [pb], start=(kc == 0), stop=(kc == kcn - 1))
            cons(n, ps, pb)

    ACT(sct, condt, AF.Silu, [cb], [cb])
    psm, psmb = PS()
    for c in range(12):
        wt, wb = load_w(dadaw[c])
        for kc in range(8):
            MM(psm[:, 2 * c:2 * c + 2], wt[:, kc * 128:(kc + 1) * 128], sct[:, 2 * kc:2 * kc + 2], [cb, wb], [psmb], start=(kc == 0), stop=(kc == 7))
    adab_sb = A.alloc(12)
    modp = A.alloc(24)
    P.dma("sp", adab_sb, dadab, (), [cb])
    TT("dve", v3(modp, 2), v3(psm[:, 0:24], 2), adab_sb.unsqueeze(2).to_broadcast([128, 12, 2]), ALU.add, [psmb, cb], [cb])
    ag2ib, ag2ob = Buf(), Buf()
    P.dma("pool", ag2_in.ap(), modp, [cb], [ag2ib])
    P.coll("pool", lambda g: g.collective_compute("AllGather", ALU.bypass, replica_groups=[[0, 2, 4, 6], [1, 3, 5, 7]],
                                                  ins=[ag2_in.ap().opt()], outs=[ag2_out.ap().opt()]), [ag2ib], [ag2ob])
    modb = Buf("mod")
    P.dma("pool", modT.rearrange("p (r c) -> p r c", r=4), ag2_out.ap().rearrange("(r p) c -> p r c", p=128), [ag2ob], [modb])
    m3 = v3(modT, 2)

    def modc(i):
        return m3[:, 8 * i:8 * i + 8, :]

    def ngb(i):
        o, _ = PRM[f"ng{i}"]
        return prm[:, o:o + 8].unsqueeze(2).to_broadcast([128, 8, 2])

    mod_done = [False]

    def derive_mod():
        if mod_done[0]:
            return
        mod_done[0] = True
        STT("dve", v3(mA1, 2), modc(1), 1.0, ngb(0), ALU.add, ALU.mult, [cb, modb], [modb])
        CP("dve", v3(mB1, 2), modc(0), [modb], [modb])
        TT("dve", v3(mG1, 2), modc(2), ngb(1), ALU.mult, [cb, modb], [modb])
        STT("dve", v3(mA2, 2), modc(4), 1.0, ngb(2), ALU.add, ALU.mult, [cb, modb], [modb])
        CP("dve", v3(mB2, 2), modc(3), [modb], [modb])
        TT("dve", v3(mG2, 2), modc(5), ngb(3), ALU.mult, [cb, modb], [modb])

    def mcol(m, kc, ci):
        return m[:, 2 * kc + ci:2 * kc + ci + 1]

    if dbg:
        dmod = dout("dbg_mod", [128, 96 + 16 * 6])
        P.dma("sp", dmod[:, 0:96], modT, [cb], [])
        for k_, m_ in enumerate([mA1, mB1, mG1, mA2, mB2, mG2]):
            P.dma("sp", dmod[:, 96 + 16 * k_:96 + 16 * (k_ + 1)], m_, [cb], [])
    base_mark = A.mark()
    shared = {}
    TOPB = 40000
    POOL_OK[0] = False

    def run_group(gn):
        isS = gn == "s"
        prm_g = prm_s if isS else prm
        lora_g = lora_s if isS else lora
        pairs = [0] if isS else list(range(4))
        heads = [0] if isS else list(range(4))

        def prmc(name, i=0, n=1):
            o, c = PRM[name]
            return prm_g[:, o + i:o + i + n]

        def wch(kind, idx=0):
            if isS:
                return dwin_s[{"r": 0, "k": 1, "v": 2, "wd": 3, "ad": 4, "gd": 5, "mq": 6, "mk": 7, "mv": 8, "mo": 9, "gi": 10, "gf": 11}[kind]]
            return dwin[{"r": 0, "k": 4, "v": 8, "wd": 12, "ad": 13, "gd": 14, "mq": 15, "mk": 19, "mv": 23, "mo": 27, "gi": 47, "gf": 48}[kind] + idx]
        T = TS if isS else 512
        nseq = 1 if isS else 2
        Tq = T // nseq
        NT = T // 512
        ci_ = 1 if isS else 0
        xd = dx[gn]
        A.release(base_mark)
        hT = A.alloc(8 * T, BF16)
        hTb = [Buf(f"hT{n}") for n in range(NT)]
        h3 = v3(hT, T)
        hmark = A.mark()
        yr = A.alloc((1 if isS else 4) * T, BF16)
        if not isS:
            ym = A.alloc(4 * T, BF16)
            ymb = Buf("ym")
        yrb = Buf("yr")
        grp_mark = A.mark()

        def x_view(dram, n):
            return dram.rearrange("(kc p) t -> p kc t", p=128)[:, :, n * 512:(n + 1) * 512]

        def norm1(xt, xtb, sq, sqb, rstd, rstdb):
            ACT(sq, xt, AF.Square, [xtb], [sqb])
            ps, pb = PS()
            for kc in range(8):
                MM(ps[:, :], onesb, sq[:, kc * 512:(kc + 1) * 512], [cb, sqb], [pb], start=(kc == 0), stop=(kc == 7))
            ACT(rstd, ps[:, :], AF.Sqrt, [pb], [rstdb], bias=epsc[:, 0:1], scale=1.0 / D)
            RECIP(rstd, rstdb)

        def norm2(xt, xtb, mA, mB, out_fn, outb, tmp, tmpb, rstd, rstdb):
            derive_mod()
            for kc in range(8):
                STT("dve", tmp[:, kc * 512:(kc + 1) * 512], xt[:, kc * 512:(kc + 1) * 512], mcol(mA, kc, ci_), rstd, ALU.mult, ALU.mult, [xtb, rstdb, modb], [tmpb])
            for kc in range(8):
                ACT(out_fn(kc), tmp[:, kc * 512:(kc + 1) * 512], AF.Identity, [tmpb, modb], [outb], bias=mcol(mB, kc, ci_))

        def norm_mod(xt, xtb, mA, mB, out_fn, outb, tmp, tmpb, sq, sqb, rstd, rstdb):
            norm1(xt, xtb, sq, sqb, rstd, rstdb)
            norm2(xt, xtb, mA, mB, out_fn, outb, tmp, tmpb, rstd, rstdb)

        def norm1va(xv, xvb, sq, sqb, rstd, rstdb):
            ACT(v3(sq, 512), xv, AF.Square, [xvb], [sqb])
            ps, pb = PS()
            for kc in range(8):
                MM(ps[:, :], onesb, sq[:, kc * 512:(kc + 1) * 512], [cb, sqb], [pb], start=(kc == 0), stop=(kc == 7))
            ACT(rstd, ps[:, :], AF.Sqrt, [pb], [rstdb], bias=epsc[:, 0:1], scale=1.0 / D)

        def norm_tiles_v(n_tiles, load_fn, mA, mB, out_bufs, xviews, nb_):
            load_fn(0)
            norm1va(xviews[0][0], xviews[0][1], nb_[0][2], nb_[0][3], nb_[0][4], nb_[0][5])
            RECIP(nb_[0][4], nb_[0][5])
            for n in range(n_tiles):
                i = n % 2
                j = (n + 1) % 2
                xv, xvb = xviews[n]
                tmp, tmpb, rstd, rstdb = nb_[i][0], nb_[i][1], nb_[i][4], nb_[i][5]
                if n + 1 < n_tiles:
                    load_fn(n + 1)
                    norm1va(xviews[n + 1][0], xviews[n + 1][1], nb_[j][2], nb_[j][3], nb_[j][4], nb_[j][5])
                derive_mod()
                for kc in range(8):
                    STT("dve", tmp[:, kc * 512:(kc + 1) * 512], xv[:, kc, :], mcol(mA, kc, ci_), rstd, ALU.mult, ALU.mult, [xvb, rstdb, modb], [tmpb])
                if n + 1 < n_tiles:
                    RECIP(nb_[j][4], nb_[j][5])
                for kc in range(8):
                    ACT(h3[:, kc, n * 512:(n + 1) * 512], tmp[:, kc * 512:(kc + 1) * 512], AF.Identity, [tmpb, modb], [out_bufs[n]], bias=mcol(mB, kc, ci_))

        def norm_tiles(n_tiles, load_fn, mA, mB, out_bufs, xt, xtb, nb_):
            load_fn(0)
            norm1(xt[0], xtb[0], nb_[0][2], nb_[0][3], nb_[0][4], nb_[0][5])
            for n in range(n_tiles):
                i = n % 2
                if n + 1 < n_tiles:
                    j = (n + 1) % 2
                    load_fn(n + 1)
                    norm1(xt[j], xtb[j], nb_[j][2], nb_[j][3], nb_[j][4], nb_[j][5])
                norm2(xt[i], xtb[i], mA, mB, lambda kc, n=n: h3[:, kc, n * 512:(n + 1) * 512], out_bufs[n], nb_[i][0], nb_[i][1], nb_[i][4], nb_[i][5])

        mk = A.mark()
        nb_ = [(A.alloc(4096), Buf(), A.alloc(4096, BF16), Buf(), A.alloc(512), Buf()) for _ in range(2)]
        if NT >= 2:
            xbig = [A.alloc(8192) for _ in range(2)]
            xbigb = [Buf() for _ in range(2)]
            xt = [None] * NT
            xtb = [None] * NT

            def load_big(n):
                if n % 2 == 0:
                    k = (n // 2) % 2
                    P.dma("sp", v3(xbig[k], 1024), xd.rearrange("(kc p) t -> p kc t", p=128)[:, :, n * 512:n * 512 + 1024], (), [xbigb[k]])

            class _XV:
                pass
            xviews = [(v3(xbig[(n // 2) % 2], 1024)[:, :, (n % 2) * 512:(n % 2) * 512 + 512], xbigb[(n // 2) % 2]) for n in range(NT)]
            norm_tiles_v(NT, load_big, mA1, mB1, hTb, xviews, nb_)
        else:
            xt = [A.alloc(4096) for _ in range(2)]
            xtb = [Buf() for _ in range(2)]
            norm_tiles(NT, lambda n: P.dma("sp", v3(xt[n % 2], 512), x_view(xd, n), (), [xtb[n % 2]]), mA1, mB1, hTb, xt, xtb, nb_)
        if dbg and not isS:
            P.dma("sp", dbg_out["h_" + gn], hT, hTb, [])
        P.barrier()
        A.release(mk)

        def h_rhs(kc, n):
            return h3[:, kc, n * 512:(n + 1) * 512]

        mk_r = A.mark()
        if not isS:
            wpool[0] = ([(A.alloc(1024, BF16), Buf()) for _ in range(3)], [0])
        Tp = Tq + 2
        twd = A.alloc(T, BF16); ad_ = A.alloc(T, BF16); sgd = A.alloc(T, BF16)
        lb = Buf("lora_in")
        zp = A.alloc(nseq * Tp); zpb = Buf("zp")
        zp3 = v3(zp, Tp)
        MSET("pool", zp, 0.0, [zpb])
        stmp_blk = A.alloc(max(T, 2048)); stb = Buf()
        stmp = stmp_blk[:, 0:T]

        def shift_proj(wap, chunk, out, outb, post=None):
            def cons(n, ps, pb):
                if isS:
                    CP("act", zp3[:, 0, 1 + n * 512:1 + (n + 1) * 512], ps[:, :], [pb], [zpb])
                else:
                    CP("act", zp3[:, :, 1:1 + Tq], v3(ps[:, :], Tq), [pb], [zpb])
            proj(wap, 8, h_rhs, hTb, NT, cons)
            zc = zp3[:, :, 1:1 + Tq]
            s3 = v3(stmp, Tq)
            TT("pool", s3, zp3[:, :, 0:Tq], zp3[:, :, 2:2 + Tq], ALU.add, [zpb], [stb])
            STT("dve", s3, s3, 0.5, zc, ALU.mult, ALU.subtract, [stb, zpb], [stb])
            if post is None:
                STT("dve", v3(out, Tq), s3, prmc("mu", chunk), zc, ALU.mult, ALU.add, [stb, zpb, cb], [outb])
            else:
                STT("dve", s3, s3, prmc("mu", chunk), zc, ALU.mult, ALU.add, [stb, zpb, cb], [stb])
                ACT(out, stmp, post, [stb], [outb])

        shift_proj(wch("wd"), 12, twd, lb, AF.Tanh)
        shift_proj(wch("ad"), 13, ad_, lb, AF.Identity)
        shift_proj(wch("gd"), 14, sgd, lb, AF.Sigmoid)

        NPB = 1 if isS else 2
        PA = []
        for _ in range(NPB):
            PA.append({k_: (A.alloc(T), Buf(k_)) for k_ in ("rs", "ks", "vs", "kk", "bacc", "Y")})
        tn = {}
        for name in ["sgw", "aa", "t1", "kd", "G", "eGn", "eGx", "KH", "BH", "t2", "VSp", "PT0", "PT1"]:
            tn[name] = (A.alloc(512), Buf(name))
        nblk = stmp_blk
        nbuf = stb
        for i_, name in enumerate(["N0", "N1", "NT0", "NT1"]):
            tn[name] = (nblk[:, i_ * 512:(i_ + 1) * 512], nbuf)
        MS = []
        for _ in range(2):
            M = {}
            for name, sz in (("AR", 1024), ("A1", 1024), ("A2", 1024), ("KT", 512), ("BT", 512), ("VT", 512), ("eG", 512), ("TT", 512)):
                M[name] = (A.alloc(sz), Buf(name))
            MS.append(M)
        ST = [(A.alloc(64), Buf("ST0")), (A.alloc(64), Buf("ST1"))]
        Xt, Xb = A.alloc(64), Buf("X")
        Ut, Ub = A.alloc(64), Buf("U")
        sttmp, sttb = A.alloc(64), Buf()
        lorav = v3(lora_g, 512)
        srw_sb = A.alloc(512); srwb = Buf()
        if isS:
            P.dma("sp", srw_sb, dsrw, (), [srwb])
        srw4 = srw_sb.rearrange("p (d q v) -> p d q v", d=2, q=4)
        osrw5 = do_srw.rearrange("p (s d q v) -> p s d q v", s=2, d=2, q=4)

        def setup_pair(p):
            pa = PA[p % NPB]
            rs, rsb = pa["rs"]; ks, ksb = pa["ks"]; vs, vsb = pa["vs"]; kk, kkb = pa["kk"]
            shift_proj(wch("r", p), p, rs, rsb)
            shift_proj(wch("k", p), 4 + p, ks, ksb)
            shift_proj(wch("v", p), 8 + p, vs, vsb)
            for n in range(NT):
                sl = slice(n * 512, (n + 1) * 512)
                t1, t1b = tn["t1"]; t2, t2b = tn["t2"]
                TSC("pool", t1, ks[:, sl], prmc("kks", p), ALU.mult, [ksb, cb], [t1b])
                TT("pool", t2, t1, t1, ALU.mult, [t1b], [t2b])
                ps, pb = PS()
                MM(ps[:, :], blk, t2, [cb, t2b], [pb])
                TSC("dve", t2, ps[:, :], 1e-24, ALU.max, [pb], [t2b])
                ACT(t2, t2, AF.Sqrt, [t2b], [t2b])
                RECIP(t2, t2b)
                TT("dve", kk[:, sl], t1, t2, ALU.mult, [t1b, t2b], [kkb])

        def stageA(p, d, m, M):
            pa = PA[p % NPB]
            rs, rsb = pa["rs"]; ks, ksb = pa["ks"]; vs, vsb = pa["vs"]; kk, kkb = pa["kk"]; bacc, baccb = pa["bacc"]
            drows = slice(d * 64, d * 64 + 64)
            n = m if d == 0 else NT - 1 - m
            sl = slice(n * 512, (n + 1) * 512)

            def V(ap):
                a_ = ap[:, sl]
                return a_ if d == 0 else a_[:, ::-1]

            def rv(ap):
                return ap if d == 0 else ap[:, ::-1]
            sgw, sgwb = tn["sgw"]; aa, aab = tn["aa"]; t1, t1b = tn["t1"]; kd, kdb = tn["kd"]
            G, Gb = tn["G"]; eGn, eGnb = tn["eGn"]; eGx, eGxb = tn["eGx"]
            KH, KHb = tn["KH"]; BH, BHb = tn["BH"]; t2, t2b = tn["t2"]; VSp, VSpb = tn["VSp"]
            eG, eGb = M["eG"]; KT, KTb = M["KT"]; BT, BTb = M["BT"]; VT, VTb = M["VT"]
            ARt, ARb = M["AR"]; A1t, A1b = M["A1"]; A2t, A2b = M["A2"]; TTf, TTfb = M["TT"]
            ps, pb = PS()
            MM(ps[:, :], lorav[drows, 0, p * 128:(p + 1) * 128], twd[drows, sl], [cb, lb], [pb])
            ACT(sgw, ps[:, :], AF.Sigmoid, [pb, cb], [sgwb], bias=prmc("w0", d * 4 + p))
            ps, pb = PS()
            MM(ps[:, :], lorav[drows, 1, p * 128:(p + 1) * 128], ad_[drows, sl], [cb, lb], [pb])
            ACT(aa, ps[:, :], AF.Sigmoid, [pb, cb], [aab], bias=prmc("a0", d * 4 + p))
            yield
            TSC("pool", t1, aa, -1.0, ALU.add, [aab, cb], [t1b], s2=prmc("ka", p), op1=ALU.mult)
            STT("pool", kd, t1, 1.0, ks[:, sl], ALU.add, ALU.mult, [t1b, ksb], [kdb])
            if d == 0:
                STT("pool", bacc[:, sl], rs[:, sl], prmc("rk", p), kd, ALU.mult, ALU.mult, [rsb, kdb, cb], [baccb])
            else:
                STT("pool", t1, rs[:, sl], prmc("rk", p), kd, ALU.mult, ALU.mult, [rsb, kdb, cb], [t1b])
                TT("pool", bacc[:, sl], bacc[:, sl], t1, ALU.add, [baccb, t1b], [baccb])
            yield
            P.op("dve", lambda g, G=G, sgw=sgw, d=d: g.tensor_tensor_scan(out=G, data0=rmask, data1=(sgw if d == 0 else sgw[:, ::-1]), initial=0.0, op0=ALU.mult, op1=ALU.add), [sgwb, cb], [Gb])
            ACT(eG, G, AF.Exp, [Gb], [eGb], scale=-DSC)
            ACT(eGn, G, AF.Exp, [Gb], [eGnb], scale=DSC)
            TT("pool", t2, G, rv(sgw), ALU.subtract, [Gb, sgwb], [t2b])
            ACT(eGx, t2, AF.Exp, [t2b], [eGxb], scale=-DSC)
            yield
            AR4 = ARt.rearrange("p (c two l) -> p c two l", two=2, l=64)
            TT("dve", AR4[:, :, 1, :], v3(V(rs), 64), v3(eG, 64), ALU.mult, [rsb, eGb], [ARb])
            STT("pool", AR4[:, :, 0, :], v3(V(kk), 64), -1.0, v3(eGx, 64), ALU.mult, ALU.mult, [kkb, eGxb], [ARb])
            yield
            TT("dve", KH, rv(kd), eGn, ALU.mult, [kdb, eGnb], [KHb])
            TT("pool", t2, V(kk), rv(aa), ALU.mult, [kkb, aab], [t2b])
            TT("pool", BH, t2, eGn, ALU.mult, [t2b, eGnb], [BHb])
            CP("pool", VSp, V(vs), [vsb], [VSpb])
            yield
            for (src, srcb, dst, dstb) in ((KH, KHb, KT, KTb), (BH, BHb, BT, BTb), (VSp, VSpb, VT, VTb)):
                ps, pb = PS()
                for c in range(8):
                    for hp in range(2):
                        hr = slice(hp * 64, hp * 64 + 64)
                        MM(ps[hr, c * 64:(c + 1) * 64], src[hr, c * 64:(c + 1) * 64], ident[hr, hr], [srcb, cb], [pb])
                CP("act", dst, ps[:, :], [pb], [dstb])
                yield
            for (lh, lhb, dst, dstb) in ((BH, BHb, A1t, A1b), (KH, KHb, A2t, A2b)):
                for half in range(2):
                    ps, pb = PS()
                    for c4 in range(4):
                        c = half * 4 + c4
                        for hp in range(2):
                            hr = slice(hp * 64, hp * 64 + 64)
                            MM(ps[hr, c4 * 128:(c4 + 1) * 128], lh[hr, c * 64:(c + 1) * 64], ARt[hr, c * 128:(c + 1) * 128], [lhb, ARb], [pb])
                    TT("dve", v3(dst[:, half * 512:(half + 1) * 512], 128), v3(ps[:, :], 128), M1.unsqueeze(1).to_broadcast([128, 4, 128]), ALU.mult, [pb, cb], [dstb])
                    yield
            N0, N0b = tn["N0"]; N1, N1b = tn["N1"]; NT0, NT0b = tn["NT0"]; NT1, NT1b = tn["NT1"]
            PT0, PT0b = tn["PT0"]; PT1, PT1b = tn["PT1"]
            ps, pb = PS()
            for c in range(8):
                for hp in range(2):
                    hr = slice(hp * 64, hp * 64 + 64)
                    MM(ps[hr, c * 64:(c + 1) * 64], AR4[hr, c, 0, :], BH[hr, c * 64:(c + 1) * 64], [ARb, BHb], [pb])
            TT("dve", v3(N0, 64), v3(ps[:, :], 64), M2.unsqueeze(1).to_broadcast([128, 8, 64]), ALU.mult, [pb, cb], [N0b])
            A13 = v3(A1t, 128)
            CP("pool", v3(NT0, 64), A13[:, :, 0:64], [A1b], [NT0b])
            TT("pool", v3(PT0, 64), A13[:, :, 0:64], Ibc.unsqueeze(1).to_broadcast([128, 8, 64]), ALU.add, [A1b, cb], [PT0b])
            yield
            Ncur, Ncb, NTcur, NTcb, PTc, PTcb = N0, N0b, NT0, NT0b, PT0, PT0b
            Nnx, Nnb, NTnx, NTnb, PTn, PTnb = N1, N1b, NT1, NT1b, PT1, PT1b
            for lev in range(1, 6):
                ps, pb = PS()
                for c in range(8):
                    for hp in range(2):
                        hr = slice(hp * 64, hp * 64 + 64)
                        cs = slice(c * 64, (c + 1) * 64)
                        MM(ps[hr, cs], NTcur[hr, cs], Ncur[hr, cs], [NTcb, Ncb], [pb])
                if lev < 5:
                    ps2, pb2 = PS()
                    for c in range(8):
                        for hp in range(2):
                            hr = slice(hp * 64, hp * 64 + 64)
                            cs = slice(c * 64, (c + 1) * 64)
                            MM(ps2[hr, cs], Ncur[hr, cs], NTcur[hr, cs], [NTcb, Ncb], [pb2])
                CP("act", Nnx, ps[:, :], [pb], [Nnb])
                if lev < 5:
                    CP("act", NTnx, ps2[:, :], [pb2], [NTnb])
                yield
                ps3, pb3 = PS()
                for c in range(8):
                    for hp in range(2):
                        hr = slice(hp * 64, hp * 64 + 64)
                        cs = slice(c * 64, (c + 1) * 64)
                        MM(ps3[hr, cs], Nnx[hr, cs], PTc[hr, cs], [Nnb, PTcb], [pb3])
                if lev == 5:
                    TT("dve", TTf, ps3[:, :], PTc, ALU.add, [pb3, PTcb], [TTfb])
                else:
                    TT("dve", PTn, ps3[:, :], PTc, ALU.add, [pb3, PTcb], [PTnb])
                yield
                Ncur, Ncb, Nnx, Nnb = Nnx, Nnb, Ncur, Ncb
                NTcur, NTcb, NTnx, NTnb = NTnx, NTnb, NTcur, NTcb
                PTc, PTcb, PTn, PTnb = PTn, PTnb, PTc, PTcb

        def stageB(p, d, m, M):
            pa = PA[p % NPB]
            Y, Yb = pa["Y"]
            n = m if d == 0 else NT - 1 - m
            sl = slice(n * 512, (n + 1) * 512)
            eG, eGb = M["eG"]; KT, KTb = M["KT"]; BT, BTb = M["BT"]; VT, VTb = M["VT"]
            ARt, ARb = M["AR"]; A1t, A1b = M["A1"]; A2t, A2b = M["A2"]; TTm, TTb = M["TT"]
            AR4 = ARt.rearrange("p (c two l) -> p c two l", two=2, l=64)
            A13 = v3(A1t, 128)
            A23 = v3(A2t, 128)
            eG3 = v3(eG, 64)
            for c in range(8):
                cs = slice(c * 64, (c + 1) * 64)
                if isS:
                    seq = 0
                    first = (m == 0 and c == 0)
                    last = False
                else:
                    seq = (c // 4) if d == 0 else 1 - (c // 4)
                    first = (c % 4 == 0)
                    last = (c % 4 == 3)
                Sc, Scb = ST[0]
                Sn, Snb = ST[1]
                if first:
                    if isS:
                        CP("pool", Sc, srw4[:, d, p, :], [srwb], [Scb])
                    else:
                        MSET("pool", Sc, 0.0, [Scb])
                ps, pb = PS()
                for hp in range(2):
                    hr = slice(hp * 64, hp * 64 + 64)
                    MM(ps[hr, 0:64], A23[hr, c, 0:64], VT[hr, cs], [A2b, VTb], [pb], start=True, stop=False)
                    MM(ps[hr, 0:64], AR4[hr, c, 0, :], Sc[hr, :], [ARb, Scb], [pb], start=False, stop=True)
                CP("act", Xt, ps[:, 0:64], [pb], [Xb])
                yield
                ps, pb = PS()
                for hp in range(2):
                    hr = slice(hp * 64, hp * 64 + 64)
                    MM(ps[hr, 0:64], TTm[hr, cs], Xt[hr, :], [TTb, Xb], [pb])
                CP("dve", Ut, ps[:, 0:64], [pb], [Ub])
                yield
                pss, pbs = PS()
                for hp in range(2):
                    hr = slice(hp * 64, hp * 64 + 64)
                    MM(pss[hr, 0:64], BT[hr, cs], Ut[hr, :], [BTb, Ub], [pbs], start=True, stop=False)
                    MM(pss[hr, 0:64], KT[hr, cs], VT[hr, cs], [KTb, VTb], [pbs], start=False, stop=True)
                psy, pby = PS()
                for hp in range(2):
                    hr = slice(hp * 64, hp * 64 + 64)
                    MM(psy[hr, 0:64], Sc[hr, :], AR4[hr, c, 1, :], [Scb, ARb], [pby], start=True, stop=False)
                    MM(psy[hr, 0:64], Ut[hr, :], A13[hr, c, 64:128], [Ub, A1b], [pby], start=False, stop=False)
                    MM(psy[hr, 0:64], VT[hr, cs], A23[hr, c, 64:128], [VTb, A2b], [pby], start=False, stop=True)
                TT("dve", sttmp, pss[:, 0:64], Sc, ALU.add, [pbs, Scb], [sttb])
                TSC("dve", Sn, sttmp, eG3[:, c, 63:64], ALU.mult, [sttb, eGb], [Snb])
                ydst = Y[:, sl][:, cs] if d == 0 else Y[:, sl][:, ::-1][:, cs]
                if d == 0:
                    CP("act", ydst, psy[:, 0:64], [pby], [Yb])
                else:
                    TT("dve", ydst, psy[:, 0:64], ydst, ALU.add, [pby, Yb], [Yb])
                ST[0], ST[1] = ST[1], ST[0]
                if last:
                    P.dma("sp", osrw5[:, seq, d, p, :], ST[0][0], [ST[0][1]], [])
                yield

        def finalize_pair(p):
            pa = PA[p % NPB]
            vs, vsb = pa["vs"]; bacc, baccb = pa["bacc"]; Y, Yb = pa["Y"]
            for n in range(NT):
                sl = slice(n * 512, (n + 1) * 512)
                t1, t1b = tn["t1"]; t2, t2b = tn["t2"]; kd, kdb = tn["kd"]
                ps, pb = PS()
                MM(ps[:, :], blk, Y[:, sl], [cb, Yb], [pb])
                TT("pool", t1, Y[:, sl], Y[:, sl], ALU.mult, [Yb], [t1b])
                ps2, pb2 = PS()
                MM(ps2[:, :], blk, t1, [cb, t1b], [pb2])
                TSC("dve", t2, ps[:, :], 1.0 / 64, ALU.mult, [pb], [t2b])
                STT("dve", kd, t2, -1.0, t2, ALU.mult, ALU.mult, [t2b], [kdb])
                STT("dve", kd, ps2[:, :], 1.0 / 64, kd, ALU.mult, ALU.add, [pb2, kdb], [kdb])
                TSC("dve", kd, kd, 64e-5, ALU.add, [kdb], [kdb])
                ACT(kd, kd, AF.Sqrt, [kdb], [kdb])
                RECIP(kd, kdb)
                TT("dve", t2, Y[:, sl], t2, ALU.subtract, [Yb, t2b], [t2b])
                TT("dve", t2, t2, kd, ALU.mult, [t2b, kdb], [t2b])
                TSC("dve", t2, t2, prmc("lng", p), ALU.mult, [t2b, cb], [t2b], s2=prmc("lnb", p), op1=ALU.add)
                ps3, pb3 = PS()
                MM(ps3[:, :], blk, bacc[:, sl], [cb, baccb], [pb3])
                TT("dve", t1, ps3[:, :], vs[:, sl], ALU.mult, [pb3, vsb], [t1b])
                TT("dve", t2, t2, t1, ALU.add, [t2b, t1b], [t2b])
                ps4, pb4 = PS()
                MM(ps4[:, :], lorav[:, 2, p * 128:(p + 1) * 128], sgd[:, sl], [cb, lb], [pb4])
                TT("dve", v3(yr, T)[:, p, sl], t2, ps4[:, :], ALU.mult, [t2b, pb4], [yrb])

        def run_rr_g(gens):
            gens = list(gens)
            while gens:
                for g_ in list(gens):
                    try:
                        next(g_)
                    except StopIteration:
                        gens.remove(g_)
                yield

        def rwkv_body():
            units = [(p, d, m) for p in pairs for d in range(2) for m in range(NT)]
            prev = None
            for idx, u in enumerate(units + [None]):
                gens = []
                if u is not None:
                    if u[1] == 0 and u[2] == 0:
                        setup_pair(u[0])
                        yield
                    gens.append(stageA(u[0], u[1], u[2], MS[idx % 2]))
                if prev is not None:
                    gens.append(stageB(prev[0], prev[1], prev[2], MS[(idx - 1) % 2]))
                yield from run_rr_g(gens)
                if prev is not None and prev[1] == 1 and prev[2] == NT - 1:
                    finalize_pair(prev[0])
                    yield
                prev = u

        gR = rwkv_body()
        if isS:
            for _ in gR:
                pass
        if isS:
            P.barrier()
            A.release(mk_r)

        if isS:
            ym = A.alloc(T, BF16)
            ymb = Buf("ym")
        mk_m = A.mark()
        BB = A.alloc(nseq * (Tq + 1)); PSI = A.alloc(T); bbb, psib = Buf(), Buf()
        BB3 = v3(BB, Tq + 1)
        NC = Tq // 128
        PSIT = A.alloc(nseq * 2 * NC * 4); psitb = Buf()
        Mrow = A.alloc(2); mrowb = Buf()
        mtmp = A.alloc(2); mtb = Buf()
        mk_gate = A.mark()
        GI = A.alloc(T); GF = A.alloc(T); gib, gfb = Buf(), Buf()

        def mlstm_body():
            def cons_g(dst, dstb, bias):
                def cons(n, ps, pb):
                    ACT(dst[:, n * 512:(n + 1) * 512], ps[:, :], AF.Identity, [pb, cb], [dstb], bias=bias)
                return cons
            proj(wch("gi"), 8, h_rhs, hTb, NT, cons_g(GI, gib, prmc("gbi")))
            yield
            proj(wch("gf"), 8, h_rhs, hTb, NT, cons_g(GF, gfb, prmc("gbf")))
            yield
            ACT(GF, GF, AF.Exp, [gfb], [gfb], scale=-1.0)
            ACT(GF, GF, AF.Ln, [gfb], [gfb], bias=onesf[:, 0:1])
            MSET("pool", BB, 0.0, [bbb])
            MSET("pool", PSI, 0.0, [psib])
            for d in range(2):
                rr = slice(32 * d, 32 * d + 4)
                for s in range(nseq):
                    ss = slice(s * Tq, (s + 1) * Tq)
                    src = GF[rr, ss] if d == 0 else GF[rr, ss][:, ::-1]
                    P.op("dve", lambda g, s=s, rr=rr, src=src: g.tensor_tensor_scan(out=BB3[rr, s, 1:Tq + 1], data0=onesf[rr, 0:1].to_broadcast([4, Tq]), data1=src, initial=0.0, op0=ALU.mult, op1=ALU.subtract), [gfb, cb], [bbb])
                    gsrc = GI[rr, ss] if d == 0 else GI[rr, ss][:, ::-1]
                    TT("dve", PSI[rr, ss], gsrc, BB3[rr, s, 1:Tq + 1], ALU.subtract, [gib, bbb], [psib])
            PSIT5 = PSIT.rearrange("p (s d c h) -> p s d c h", s=nseq, d=2, c=NC)
            ps, pb = PS()
            psv = ps[:, 0:nseq * 2 * NC * 4].rearrange("p (s d c h) -> p s d c h", s=nseq, d=2, c=NC)
            for s in range(nseq):
                for d in range(2):
                    rr = slice(32 * d, 32 * d + 4)
                    for c in range(NC):
                        MM(psv[:, s, d, c, :], PSI[rr, s * Tq + c * 128:s * Tq + (c + 1) * 128], ident[rr, 32 * d:32 * d + 4], [psib, cb], [pb])
            CP("dve", PSIT, ps[:, 0:nseq * 2 * NC * 4], [pb], [psitb])
            if not isS:
                P.op("dve", lambda g: g.tensor_reduce(out=mtmp, in_=v3(PSI, Tq), axis=AX.X, op=ALU.max), [psib], [mtb])
                TSC("dve", mtmp, mtmp, 0.0, ALU.max, [mtb], [mtb])
                TT("dve", Mrow, mtmp, BB3[:, :, Tq], ALU.add, [mtb, bbb], [mrowb])
                P.dma("sp", do_smm, Mrow, [mrowb], [])
            if isS:
                P.barrier()
                A.release(mk_gate)
            yield

            Q = [A.alloc(T), A.alloc(T)]; K = [A.alloc(T), A.alloc(T)]; Vv = [A.alloc(T), A.alloc(T)]
            Qb, Kb, Vb = [Buf(), Buf()], [Buf(), Buf()], [Buf(), Buf()]
            Hs = [(A.alloc(T), Buf("H0")), (A.alloc(T), Buf("H1"))]
            Hh, Hb = Hs[0]
            if isS:
                cR, cW = 34, 66
            else:
                cR, cW = 2, Tq + 2
            cpad = A.alloc(cR * cW, BF16); cpb = Buf()
            cp3 = v3(cpad, cW)
            MSET("pool", cpad, 0.0, [cpb])
            dgm = A.alloc(9 * 128, BF16); dgmb = Buf()
            BBC = [A.alloc(nseq * (Tq + 1)), A.alloc(nseq * (Tq + 1))]; bbcb = [Buf(), Buf()]
            NBC = A.alloc(nseq * 2 * (NC + 1)); nbcb = Buf()
            MB = []
            for _ in range(2):
                B_ = {}
                for nm_, sz_ in (("DT", 128), ("PT", 128), ("E", 128), ("qe", 128), ("KW", 128), ("VT", 128), ("om", 2), ("dec", 2),
                                 ("CT", 128), ("NM", 128), ("dmx", 128), ("emt", 2), ("cout", 128)):
                    B_[nm_] = (A.alloc(sz_), Buf(nm_))
                MB.append(B_)
            nout, noutb = A.alloc(16), Buf()
            smC_sb = A.alloc(1024); smn_sb = A.alloc(8); smb = Buf()
            if isS:
                P.dma("sp", smC_sb, dsmC, (), [smb])
                P.dma("sp", smn_sb, dsmn, (), [smb])
            smC4 = smC_sb.rearrange("p (d h v) -> p d h v", d=2, h=4)
            osmC5 = do_smC.rearrange("p (s d h v) -> p s d h v", s=2, d=2, h=4)
            so, sob = A.alloc(512), Buf()
            gt1, gt1b = A.alloc(512), Buf(); gt2, gt2b = A.alloc(512), Buf(); gt3, gt3b = A.alloc(512), Buf()

            def conv_proj(chunk, wname, widx_fn, out, outb, post_scale):
                def cons(n, ps, pb):
                    if isS:
                        CP("act", cp3[:, 1 + 8 * n:9 + 8 * n, 1:65], v3(ps[:, :], 64), [pb], [cpb])
                    else:
                        CP("act", cp3[:, :, 1:1 + Tq], v3(ps[:, :], Tq), [pb], [cpb])
                proj(chunk, 8, h_rhs, hTb, NT, cons)
                taps_ = [(dr, dc) for dr in (-1, 0, 1) for dc in (-1, 0, 1)] if isS else [(0, dc) for dc in (-1, 0, 1)]
                dgm3 = v3(dgm, 128)
                for i_, (dr, dc) in enumerate(taps_):
                    TSC("dve", dgm3[:, i_, :], ident, widx_fn((dr + 1) * 3 + (dc + 1)), ALU.mult, [cb], [dgmb])
                for n in range(NT):
                    ps, pb = PS()
                    for i_, (dr, dc) in enumerate(taps_):
                        if isS:
                            view = cp3[:, 1 + dr + 8 * n:9 + dr + 8 * n, 1 + dc:65 + dc]
                            po = v3(ps[:, :], 64)
                        else:
                            view = cp3[:, :, 1 + dc:1 + dc + Tq]
                            po = v3(ps[:, :], Tq)
                        MM(po, dgm3[:, i_, :], view, [dgmb, cpb], [pb], start=(i_ == 0), stop=(i_ == len(taps_) - 1))
                    ACT(out[:, n * 512:(n + 1) * 512], ps[:, :], AF.Silu, [pb], [outb])
                if post_scale != 1.0:
                    TSC("pool", out, out, post_scale, ALU.mult, [outb], [outb])

            def conv_apply(wname, widx_fn, out, outb, bias_col):
                taps = [(dr, dc) for dr in (-1, 0, 1) for dc in (-1, 0, 1)] if isS else [(0, dc) for dc in (-1, 0, 1)]
                o3 = v3(out, 64) if isS else v3(out, Tq)
                e = "dve"
                for i, (dr, dc) in enumerate(taps):
                    if isS:
                        view = cp3[:, 1 + dr:33 + dr, 1 + dc:65 + dc]
                    else:
                        view = cp3[:, :, 1 + dc:1 + dc + Tq]
                    wcol = widx_fn((dr + 1) * 3 + (dc + 1))
                    if i == 0:
                        if bias_col is None:
                            TSC(e, o3, view, wcol, ALU.mult, [cpb, cb], [outb])
                        else:
                            TSC(e, o3, view, wcol, ALU.mult, [cpb, cb], [outb], s2=bias_col, op1=ALU.add)
                    else:
                        STT(e, o3, view, wcol, o3, ALU.mult, ALU.add, [cpb, cb, outb], [outb])

            omc, _ = PRM["mconv"]
            for h in heads:
                conv_proj(wch("mq", h), "mconv", lambda tap, h=h: prm_g[:, omc + tap * 8 + h:omc + tap * 8 + h + 1], Q[0], Qb[0], 128 ** -0.5)
                yield
                conv_proj(wch("mk", h), "mconv", lambda tap, h=h: prm_g[:, omc + tap * 8 + 4 + h:omc + tap * 8 + 4 + h + 1], K[0], Kb[0], 1.0)

                def cons_v(n, ps, pb):
                    CP("act", Vv[0][:, n * 512:(n + 1) * 512], ps[:, :], [pb], [Vb[0]])
                yield
                proj(wch("mv", h), 8, h_rhs, hTb, NT, cons_v)
                yield
                for (src, srcb) in ((Q, Qb), (K, Kb), (Vv, Vb)):
                    for s in range(nseq):
                        ss = slice(s * Tq, (s + 1) * Tq)
                        CP("pool", src[1][:, ss], src[0][:, ss][:, ::-1], [srcb[0]], [srcb[1]])
                for d in range(2):
                    rr = slice(32 * d, 32 * d + 4)
                    bc3 = v3(BBC[d], Tq + 1)
                    for s in range(nseq):
                        for c0 in range(0, Tq + 1, 512):
                            w = min(512, Tq + 1 - c0)
                            ps, pb = PS()
                            MM(ps[:, 0:w], v3(selh, 128)[rr, h, :], BB3[rr, s, c0:c0 + w], [cb, bbb], [pb])
                            CP("act", bc3[:, s, c0:c0 + w], ps[:, 0:w], [pb], [bbcb[d]])
                    NBC4 = NBC.rearrange("p (s d c) -> p s d c", s=nseq, d=2)
                    for s in range(nseq):
                        TSC("pool", NBC4[:, s, d, :], bc3[:, s, 0:Tq + 1:128], -1.0, ALU.mult, [bbcb[d]], [nbcb])
                def mstream(h, d, B_):
                    DT_, DTb = B_["DT"]; PTm, PTmb = B_["PT"]; Et, Etb = B_["E"]; qe, qeb = B_["qe"]; KW, KWb = B_["KW"]; VTm, VTmb = B_["VT"]
                    om, omb = B_["om"]; dec, decb = B_["dec"]; CT, CTb = B_["CT"]; NM, NMb = B_["NM"]; dmx, dmxb = B_["dmx"]
                    emt, emtb = B_["emt"]; cout, coutb = B_["cout"]
                    Hd, Hdb = Hs[d]
                    bc3 = v3(BBC[d], Tq + 1)
                    NBC4 = NBC.rearrange("p (s d c) -> p s d c", s=nseq, d=2)
                    for s in range(nseq):
                        if isS:
                            TSC("dve", CT, smC4[:, d, h, :], em0[:, d * 4 + h:d * 4 + h + 1], ALU.mult, [smb, cb], [CTb])
                            TSC("dve", NM, onesf, smn_sb[:, d * 4 + h:d * 4 + h + 1], ALU.mult, [smb, cb], [NMb], s2=em0[:, d * 4 + h:d * 4 + h + 1], op1=ALU.mult)
                        else:
                            MSET("pool", CT, 0.0, [CTb])
                            MSET("pool", NM, 0.0, [NMb])
                        yield
                        for c in range(NC):
                            t0 = s * Tq + c * 128
                            ts_ = slice(t0, t0 + 128)
                            psic = PSIT5[:, s, d, c, h:h + 1]
                            ACT(DT_, bc3[:, s, 1 + c * 128:1 + (c + 1) * 128], AF.Exp, [bbcb[d], psitb], [DTb], bias=psic)
                            TT("pool", DT_, DT_, triu, ALU.mult, [DTb, cb], [DTb])
                            ps, pb = PS()
                            MM(ps[:, 0:128], K[d][:, ts_], Q[d][:, ts_], [Kb[d], Qb[d]], [pb])
                            TT("dve", PTm, ps[:, 0:128], DT_, ALU.mult, [pb, DTb], [PTmb])
                            yield
                            ACT(Et, bc3[:, s, 1 + c * 128:1 + (c + 1) * 128], AF.Exp, [bbcb[d], nbcb], [Etb], bias=NBC4[:, s, d, c:c + 1])
                            TT("pool", qe, Q[d][:, ts_], Et, ALU.mult, [Qb[d], Etb], [qeb])
                            ACT(om[:, 0:1], psic, AF.Exp, [psitb, bbcb[d]], [omb], bias=bc3[:, s, (c + 1) * 128:(c + 1) * 128 + 1])
                            ACT(dec[:, 0:1], bc3[:, s, (c + 1) * 128:(c + 1) * 128 + 1], AF.Exp, [bbcb[d], nbcb], [decb], bias=NBC4[:, s, d, c:c + 1])
                            yield
                            ps, pb = PS()
                            MM(ps[:, 0:128], K[d][:, ts_], ident, [Kb[d], cb], [pb])
                            TSC("dve", KW, ps[:, 0:128], om[:, 0:1], ALU.mult, [pb, omb], [KWb])
                            ps, pb = PS()
                            MM(ps[:, 0:128], Vv[d][:, ts_], ident, [Vb[d], cb], [pb])
                            CP("act", VTm, ps[:, 0:128], [pb], [VTmb])
                            yield
                            psn, pbn = PS()
                            MM(psn[:, 0:128], VTm, PTm, [VTmb, PTmb], [pbn], start=True, stop=False)
                            MM(psn[:, 0:128], CT, qe, [CTb, qeb], [pbn], start=False, stop=True)
                            psd, pbd = PS()
                            MM(psd[:, 0:128], onesf, PTm, [cb, PTmb], [pbd], start=True, stop=False)
                            MM(psd[:, 0:128], NM, qe, [NMb, qeb], [pbd], start=False, stop=True)
                            psc, pbc = PS()
                            MM(psc[:, 0:128], KW, VTm, [KWb, VTmb], [pbc])
                            psn2, pbn2 = PS()
                            MM(psn2[:, 0:128], KW, onesf, [KWb, cb], [pbn2])
                            STT("dve", CT, CT, dec[:, 0:1], psc[:, 0:128], ALU.mult, ALU.add, [CTb, decb, pbc], [CTb])
                            STT("dve", NM, NM, dec[:, 0:1], psn2[:, 0:128], ALU.mult, ALU.add, [NMb, decb, pbn2], [NMb])
                            ACT(dmx, psd[:, 0:128], AF.Abs, [pbd], [dmxb])
                            TSC("dve", dmx, dmx, 1.0, ALU.max, [dmxb], [dmxb])
                            RECIP(dmx, dmxb)
                            hdst = Hd[:, ts_] if d == 0 else Hd[:, s * Tq:(s + 1) * Tq][:, ::-1][:, c * 128:(c + 1) * 128]
                            TT("dve", hdst, psn[:, 0:128], dmx, ALU.mult, [pbn, dmxb], [Hdb])
                            yield
                        if not isS:
                            rr = slice(32 * d, 32 * d + 4)
                            ps, pb = PS()
                            MM(ps[:, 0:1], v3(selh, 128)[rr, h, :], Mrow[rr, s:s + 1], [cb, mrowb], [pb])
                            ACT(emt[:, 0:1], ps[:, 0:1], AF.Exp, [pb], [emtb], scale=-1.0)
                            TSC("dve", cout, CT, emt[:, 0:1], ALU.mult, [CTb, emtb], [coutb])
                            ni = s * 8 + d * 4 + h
                            TSC("dve", nout[:, ni:ni + 1], NM[:, 0:1], emt[:, 0:1], ALU.mult, [NMb, emtb], [noutb])
                            P.dma("sp", osmC5[:, s, d, h, :], cout, [coutb], [])
                            yield

                yield from run_rr_g([mstream(h, 0, MB[0]), mstream(h, 1, MB[1])])
                for n in range(NT):
                    sl = slice(n * 512, (n + 1) * 512)
                    TT("dve", Hh[:, sl], Hh[:, sl], Hs[1][0][:, sl], ALU.add, [Hb, Hs[1][1]], [Hb])
                wt, wb = load_w(wch("mo", h))
                for n in range(NT):
                    sl = slice(n * 512, (n + 1) * 512)
                    ps, pb = PS()
                    for kc in range(8):
                        MM(ps[:, :], wt[:, kc * 128:(kc + 1) * 128], h_rhs(kc, n), hTb + [wb], [pb], start=(kc == 0), stop=(kc == 7))
                    ACT(so, ps[:, :], AF.Sigmoid, [pb], [sob])
                    ps1, pb1 = PS()
                    MM(ps1[:, :], onesf, Hh[:, sl], [cb, Hb], [pb1])
                    TT("pool", gt1, Hh[:, sl], Hh[:, sl], ALU.mult, [Hb], [gt1b])
                    ps2, pb2 = PS()
                    MM(ps2[:, :], onesf, gt1, [cb, gt1b], [pb2])
                    TSC("dve", gt2, ps1[:, :], 1.0 / 128, ALU.mult, [pb1], [gt2b])
                    STT("dve", gt3, gt2, -1.0, gt2, ALU.mult, ALU.mult, [gt2b], [gt3b])
                    STT("dve", gt3, ps2[:, :], 1.0 / 128, gt3, ALU.mult, ALU.add, [pb2, gt3b], [gt3b])
                    TSC("dve", gt3, gt3, 1e-5, ALU.add, [gt3b], [gt3b])
                    ACT(gt3, gt3, AF.Sqrt, [gt3b], [gt3b])
                    RECIP(gt3, gt3b)
                    TT("dve", gt2, Hh[:, sl], gt2, ALU.subtract, [Hb, gt2b], [gt2b])
                    TT("dve", gt2, gt2, gt3, ALU.mult, [gt2b, gt3b], [gt2b])
                    STT("dve", v3(ym, T)[:, h, sl], gt2, prmc("gng", h), so, ALU.mult, ALU.mult, [gt2b, sob, cb], [ymb])
                    yield
            if not isS:
                P.dma("sp", do_smn, nout, [noutb], [])

        gM = mlstm_body()
        if isS:
            for _ in gM:
                pass
        else:
            for _ in run_rr_g([gR, gM]):
                pass
        P.barrier()
        A.release(mk_m if isS else mk_r)
        wpool[0] = None
        if isS:
            ypad = [A.alloc(2560, BF16), A.alloc(2560, BF16)]
            ypb = [Buf(), Buf()]
            for k_, (src_, srcb_) in enumerate(((yr, yrb), (ym, ymb))):
                MSET("dve", ypad[k_], 0.0, [ypb[k_]])
                CP("dve", ypad[k_][:, 256:256 + TS], src_[:, 0:TS], [srcb_], [ypb[k_]])
            dlt_sb = A.alloc(4)
            dltb = Buf()
            P.dma("sp", dlt_sb, ddlt, (), [dltb])
            msk = [(A.alloc(2560, BF16), Buf()) for _ in range(2)]
            rs5 = rs_in.ap().rearrange("(q r k p) t -> p q r k t", q=4, r=4, k=2, p=128)
            rsibs = []
            for r_ in range(4):
                for k_ in range(2):
                    m_, mb_ = msk[(r_ * 2 + k_) % 2]
                    TSC("dve", m_, ypad[k_], dlt_sb[:, r_:r_ + 1], ALU.mult, [ypb[k_], dltb], [mb_])
                    bb_ = Buf()
                    src_ = bass.AP(m_.tensor, m_.offset, [list(m_.ap[0]), [512, 4], [1, 1024]])
                    P.dma("sp", rs5[:, :, r_, k_, :], src_, [mb_], [bb_])
                    rsibs.append(bb_)
            rsob = Buf("rs_out")
            P.coll("pool", lambda g: g.collective_compute("ReduceScatter", ALU.add, replica_groups=[[0, 2, 4, 6], [1, 3, 5, 7]],
                                                          ins=[rs_in.ap().opt()], outs=[rs_out.ap().opt()]), rsibs, [rsob])
            shared["rsob"] = rsob
        if isS:
            P.barrier()
            yield
            A.release(base_mark)
            T = 1024
            NT = 2
            xd = dxw
            rowm_sb = A.alloc(16)
            P.dma("sp", rowm_sb, drowm, (), [cb])
            hT = A.alloc(8 * T, BF16)
            hTb = [Buf(f"hq{n}") for n in range(NT)]
            h3 = v3(hT, T)
            hmark = A.mark()
            yr, yrb, ym, ymb, xb_pref, xbb_pref = shared["pref"]
            mkq = A.mark()
            nb_ = [(A.alloc(4096), Buf(), A.alloc(4096, BF16), Buf(), A.alloc(512), Buf()) for _ in range(2)]
            xbig = [xb_pref]
            xbigb = [xbb_pref]
            xviews = [(v3(xbig[0], 1024)[:, :, n * 512:n * 512 + 512], xbigb[0]) for n in range(NT)]
            norm_tiles_v(NT, lambda n: None, mA1, mB1, hTb, xviews, nb_)
            P.barrier()
            A.release(mkq)
        if dbg:
            if isS:
                P.dma("sp", dbg_out["h_" + gn], hT, hTb, [])
            P.dma("sp", dbg_out["yr_" + gn], yr, [yrb], [])
            P.dma("sp", dbg_out["ym_" + gn], ym, [ymb], [])
        Ti = 512 if isS else T
        inner = [(0, 256)] if isS else [(n, n * 512) for n in range(NT)]

        mk_g = A.mark()
        mg = A.alloc(8 * T, BF16); mgb = Buf()
        mg3 = v3(mg, T)
        yr3 = v3(yr, T); ym3 = v3(ym, T)
        wo = A.alloc(8 * 1024, BF16)
        wobs = [Buf() for _ in range(8)]
        for i in range(8):
            P.dma("pool", wo[:, i * 1024:(i + 1) * 1024], dwout[i], (), [wobs[i]])
        mk_g2 = A.mark()
        gtmp = [[(A.alloc(512), Buf()) for _ in range(3)] for _ in range(2)]
        for i in range(8):
            if i == 0:
                mwb = [[(A.alloc(1024, BF16), Buf()), (A.alloc(1024, BF16), Buf()), (A.alloc(512, BF16), Buf()), (A.alloc(512, BF16), Buf())] for _ in range(2)]
            (wgr, wgrb), (wgm, wgmb), (wt_r, wrb), (wt_m, wmb) = mwb[i % 2]
            P.dma("pool", wgr, dwin[31 + i], (), [wgrb])
            P.dma("pool", wgm, dwin[39 + i], (), [wgmb])
            P.dma("pool", wt_r, dwbr[i], (), [wrb])
            P.dma("pool", wt_m, dwbm[i], (), [wmb])
            for n in range(NT):
                sl = slice(n * 512, (n + 1) * 512)
                (gr, grb), (gm, gmb), (mt, mtb2) = gtmp[(i * NT + n) % 2]
                ps, pb = PS()
                for kc in range(8):
                    MM(ps[:, :], wgr[:, kc * 128:(kc + 1) * 128], h_rhs(kc, n), hTb + [wgrb], [pb], start=(kc == 0), stop=(kc == 7))
                ACT(gr, ps[:, :], AF.Sigmoid, [pb], [grb])
                ps, pb = PS()
                for kc in range(8):
                    MM(ps[:, :], wgm[:, kc * 128:(kc + 1) * 128], h_rhs(kc, n), hTb + [wgmb], [pb], start=(kc == 0), stop=(kc == 7))
                ACT(gm, ps[:, :], AF.Sigmoid, [pb], [gmb])
                ps, pb = PS()
                for kc in range(4):
                    MM(ps[:, :], wt_r[:, kc * 128:(kc + 1) * 128], yr3[:, kc, sl], [wrb, yrb], [pb], start=(kc == 0), stop=(kc == 3))
                TT("dve", mt, ps[:, :], gr, ALU.mult, [pb, grb], [mtb2])
                ps, pb = PS()
                for kc in range(4):
                    MM(ps[:, :], wt_m[:, kc * 128:(kc + 1) * 128], ym3[:, kc, sl], [wmb, ymb], [pb], start=(kc == 0), stop=(kc == 3))
                TT("dve", gm, ps[:, :], gm, ALU.mult, [pb, gmb], [gmb])
                TT("pool", mg3[:, i, sl], mt, gm, ALU.add, [mtb2, gmb], [mgb])
        if dbg:
            P.dma("sp", dbg_out["mg_" + gn], mg, [mgb], [])
        P.barrier()
        A.release(mk_g2)
        if isS:
            A.guard = None
        h2, h2b = hT, hTb
        ot, otb = A.alloc(4096), Buf()
        if dbg:
            dwo = nc.dram_tensor("dbg_wo_" + gn, [128, 8192], BF16, kind="ExternalOutput").ap()
            P.dma("sp", dwo, wo, wobs, [])
            dot = dout("dbg_ot_" + gn, [128, 4096])
        xt2, xt2b = A.alloc(4096), Buf()
        tmp = A.alloc(4096); tmpb = Buf()
        sq = A.alloc(4096, BF16); sqb = Buf()
        rstd = A.alloc(512); rstdb = Buf()
        for n in range(NT):
            sl = slice(n * 512, (n + 1) * 512)
            for i in range(8):
                ps, pb = PS()
                for kc in range(8):
                    MM(ps[:, :], wo[:, i * 1024 + kc * 128:i * 1024 + (kc + 1) * 128], mg3[:, kc, sl], [wobs[i], mgb], [pb], start=(kc == 0), stop=(kc == 7))
                CP("act", ot[:, i * 512:(i + 1) * 512], ps[:, :], [pb], [otb])
            P.dma("sp", v3(xt2, 512), x_view(xd, n), (), [xt2b])
            if dbg and n == 0:
                P.dma("sp", dot, ot, [otb], [])
            ACT(sq, ot, AF.Square, [otb], [sqb])
            ps, pb = PS()
            for kc in range(8):
                MM(ps[:, :], onesb, sq[:, kc * 512:(kc + 1) * 512], [cb, sqb], [pb], start=(kc == 0), stop=(kc == 7))
            ACT(rstd, ps[:, :], AF.Sqrt, [pb], [rstdb], bias=epsc[:, 0:1], scale=1.0 / D)
            RECIP(rstd, rstdb)
            for kc in range(8):
                ks_ = slice(kc * 512, (kc + 1) * 512)
                STT("dve" if kc % 2 == 0 else "pool", ot[:, ks_], ot[:, ks_], mcol(mG1, kc, ci_), rstd, ALU.mult, ALU.mult, [otb, rstdb, modb], [otb])
            TT("dve", xt2, xt2, ot, ALU.add, [xt2b, otb], [xt2b])
            P.dma("sp", x_view(dxm[gn], n), v3(xt2, 512), [xt2b], [dxm_buf[gn]])
            norm_mod(xt2, xt2b, mA2, mB2, lambda kc, n=n: h3[:, kc, n * 512:(n + 1) * 512], h2b[n], tmp, tmpb, sq, sqb, rstd, rstdb)
        P.barrier()
        A.release(mk_g)

        A.release(hmark)
        mk_f = A.mark()
        if (not isS) and ("rsob" in shared) and ("pref" not in shared):
            yr_t = arena_t[:, TOPB:TOPB + 2048].bitcast(BF16)
            ym_t = arena_t[:, TOPB + 2048:TOPB + 4096].bitcast(BF16)
            xb_t = arena_t[:, TOPB + 4096:TOPB + 4096 + 8192]
            yrb_t, ymb_t, xbb_t = Buf("yr_t"), Buf("ym_t"), Buf("xb_t")
            rso4_ = rs_out.ap().rearrange("(r k p) t -> p r k t", r=4, k=2, p=128)
            P.dma("sp", v3(yr_t, 1024), rso4_[:, :, 0, :], [shared["rsob"]], [yrb_t])
            P.dma("sp", v3(ym_t, 1024), rso4_[:, :, 1, :], [shared["rsob"]], [ymb_t])
            P.dma("sp", v3(xb_t, 1024), dxw.rearrange("(kc p) t -> p kc t", p=128), (), [xbb_t])
            shared["pref"] = (yr_t, yrb_t, ym_t, ymb_t, xb_t, xbb_t)
            A.guard = TOPB
        facc = A.alloc(8 * Ti); faccb = Buf()
        f3 = v3(facc, Ti)
        mk_f2 = A.mark()
        if isS:
            cR, cW = 18, 66
        else:
            cR, cW = 2, Tq + 2
        cpads = [A.alloc(cR * cW, BF16) for _ in range(2)]
        cpbs = [Buf(), Buf()]
        for c_ in range(2):
            MSET("pool", cpads[c_], 0.0, [cpbs[c_]])
        dgs = [(A.alloc(9 * 128, BF16), Buf()) for _ in range(2)]
        uas = [(A.alloc(Ti), Buf()) for _ in range(2)]
        uvs = [(A.alloc(Ti), Buf()) for _ in range(2)]
        GS = 4
        pch = [A.alloc(Ti, BF16) for _ in range(2 * GS)]
        pchb = [Buf() for _ in range(2 * GS)]
        fdw = [A.alloc(GS * 1024, BF16) for _ in range(2)]
        fdwb = [Buf(), Buf()]
        ftm = [(A.alloc(512), Buf()) for _ in range(2)]
        ftm_rr = [0]
        faccbs = [[Buf() for _ in range(len(inner))] for _ in range(8)]
        faccb = [bb for row in faccbs for bb in row]
        ofc, _ = PRM["fconv"]
        ngroups = (NFF + GS - 1) // GS
        if A.top + 6 * 512 + 2048 < A.ncols:
            wpool[0] = ([(A.alloc(1024, BF16), Buf()) for _ in range(6)], [0])

        def down(g_):
            j0 = g_ * GS
            nj = min(GS, NFF - j0)
            fi = g_ % 2
            for i in range(8):
                for n in range(len(inner)):
                    sl = slice(n * 512, (n + 1) * 512)
                    ps, pb = PS()
                    for jj in range(nj):
                        pi = (j0 + jj) % (2 * GS)
                        MM(ps[:, :], fdw[fi][:, jj * 1024 + i * 128:jj * 1024 + (i + 1) * 128], pch[pi][:, sl], [fdwb[fi], pchb[pi]], [pb], start=(jj == 0), stop=(jj == nj - 1))
                    if g_ == 0:
                        CP("act", f3[:, i, sl], ps[:, :], [pb], [faccbs[i][n]])
                    else:
                        TT("dve", f3[:, i, sl], f3[:, i, sl], ps[:, :], ALU.add, [faccbs[i][n], pb], [faccbs[i][n]])

        taps = [(dr, dc) for dr in (-1, 0, 1) for dc in (-1, 0, 1)] if isS else [(0, dc) for dc in (-1, 0, 1)]

        def front(j):
            g_ = j // GS
            if j % GS == 0:
                nj = min(GS, NFF - j)
                fi = g_ % 2
                P.dma("pool", v3(fdw[fi], 1024)[:, 0:nj, :], dfdn[j:j + nj].rearrange("k p n -> p k n"), (), [fdwb[fi]])
            cpad, cpb = cpads[j % 2], cpbs[j % 2]
            cp3 = v3(cpad, cW)
            uv, uvb = uvs[j % 2]
            dg, dgb = dgs[j % 2]
            dg3 = v3(dg, 128)
            for i, (dr, dc) in enumerate(taps):
                tap = (dr + 1) * 3 + (dc + 1)
                wcol = prm[:, ofc + tap * 22 + j:ofc + tap * 22 + j + 1]
                TSC("dve", dg3[:, i, :], ident, wcol, ALU.mult, [cb], [dgb])

            def cons_a(n, ps, pb, cp3=cp3, cpb=cpb):
                if isS:
                    CP("act", cp3[:, 1 + 8 * n:9 + 8 * n, 1:65], v3(ps[:, :], 64), [pb], [cpb])
                else:
                    CP("act", cp3[:, :, 1:1 + Tq], v3(ps[:, :], Tq), [pb], [cpb])
            proj(dfup[j], 8, h_rhs, h2b, NT, cons_a)
            if isS:
                ci3 = cp3[:, 1:17, 1:65]
                TT("dve", ci3, ci3, rowm_sb.unsqueeze(2).to_broadcast([128, 16, 64]), ALU.mult, [cpb, cb], [cpb])

            def cons_vv(n, ps, pb, uv=uv, uvb=uvb):
                CP("act", uv[:, n * 512:(n + 1) * 512], ps[:, :], [pb], [uvb])
            proj(dfup[22 + j], 8, lambda kc, k_: h3[:, kc, inner[k_][1]:inner[k_][1] + 512], h2b, len(inner), cons_vv)

        def back(j):
            cpad, cpb = cpads[j % 2], cpbs[j % 2]
            cp3 = v3(cpad, cW)
            ua, uab = uas[j % 2]
            uv, uvb = uvs[j % 2]
            dg, dgb = dgs[j % 2]
            dg3 = v3(dg, 128)
            fcbcol = prm[:, PRM["fcb"][0] + j:PRM["fcb"][0] + j + 1]
            for n in range(len(inner)):
                ps, pb = PS()
                for i, (dr, dc) in enumerate(taps):
                    if isS:
                        r0_ = inner[n][1] // 64
                        view = cp3[:, 1 + dr + r0_:9 + dr + r0_, 1 + dc:65 + dc]
                        po = v3(ps[:, :], 64)
                    else:
                        view = cp3[:, :, 1 + dc:1 + dc + Tq]
                        po = v3(ps[:, :], Tq)
                    MM(po, dg3[:, i, :], view, [dgb, cpb], [pb], start=(i == 0), stop=(i == len(taps) - 1))
                ACT(ua[:, n * 512:(n + 1) * 512], ps[:, :], AF.Silu, [pb, cb], [uab], bias=fcbcol)
            TT("pool", pch[j % (2 * GS)], ua, uv, ALU.mult, [uab, uvb], [pchb[j % (2 * GS)]])

        front(0)
        for j in range(NFF):
            if j + 1 < NFF:
                front(j + 1)
            back(j)
            if j % GS == 0 and j >= GS:
                down(j // GS - 1)
        down(ngroups - 1)
        wpool[0] = None
        if dbg:
            P.dma("sp", dbg_out["f_" + gn], facc, faccb, [])
        P.barrier()
        A.release(mk_f2)
        xt2, xt2b = A.alloc(4096), Buf()
        ft, ftb = A.alloc(4096), Buf()
        sq = A.alloc(4096, BF16); sqb = Buf()
        rstd = A.alloc(512); rstdb = Buf()
        for n in range(len(inner)):
            sl = slice(n * 512, (n + 1) * 512)
            P.dma("sp", v3(xt2, 512), dxm[gn].rearrange("(kc p) t -> p kc t", p=128)[:, :, inner[n][1]:inner[n][1] + 512], [dxm_buf[gn]], [xt2b])
            CP("pool", v3(ft, 512), f3[:, :, sl], [faccbs[i_][n] for i_ in range(8)], [ftb])
            ACT(sq, ft, AF.Square, [ftb], [sqb])
            ps, pb = PS()
            for kc in range(8):
                MM(ps[:, :], onesb, sq[:, kc * 512:(kc + 1) * 512], [cb, sqb], [pb], start=(kc == 0), stop=(kc == 7))
            ACT(rstd, ps[:, :], AF.Sqrt, [pb], [rstdb], bias=epsc[:, 0:1], scale=1.0 / D)
            RECIP(rstd, rstdb)
            for kc in range(8):
                ks_ = slice(kc * 512, (kc + 1) * 512)
                STT("dve" if kc % 2 == 0 else "pool", ft[:, ks_], ft[:, ks_], mcol(mG2, kc, ci_), rstd, ALU.mult, ALU.mult, [ftb, rstdb, modb], [ftb])
            TT("dve", xt2, xt2, ft, ALU.add, [xt2b, ftb], [xt2b])
            P.dma("sp", x_view(dy[gn], n), v3(xt2, 512), [xt2b], [])
        P.barrier()
        A.release(mk_f)
        yield

    gS = run_group("s")
    next(gS)
    for _ in run_group("p"):
        pass
    for _ in gS:
        pass
    P.barrier(final=True)
    P.emit()
    es.close()
    return P, A


_CACHE = {}


def _prep_shared(inp):
    f = lambda k: np.asarray(inp[k], np.float32)
    prm = np.zeros((128, NPRM), np.float32)

    def put(name, arr):
        o, c = PRM[name]
        assert arr.shape == (128, c), (name, arr.shape)
        prm[:, o:o + c] = arr
    put("ada_b", _fm(f("ada_b")[0]))
    for i in range(4):
        put(f"ng{i}", _fm(f("norm_g")[0, i]))
    put("mu", _fm(f("rwkv_mu")[0]))
    put("w0", _fm(f("rwkv_w0")[0].reshape(-1)))
    put("a0", _fm(f("rwkv_a0")[0].reshape(-1)))
    put("kks", _fm(f("rwkv_kk_scale")[0]))
    put("ka", _fm(f("rwkv_k_a")[0]))
    put("rk", _fm(f("rwkv_r_k")[0].reshape(-1)))
    put("lng", _fm(f("rwkv_lnx_g")[0]))
    put("lnb", _fm(f("rwkv_lnx_b")[0]))
    put("mconv", np.ascontiguousarray(f("mlstm_conv")[0].reshape(9, 8, 128).transpose(2, 0, 1).reshape(128, 72)))
    gb = f("mlstm_gate_b")[0]
    gbi = np.zeros((128, 1), np.float32)
    gbf = np.zeros((128, 1), np.float32)
    for d in range(2):
        gbi[32 * d:32 * d + 4, 0] = gb[0, d]
        gbf[32 * d:32 * d + 4, 0] = gb[1, d]
    put("gbi", gbi)
    put("gbf", gbf)
    put("gng", _fm(f("mlstm_gn_g")[0]))
    put("fconv", np.ascontiguousarray(f("ffn_conv")[0].reshape(9, 22, 128).transpose(2, 0, 1).reshape(128, 198)))
    put("fcb", _fm(f("ffn_conv_b")[0]))
    w_in = f("w_in")[0]
    gi = np.zeros((1024, 128), np.float32)
    gf = np.zeros((1024, 128), np.float32)
    for d in range(2):
        gi[:, 32 * d:32 * d + 4] = w_in[:, 3968 + d * 4:3968 + d * 4 + 4]
        gf[:, 32 * d:32 * d + 4] = w_in[:, 3968 + 8 + d * 4:3968 + 8 + d * 4 + 4]
    w_in2 = np.concatenate([w_in[:, 0:3968], w_in[:, 3984:6032], gi, gf], axis=1)
    lora = np.concatenate([f("rwkv_w_up")[0].reshape(128, 512), f("rwkv_a_up")[0].reshape(128, 512), f("rwkv_g_up")[0]], axis=1)
    sh = {
        "prm": prm,
        "lora": np.ascontiguousarray(lora),
        "adaw_full": _tile_w(f("ada_w")[0]),
        "win": _tile_w(w_in2),
        "wbr": _tile_w(f("w_branch_rwkv")[0]),
        "wbm": _tile_w(f("w_branch_mlstm")[0]),
        "wout": _tile_w(f("w_out")[0]),
        "fup": _tile_w(f("ffn_up")[0]),
        "fdn": np.ascontiguousarray(f("ffn_down")[0].reshape(22, 128, 1024)),
    }
    aux = {"win_t": sh["win"], "prm": prm, "lora": sh["lora"], "w_in": w_in}
    return sh, aux


def _prep_core(inp, i, sh, aux):
    f = lambda k: np.asarray(inp[k], np.float32)
    b = i % 2
    r = i // 2
    m = dict(sh)
    adaw_full = m.pop("adaw_full")
    m["adaw"] = np.ascontiguousarray(adaw_full[12 * r:12 * r + 12])
    o_ab, _ = PRM["ada_b"]
    m["adab"] = np.ascontiguousarray(aux["prm"][:, o_ab + 12 * r:o_ab + 12 * r + 12])
    wt = aux["win_t"]
    w_in = aux["w_in"]
    gi = np.zeros((1024, 128), np.float32)
    gf = np.zeros((1024, 128), np.float32)
    for d in range(2):
        gi[:, 32 * d] = w_in[:, 3968 + d * 4 + r]
        gf[:, 32 * d] = w_in[:, 3968 + 8 + d * 4 + r]
    m["win_s"] = np.ascontiguousarray(np.concatenate([wt[[r, 4 + r, 8 + r, 12, 13, 14, 15 + r, 19 + r, 23 + r, 27 + r]], _tile_w(np.concatenate([gi, gf], axis=1))], axis=0))
    ps = aux["prm"].copy()

    def mv(name, dst, src):
        o, c = PRM[name]
        ps[:, o + dst] = aux["prm"][:, o + src]
    for base in (0, 4, 8):
        mv("mu", base, base + r)
    for d in range(2):
        mv("w0", d * 4, d * 4 + r)
        mv("a0", d * 4, d * 4 + r)
    for nm in ("kks", "ka", "rk", "lng", "lnb", "gng"):
        mv(nm, 0, r)
    for tap in range(9):
        mv("mconv", tap * 8, tap * 8 + r)
        mv("mconv", tap * 8 + 4, tap * 8 + 4 + r)
    gb = f("mlstm_gate_b")[0]
    og, _ = PRM["gbi"]
    ogf, _ = PRM["gbf"]
    ps[:, og] = 0.0
    ps[:, ogf] = 0.0
    for d in range(2):
        ps[32 * d, og] = gb[0, d, r]
        ps[32 * d, ogf] = gb[1, d, r]
    m["prm_s"] = ps
    lo = aux["lora"].copy()
    for k in range(3):
        lo[:, k * 512:k * 512 + 128] = aux["lora"][:, k * 512 + r * 128:k * 512 + (r + 1) * 128]
    m["lora_s"] = lo
    m["xs"] = np.ascontiguousarray(f("x_sample")[b].T)
    xpad = np.zeros((1024, 256 + 2048 + 256), np.float32)
    xpad[:, 256:256 + 2048] = m["xs"]
    m["xw"] = np.ascontiguousarray(xpad[:, 512 * r:512 * r + 1024])
    dl = np.zeros((128, 4), np.float32)
    dl[:, r] = 1.0
    m["dlt"] = dl
    rows = 8 * r - 4 + np.arange(16)
    m["rowm"] = np.ascontiguousarray(np.broadcast_to(((rows >= 0) & (rows < 32)).astype(np.float32)[None, :], (128, 16)))
    m["xp"] = np.ascontiguousarray(np.concatenate([f("x_prompt")[2 * i].T, f("x_prompt")[2 * i + 1].T], axis=1))
    vecs = np.stack([f("c_ctx"), f("c")[b]])
    m["cond"] = np.ascontiguousarray(vecs.reshape(2, 8, 128).transpose(2, 1, 0).reshape(128, 16))
    sr = f("state_rwkv")[b, 0]
    sr5 = sr.reshape(2, 4, 2, 64, 64).transpose(2, 4, 0, 1, 3).copy()
    sr5[:, :, :, 0] = sr5[:, :, :, r]
    m["srw"] = np.ascontiguousarray(sr5.reshape(128, 512))
    sc = f("state_mlstm_C")[b, 0].transpose(3, 0, 1, 2).copy()
    sc[:, :, 0] = sc[:, :, r]
    m["smC"] = np.ascontiguousarray(sc.reshape(128, 1024))
    sn = f("state_mlstm_n")[b, 0].transpose(2, 0, 1).copy()
    sn[:, :, 0] = sn[:, :, r]
    m["smn"] = np.ascontiguousarray(sn.reshape(128, 8))
    sm = f("state_mlstm_m")[b, 0].copy()
    sm[:, 0] = sm[:, r]
    m["smm"] = np.ascontiguousarray(np.broadcast_to(sm.reshape(1, 8), (128, 8)))
    return m


def _get_nc(dbg=False):
    key = ("nc", dbg)
    if key not in _CACHE:
        nc = bass.Bass("TRN2", target_bir_lowering=False)
        build(nc, dbg=dbg)
        _CACHE[key] = nc
    return _CACHE[key]


def kernel(**inp):
    nc = _get_nc(False)
    sh, aux = _prep_shared(inp)
    in_maps = [_prep_core(inp, i, sh, aux) for i in range(8)]
    res = run_bass_kernel_spmd(nc, in_maps, core_ids=list(range(8)))
    R = res.results
    y_prompt = np.zeros((16, 256, 1024), np.float32)
    y_sample = np.zeros((2, 2048, 1024), np.float32)
    nS = np.zeros((16, 1, 2, 8, 64, 64), np.float32)
    nC = np.zeros((16, 1, 2, 4, 128, 128), np.float32)
    nn = np.zeros((16, 1, 2, 4, 128), np.float32)
    nm = np.zeros((16, 1, 2, 4), np.float32)
    for i in range(8):
        r = R[i]
        yp = np.asarray(r["yp"])
        for s in range(2):
            y_prompt[2 * i + s] = yp[:, s * 256:(s + 1) * 256].T
        q = i // 2
        y_sample[i % 2, 512 * q:512 * q + 512] = np.asarray(r["ys"]).T
        osr = np.asarray(r["o_srw"]).reshape(2, 64, 2, 2, 4, 64)
        nS[2 * i:2 * i + 2, 0] = osr.transpose(2, 3, 4, 0, 5, 1).reshape(2, 2, 8, 64, 64)
        oc = np.asarray(r["o_smC"]).reshape(128, 2, 2, 4, 128)
        nC[2 * i:2 * i + 2, 0] = oc.transpose(1, 2, 3, 4, 0)
        on = np.asarray(r["o_smn"]).reshape(128, 2, 2, 4)
        nn[2 * i:2 * i + 2, 0] = on.transpose(1, 2, 3, 0)
        om = np.asarray(r["o_smm"])
        for d in range(2):
            nm[2 * i:2 * i + 2, 0, d, :] = om[32 * d:32 * d + 4, :].T
    return (y_prompt, y_sample, nS, nC, nn, nm)
```
